# Optimizing a Trainium2 kernel written in Bass

```python
import math
import jax, jax.numpy as jnp
from jax import lax
import numpy as np

D_MODEL = 1024
BATCH = 8
SEQ = 8192
DEPTH = 2

D_FF = 2816
N_EVEN = (DEPTH + 1) // 2
N_ODD = DEPTH // 2
S5_WIDTH = D_MODEL // 2
S5_GROUP = 16
S5_GROUPS = S5_WIDTH // S5_GROUP
S5_STATE = 64
GLA_HEADS = 4
GLA_DK = D_MODEL // 4 // GLA_HEADS
GLA_DV = D_MODEL // 2 // GLA_HEADS
GLA_RANK = 16
GLA_GATE_NORM = 16.0
RET_HEADS = 8
RET_DK = D_MODEL // RET_HEADS
RET_DV = 2 * D_MODEL // RET_HEADS
ROPE_BASE = 10000.0
CHUNK = 64
EPS = 1e-6

AB_IN = S5_WIDTH + 2 * GLA_HEADS * GLA_DK + 2 * GLA_HEADS * GLA_DV + 2 * GLA_RANK
AB_OUT = S5_WIDTH + GLA_HEADS * GLA_DV
RET_IN = 2 * RET_HEADS * RET_DK + 2 * RET_HEADS * RET_DV
RET_OUT = RET_HEADS * RET_DV

kernel_name = 'hybrid_s5_gla_retention_macaron_encoder'


def rmsnorm(x, g):
    xf = x.astype(jnp.float32)
    y = xf * lax.rsqrt(jnp.mean(xf * xf, axis=-1, keepdims=True) + EPS)
    return (y * g.astype(jnp.float32)).astype(x.dtype)


def head_rmsnorm(o, g):
    y = o * lax.rsqrt(jnp.mean(o * o, axis=-1, keepdims=True) + EPS)
    return y * g.astype(jnp.float32).reshape(o.shape[-2], o.shape[-1])


def swiglu_ffn(h, w1, w2):
    gate, up = jnp.split(h @ w1, 2, axis=-1)
    return (jax.nn.silu(gate) * up) @ w2


def to_heads(t, n_heads):
    b, l, _ = t.shape
    return t.reshape(b, l, n_heads, -1).transpose(0, 2, 1, 3)


def flip_seq(t):
    return jnp.flip(t, axis=2)


def _complex_affine_combine(e1, e2):
    a1r, a1i, b1r, b1i = e1
    a2r, a2i, b2r, b2i = e2
    return (a2r * a1r - a2i * a1i,
            a2r * a1i + a2i * a1r,
            a2r * b1r - a2i * b1i + b2r,
            a2r * b1i + a2i * b1r + b2i)


def s5_bidirectional(u, lam_re, lam_im, b_re, b_im, c_re, c_im, log_dt, d_skip):
    f32 = jnp.float32
    uf = u.astype(f32)
    seq_len = u.shape[1]
    y = uf * d_skip.astype(f32).reshape(S5_GROUPS, S5_GROUP)
    for direction in range(2):
        lr = jnp.minimum(lam_re[direction].astype(f32), -1e-4)
        li = lam_im[direction].astype(f32)
        dt = jnp.exp(log_dt[direction].astype(f32))[:, None]
        mag = jnp.exp(lr * dt)
        ar = mag * jnp.cos(li * dt)
        ai = mag * jnp.sin(li * dt)
        den = lr * lr + li * li
        cr = ((ar - 1.0) * lr + ai * li) / den
        ci = (ai * lr - (ar - 1.0) * li) / den
        br = b_re[direction].astype(f32)
        bi = b_im[direction].astype(f32)
        bbr = cr[..., None] * br - ci[..., None] * bi
        bbi = cr[..., None] * bi + ci[..., None] * br
        bu_r = jnp.einsum('blgp,gnp->blgn', uf, bbr)
        bu_i = jnp.einsum('blgp,gnp->blgn', uf, bbi)
        a_r = jnp.broadcast_to(ar[None, None], (1, seq_len) + ar.shape)
        a_i = jnp.broadcast_to(ai[None, None], (1, seq_len) + ai.shape)
        _, _, xr, xi = lax.associative_scan(_complex_affine_combine, (a_r, a_i, bu_r, bu_i),
                                            reverse=(direction == 1), axis=1)
        y = y + jnp.einsum('blgn,gpn->blgp', xr, c_re[direction].astype(f32)) \
              - jnp.einsum('blgn,gpn->blgp', xi, c_im[direction].astype(f32))
    return y


def gla_chunked(q, k, v, g, strict):
    b_, h_, l_, dk = q.shape
    dv = v.shape[-1]
    n = l_ // CHUNK
    q = q.reshape(b_, h_, n, CHUNK, dk)
    k = k.reshape(b_, h_, n, CHUNK, dk)
    v = v.reshape(b_, h_, n, CHUNK, dv)
    g = g.reshape(b_, h_, n, CHUNK, dk)
    cum = jnp.cumsum(g, axis=3)
    q_dec = q * jnp.exp(cum)
    k_inv = k * jnp.exp(-cum)
    mask = jnp.tril(jnp.ones((CHUNK, CHUNK), dtype=bool), k=-1 if strict else 0)
    scores = jnp.where(mask, jnp.einsum('bhnid,bhnjd->bhnij', q_dec, k_inv), 0.0)
    o_intra = jnp.einsum('bhnij,bhnjv->bhniv', scores, v)
    last = cum[:, :, :, -1:, :]
    kv = jnp.einsum('bhnjd,bhnjv->bhndv', k * jnp.exp(last - cum), v)
    chunk_decay = jnp.exp(last[:, :, :, 0, :])

    def step(state, inp):
        dec, kv_c = inp
        return dec[..., None] * state + kv_c, state

    _, s_before = lax.scan(step, jnp.zeros((b_, h_, dk, dv), jnp.float32),
                           (jnp.moveaxis(chunk_decay, 2, 0), jnp.moveaxis(kv, 2, 0)))
    s_before = jnp.moveaxis(s_before, 0, 2)
    o_inter = jnp.einsum('bhnid,bhndv->bhniv', q_dec, s_before)
    return (o_intra + o_inter).reshape(b_, h_, l_, dv)


def retention_chunkwise(q, k, v, log_gamma, strict):
    b_, h_, l_, dk = q.shape
    dv = v.shape[-1]
    n = l_ // CHUNK
    q = q.reshape(b_, h_, n, CHUNK, dk)
    k = k.reshape(b_, h_, n, CHUNK, dk)
    v = v.reshape(b_, h_, n, CHUNK, dv)
    idx_i = jnp.arange(CHUNK)
    diff_i = idx_i[:, None] - idx_i[None, :]
    mask = diff_i >= (1 if strict else 0)
    idx = idx_i.astype(jnp.float32)
    diff = jnp.maximum(diff_i, 0).astype(jnp.float32)
    decay_mat = jnp.where(mask[None], jnp.exp(diff[None] * log_gamma[:, None, None]), 0.0)
    scores = jnp.einsum('bhnid,bhnjd->bhnij', q, k) * decay_mat[None, :, None]
    o_intra = jnp.einsum('bhnij,bhnjv->bhniv', scores, v)
    zeta = jnp.exp((CHUNK - 1.0 - idx)[None, :] * log_gamma[:, None])
    xi = jnp.exp((idx + 1.0)[None, :] * log_gamma[:, None])
    kv = jnp.einsum('bhnjd,bhnjv,hj->bhndv', k, v, zeta)
    chunk_decay = jnp.exp(CHUNK * log_gamma)

    def step(state, kv_c):
        return chunk_decay[None, :, None, None] * state + kv_c, state

    _, r_before = lax.scan(step, jnp.zeros((b_, h_, dk, dv), jnp.float32), jnp.moveaxis(kv, 2, 0))
    r_before = jnp.moveaxis(r_before, 0, 2)
    o_inter = jnp.einsum('bhnid,bhndv,hi->bhniv', q, r_before, xi)
    return (o_intra + o_inter).reshape(b_, h_, l_, dv)


def rotary(t):
    dk = t.shape[-1]
    half = dk // 2
    pos = jnp.arange(t.shape[2], dtype=jnp.float32)
    inv = jnp.exp(-math.log(ROPE_BASE) * jnp.arange(half, dtype=jnp.float32) / half)
    ang = pos[:, None] * inv[None, :]
    cos, sin = jnp.cos(ang), jnp.sin(ang)
    t1, t2 = t[..., :half], t[..., half:]
    return jnp.concatenate([t1 * cos - t2 * sin, t1 * sin + t2 * cos], axis=-1)


def s5_gla_mixer(h, w_in, lam_re, lam_im, b_re, b_im, c_re, c_im, log_dt, d_skip, w_glu,
                 w_gk, b_gk, gla_norm, w_out):
    b_, l_, _ = h.shape
    hk = GLA_HEADS * GLA_DK
    hv = GLA_HEADS * GLA_DV
    proj = h @ w_in
    cuts = [S5_WIDTH, S5_WIDTH + hk, S5_WIDTH + 2 * hk, S5_WIDTH + 2 * hk + hv, S5_WIDTH + 2 * hk + 2 * hv]
    u, q, k, v, og, glo = jnp.split(proj, cuts, axis=-1)
    y = s5_bidirectional(u.reshape(b_, l_, S5_GROUPS, S5_GROUP), lam_re, lam_im, b_re, b_im,
                         c_re, c_im, log_dt, d_skip).reshape(b_, l_, S5_WIDTH)
    gy = jax.nn.gelu(y).astype(h.dtype)
    s5_out = gy * jax.nn.sigmoid(gy @ w_glu)
    glo = glo.reshape(b_, l_, 2, GLA_RANK)
    gk = jnp.einsum('blsr,srk->blsk', glo, w_gk) + b_gk
    gk = jax.nn.log_sigmoid(gk.astype(jnp.float32)) / GLA_GATE_NORM
    qh = to_heads(q, GLA_HEADS).astype(jnp.float32) * GLA_DK ** -0.5
    kh = to_heads(k, GLA_HEADS).astype(jnp.float32)
    vh = to_heads(v, GLA_HEADS).astype(jnp.float32)
    gf = to_heads(gk[:, :, 0], GLA_HEADS)
    gb = to_heads(gk[:, :, 1], GLA_HEADS)
    o_f = gla_chunked(qh, kh, vh, gf, strict=False)
    o_b = flip_seq(gla_chunked(flip_seq(qh), flip_seq(kh), flip_seq(vh), flip_seq(gb), strict=True))
    o = head_rmsnorm((o_f + o_b).transpose(0, 2, 1, 3), gla_norm).reshape(b_, l_, hv)
    gla_out = o.astype(h.dtype) * jax.nn.silu(og)
    return jnp.concatenate([s5_out, gla_out], axis=-1) @ w_out


def retention_mixer(h, w_in, ret_norm, w_out):
    b_, l_, _ = h.shape
    hk = RET_HEADS * RET_DK
    hv = RET_HEADS * RET_DV
    q, k, v, og = jnp.split(h @ w_in, [hk, 2 * hk, 2 * hk + hv], axis=-1)
    qh = rotary(to_heads(q, RET_HEADS).astype(jnp.float32))
    kh = rotary(to_heads(k, RET_HEADS).astype(jnp.float32)) * RET_DK ** -0.5
    vh = to_heads(v, RET_HEADS).astype(jnp.float32)
    lg_f = jnp.log1p(-jnp.exp2(-5.0 - jnp.arange(RET_HEADS, dtype=jnp.float32)))
    lg_b = lg_f[::-1]
    o_f = retention_chunkwise(qh, kh, vh, lg_f, strict=False)
    o_b = flip_seq(retention_chunkwise(flip_seq(qh), flip_seq(kh), flip_seq(vh), lg_b, strict=True))
    o = head_rmsnorm((o_f + o_b).transpose(0, 2, 1, 3), ret_norm).reshape(b_, l_, hv)
    return (o.astype(h.dtype) * jax.nn.silu(og)) @ w_out


def setup_inputs(seed: int = 0) -> dict:
    key = jax.random.key(seed)
    ks = jax.random.split(key, 32)
    f32 = jnp.float32

    def nrm(k, shape, scale):
        return jax.random.normal(k, shape, f32) * scale

    def gain(k, shape):
        return 1.0 + 0.02 * jax.random.normal(k, shape, f32)

    n_idx = jnp.arange(S5_STATE, dtype=f32)
    return {
        'x': nrm(ks[0], (BATCH, SEQ, D_MODEL), 1.0),
        'ffn1_norm': gain(ks[1], (DEPTH, D_MODEL)),
        'ffn1_w1': nrm(ks[2], (DEPTH, D_MODEL, 2 * D_FF), D_MODEL ** -0.5),
        'ffn1_w2': nrm(ks[3], (DEPTH, D_FF, D_MODEL), D_FF ** -0.5),
        'mix_norm': gain(ks[4], (DEPTH, D_MODEL)),
        'ffn2_norm': gain(ks[5], (DEPTH, D_MODEL)),
        'ffn2_w1': nrm(ks[6], (DEPTH, D_MODEL, 2 * D_FF), D_MODEL ** -0.5),
        'ffn2_w2': nrm(ks[7], (DEPTH, D_FF, D_MODEL), D_FF ** -0.5),
        'ab_w_in': nrm(ks[8], (N_EVEN, D_MODEL, AB_IN), D_MODEL ** -0.5),
        's5_lambda_re': -0.5 + 0.01 * jax.random.normal(ks[9], (N_EVEN, 2, S5_GROUPS, S5_STATE), f32),
        's5_lambda_im': math.pi * n_idx + 0.01 * jax.random.normal(ks[10], (N_EVEN, 2, S5_GROUPS, S5_STATE), f32),
        's5_b_re': nrm(ks[11], (N_EVEN, 2, S5_GROUPS, S5_STATE, S5_GROUP), (2 * S5_GROUP) ** -0.5),
        's5_b_im': nrm(ks[12], (N_EVEN, 2, S5_GROUPS, S5_STATE, S5_GROUP), (2 * S5_GROUP) ** -0.5),
        's5_c_re': nrm(ks[13], (N_EVEN, 2, S5_GROUPS, S5_GROUP, S5_STATE), (2 * S5_STATE) ** -0.5),
        's5_c_im': nrm(ks[14], (N_EVEN, 2, S5_GROUPS, S5_GROUP, S5_STATE), (2 * S5_STATE) ** -0.5),
        's5_log_dt': jax.random.uniform(ks[15], (N_EVEN, 2, S5_GROUPS), f32, math.log(1e-3), math.log(1e-1)),
        's5_d': nrm(ks[16], (N_EVEN, S5_WIDTH), 1.0),
        's5_w_glu': nrm(ks[17], (N_EVEN, S5_WIDTH, S5_WIDTH), S5_WIDTH ** -0.5),
        'gla_w_gk': nrm(ks[18], (N_EVEN, 2, GLA_RANK, GLA_HEADS * GLA_DK), GLA_RANK ** -0.5),
        'gla_b_gk': nrm(ks[19], (N_EVEN, 2, GLA_HEADS * GLA_DK), 0.1),
        'gla_norm': gain(ks[20], (N_EVEN, GLA_HEADS * GLA_DV)),
        'ab_w_out': nrm(ks[21], (N_EVEN, AB_OUT, D_MODEL), AB_OUT ** -0.5),
        'ret_w_in': nrm(ks[22], (N_ODD, D_MODEL, RET_IN), D_MODEL ** -0.5),
        'ret_norm': gain(ks[23], (N_ODD, RET_OUT)),
        'ret_w_out': nrm(ks[24], (N_ODD, RET_OUT, D_MODEL), RET_OUT ** -0.5),
        'final_norm': gain(ks[25], (D_MODEL,)),
    }


def reference(x, ffn1_norm, ffn1_w1, ffn1_w2, mix_norm, ffn2_norm, ffn2_w1, ffn2_w2,
              ab_w_in, s5_lambda_re, s5_lambda_im, s5_b_re, s5_b_im, s5_c_re, s5_c_im,
              s5_log_dt, s5_d, s5_w_glu, gla_w_gk, gla_b_gk, gla_norm, ab_w_out,
              ret_w_in, ret_norm, ret_w_out, final_norm):
    for i in range(DEPTH):
        j = i // 2
        x = x + 0.5 * swiglu_ffn(rmsnorm(x, ffn1_norm[i]), ffn1_w1[i], ffn1_w2[i])
        h = rmsnorm(x, mix_norm[i])
        if i % 2 == 0:
            x = x + s5_gla_mixer(h, ab_w_in[j], s5_lambda_re[j], s5_lambda_im[j], s5_b_re[j], s5_b_im[j],
                                 s5_c_re[j], s5_c_im[j], s5_log_dt[j], s5_d[j], s5_w_glu[j],
                                 gla_w_gk[j], gla_b_gk[j], gla_norm[j], ab_w_out[j])
        else:
            x = x + retention_mixer(h, ret_w_in[j], ret_norm[j], ret_w_out[j])
        x = x + 0.5 * swiglu_ffn(rmsnorm(x, ffn2_norm[i]), ffn2_w1[i], ffn2_w2[i])
    return rmsnorm(x, final_norm)
```

```python
import math
from contextlib import ExitStack
import numpy as np
import concourse.bass as bass
import concourse.mybir as mybir
from concourse.bass_utils import run_bass_kernel_spmd

F32 = mybir.dt.float32
BF16 = mybir.dt.bfloat16
ALU = mybir.AluOpType
AF = mybir.ActivationFunctionType

D = 1024
DFF = 2816
NJT = DFF // 128
EPS = 1e-6
TT = 512


class Buf:
    __slots__ = ("name", "w", "r")

    def __init__(self, name=""):
        self.name = name
        self.w = None
        self.r = {}


class Eng:
    def __init__(self, name, h, sem):
        self.name = name
        self.h = h
        self.sem = sem
        self.cnt = 0
        self.pending = False
        self.seen = {}


class Prog:
    def __init__(self, nc, stack):
        self.nc = nc
        self.stack = stack
        self.sems = {}
        self.dmaval = {}
        self.E = {}
        for name, h in (("pe", nc.tensor), ("act", nc.scalar), ("dve", nc.vector),
                        ("pool", nc.gpsimd), ("sp", nc.sync)):
            sem = stack.enter_context(nc.semaphore("sem_" + name))
            self.sems[name] = sem
            self.E[name] = Eng(name, h, sem)
        self.nuid = 0
        self.ninst = 0

    def sb(self, shape, dtype, name=None, stack=None):
        self.nuid += 1
        return (stack or self.stack).enter_context(
            self.nc.sbuf_tensor(name or f"sb{self.nuid}", list(shape), dtype))

    def slot(self, name):
        if name not in self.sems:
            self.sems[name] = self.stack.enter_context(self.nc.semaphore("dq_" + name))
            self.dmaval[name] = 0
        return name

    def _deps(self, reads, writes):
        deps = {}

        def add(k, v):
            if deps.get(k, 0) < v:
                deps[k] = v
        for b in reads:
            if b.w is not None:
                add(*b.w)
        for b in writes:
            if b.w is not None:
                add(*b.w)
            for k, v in b.r.items():
                add(k, v)
        return deps

    def _wait(self, e, deps, skip_self=False):
        for k, v in deps.items():
            if skip_self and k == e.name:
                continue
            if e.seen.get(k, 0) >= v:
                continue
            e.h.wait_ge(self.sems[k], v)
            e.seen[k] = v

    def op(self, eng, fn, reads=(), writes=(), signal=True):
        e = self.E[eng]
        self._wait(e, self._deps(reads, writes), skip_self=(eng == "pe"))
        ins = fn()
        self.ninst += 1
        if signal:
            e.cnt += 1
            ins.then_inc(e.sem, 1)
            e.pending = False
            val = e.cnt
        else:
            e.pending = True
            val = e.cnt + 1
        for b in writes:
            b.w = (eng, val)
            b.r = {}
        for b in reads:
            if b.r.get(eng, 0) < val:
                b.r[eng] = val
        return ins

    def dma(self, q, out, in_, reads=(), writes=(), slot=None, **kw):
        e = self.E[q]
        deps = self._deps(reads, writes)
        prev = self.dmaval[slot]
        if prev and deps.get(slot, 0) < prev:
            deps[slot] = prev
        self._wait(e, deps)
        ins = e.h.dma_start(out=out, in_=in_, **kw)
        self.ninst += 1
        self.dmaval[slot] += 16
        val = self.dmaval[slot]
        ins.then_inc(self.sems[slot], 16)
        for b in writes:
            b.w = (slot, val)
            b.r = {}
        for b in reads:
            if b.r.get(slot, 0) < val:
                b.r[slot] = val
        return ins

    def barrier(self):
        targets = {}
        for name, e in self.E.items():
            assert not e.pending, f"{name} has unsignalled instructions at barrier"
            if e.cnt:
                targets[name] = e.cnt
        for s, v in self.dmaval.items():
            if v and s not in getattr(self, "bar_exclude", ()):
                targets[s] = v
        for name, e in self.E.items():
            self._wait(e, {k: v for k, v in targets.items() if k != name})


class WPack:
    def __init__(self):
        self.chunks = {}
        self.arrays = []
        self.off = 0

    def add(self, key, arr):
        arr = np.ascontiguousarray(arr, dtype=np.float32)
        assert arr.shape[0] == 128
        n = arr.size // 128
        self.chunks[key] = (self.off, n)
        self.arrays.append(arr.reshape(-1))
        self.off += arr.size

    def add_meta(self, key, n):
        self.chunks[key] = (self.off, n)
        self.off += 128 * n

    def image(self):
        return np.concatenate(self.arrays)


def kt_split(w):
    k, n = w.shape
    return w.reshape(k // 128, 128, n).transpose(1, 0, 2)


def pack_ffn(pk, tag, w1, w2, meta_only=False):
    for f in range(NJT):
        key = (tag, "w1", f)
        if meta_only:
            pk.add_meta(key, 2 * 8 * 128)
            continue
        g = kt_split(w1[:, f * 128:(f + 1) * 128])
        u = kt_split(w1[:, DFF + f * 128:DFF + (f + 1) * 128])
        pk.add(key, np.stack([g, u], axis=1))
    for m in range(8):
        key = (tag, "w2", m)
        if meta_only:
            pk.add_meta(key, NJT * 128)
            continue
        pk.add(key, kt_split(w2[:, m * 128:(m + 1) * 128]))


def pack_fm(pk, tag, w, ncols_chunk, meta_only=False):
    k, n = w.shape
    assert n % ncols_chunk == 0
    for c in range(n // ncols_chunk):
        key = (tag, c)
        if meta_only:
            pk.add_meta(key, (k // 128) * ncols_chunk)
        else:
            pk.add(key, kt_split(w[:, c * ncols_chunk:(c + 1) * ncols_chunk]))


class Shape:
    def __init__(self, *s):
        self.shape = s


def build_pack(inp, meta_only):
    pk = WPack()
    g = (lambda k: inp[k]) if not meta_only else None
    for i, (tag, a, b) in enumerate([("f10", "ffn1", 0), ("f20", "ffn2", 0), ("f11", "ffn1", 1), ("f21", "ffn2", 1)]):
        if meta_only:
            pack_ffn(pk, tag, None, None, True)
        else:
            pack_ffn(pk, tag, g(a + "_w1")[b], g(a + "_w2")[b])
    return pk


class Ctx:
    def __init__(self, nc, P, L, pk):
        self.nc = nc
        self.P = P
        self.L = L
        self.NT = L // TT
        self.pk = pk
        self.ps = []
        self.psb = []
        for i in range(8):
            self.ps.append(P.stack.enter_context(nc.psum_tensor(f"psb{i}", [128, 512], F32)))
            self.psb.append(Buf(f"ps{i}"))
        self.psi = 0
        self.rot = {}

    def psum(self):
        i = self.psi
        self.psi = (self.psi + 1) % 7
        return self.ps[i], self.psb[i]

    def psum_stat(self):
        return self.ps[7], self.psb[7]

    def psum_fixed(self, i):
        return self.ps[i], self.psb[i]

    def psum_pool(self, name, banks):
        if not hasattr(self, "_pools"):
            self._pools = {}
        st = self._pools.setdefault(name, [0])
        i = banks[st[0] % len(banks)]
        st[0] += 1
        return self.ps[i], self.psb[i]

    def rotbuf(self, key, n, shape, dtype, nb=None):
        if key not in self.rot:
            self.nrot = getattr(self, "nrot", 0) + 1
            stk = getattr(self, "pstack", None)
            mk = (lambda i: Buf(f"{key}{i}")) if nb is None else (lambda i: [Buf(f"{key}{i}_{j}") for j in range(nb)])
            self.rot[key] = [[(self.P.sb(shape, dtype, name=f"{key}{i}_{self.nrot}", stack=stk), mk(i))
                              for i in range(n)], 0]
        lst, i = self.rot[key]
        self.rot[key][1] = (i + 1) % n
        return lst[i]


class WStream:
    def __init__(self, C, wbf, nslots=3, slot_elems=4096, q="sp", name="w"):
        P = C.P
        self.C = C
        self.wbf = wbf
        self.q = q
        self.ns = nslots
        self.tiles = [P.sb([128, slot_elems], BF16, name=f"{name}slot{i}") for i in range(nslots)]
        self.bufs = [Buf(f"{name}slot{i}") for i in range(nslots)]
        self.slots = [P.slot(f"{name}{i}") for i in range(nslots)]
        self.queue = []
        self.issued = 0
        self.consumed = 0

    def schedule(self, keys):
        self.queue.extend(keys)

    def _issue(self):
        P = self.C.P
        while self.issued < min(self.consumed + self.ns, len(self.queue)):
            i = self.issued
            off, n = self.C.pk.chunks[self.queue[i]]
            s = i % self.ns
            src = bass.AP(self.wbf, off, [[n, 128], [1, n]])
            P.dma(self.q, self.tiles[s][:, 0:n], src, writes=[self.bufs[s]], slot=self.slots[s])
            self.issued += 1

    def get(self, key):
        assert self.queue[self.consumed] == key, (self.queue[self.consumed], key)
        bg = getattr(self, "bg", None)
        if bg is not None and self.consumed % self.bg_every == 0:
            if next(bg, "done") == "done":
                self.bg = None
        self._issue()
        i = self.consumed
        self.consumed += 1
        s = i % self.ns
        return self.tiles[s], self.bufs[s]


def mm(C, out, lhsT, rhs, start, stop, reads, writes, signal=None):
    if signal is None:
        signal = stop
    return C.P.op("pe", lambda: C.nc.tensor.matmul(out, lhsT, rhs, start=start, stop=stop),
                  reads=reads, writes=writes, signal=signal)


class RmsState:
    pass


def rms_begin(C):
    st = RmsState()
    st.SQ, st.bSQ = C.rotbuf("rms_sq", 1, [128, 8, TT], BF16, nb=8)
    st.R, st.bR = C.rotbuf("rms_r", 2, [128, TT], F32)
    st.ps, st.bps = C.psum_stat()
    return st


def rms_square(C, st, X, bX, m):
    C.P.op("act", lambda: C.nc.scalar.activation(st.SQ[:, m, :], X[:, m, :], AF.Square), reads=[bX[m]], writes=[st.bSQ[m]])


def rms_stat(C, st, m):
    mm(C, st.ps[:, :], C.ones[:, :], st.SQ[:, m, :], m == 0, m == 7, [st.bSQ[m], C.b_ones], [st.bps])


def rms_finish(C, st, X, bX, G, gcol, H, bH, perm=None):
    P, nc = C.P, C.nc
    R, bR = st.R, st.bR
    P.op("act", lambda: nc.scalar.activation(R[:, :], st.ps[:, :], AF.Ln, bias=EPS, scale=1.0 / D),
         reads=[st.bps], writes=[bR])
    P.op("act", lambda: nc.scalar.activation(R[:, :], R[:, :], AF.Exp, scale=-0.5), reads=[bR], writes=[bR])
    for kt in range(8):
        P.op("dve", lambda kt=kt: nc.vector.scalar_tensor_tensor(
            H[:, kt, :], X[:, kt, :], G[:, gcol + kt:gcol + kt + 1], R[:, :], ALU.mult, ALU.mult),
            reads=[bX[kt], bR, C.bG], writes=[bH[kt]])
        if perm is not None:
            Hp, bHp = perm
            P.op("act", lambda kt=kt: nc.scalar.copy(
                Hp[:, kt, :].rearrange("p (s c) -> p c s", c=32), H[:, kt, :].rearrange("p (c s) -> p c s", s=16)),
                reads=[bH[kt]], writes=[bHp[kt]])


def emit_rmsnorm(C, X, bX, G, gcol, H, bH):
    st = rms_begin(C)
    for m in range(8):
        rms_square(C, st, X, bX, m)
        rms_stat(C, st, m)
    rms_finish(C, st, X, bX, G, gcol, H, bH)


def ffn_keys(tag):
    return [(tag, "w1", f) for f in range(NJT)] + [(tag, "w2", m) for m in range(8)]


def emit_ffn(C, tag, X, bX, H, bH, nxt=None):
    P, nc = C.P, C.nc
    A, bA = C.rotbuf("ffn_a", 1, [128, NJT, TT], BF16, nb=NJT)
    for f in range(NJT):
        wt, bw = C.W.get((tag, "w1", f))
        wv = wt[:, 0:2048].rearrange("p (g k c) -> p g k c", g=2, k=8)
        pg, bpg = C.psum()
        pu, bpu = C.psum()
        for kt in range(8):
            mm(C, pg[:, :], wv[:, 0, kt, :], H[:, kt, :], kt == 0, kt == 7, [bw, bH[kt]], [bpg])
        for kt in range(8):
            mm(C, pu[:, :], wv[:, 1, kt, :], H[:, kt, :], kt == 0, kt == 7, [bw, bH[kt]], [bpu])
        S, bS = C.rotbuf("ffn_s", 2, [128, TT], F32)
        P.op("act", lambda: nc.scalar.activation(S[:, :], pg[:, :], AF.Silu), reads=[bpg], writes=[bS])
        P.op("dve", lambda: nc.vector.tensor_tensor(A[:, f, :], S[:, :], pu[:, :], ALU.mult),
             reads=[bS, bpu], writes=[bA[f]])
    st = rms_begin(C) if nxt is not None else None
    for m in range(8):
        wt, bw = C.W.get((tag, "w2", m))
        wv = wt[:, 0:NJT * 128].rearrange("p (j c) -> p j c", j=NJT)
        py, bpy = C.psum()
        for j in range(NJT):
            mm(C, py[:, :], wv[:, j, :], A[:, j, :], j == 0, j == NJT - 1, [bw, bA[j]], [bpy])
        if st is not None and m >= 1:
            rms_stat(C, st, m - 1)
        P.op("dve", lambda: nc.vector.scalar_tensor_tensor(X[:, m, :], py[:, :], 0.5, X[:, m, :], ALU.mult, ALU.add),
             reads=[bpy, bX[m]], writes=[bX[m]])
        if st is not None:
            rms_square(C, st, X, bX, m)
    if st is not None:
        rms_stat(C, st, 7)
        G, gcol, Hn, bHn = nxt[:4]
        rms_finish(C, st, X, bX, G, gcol, Hn, bHn, perm=(nxt[4] if len(nxt) > 4 else None))


def cast_gen(C, wf32, wbf, lo, hi, tag):
    P = C.P
    CH = 128 * 4096
    sl = [P.slot(f"cast{tag}{i}") for i in range(4)]
    off = lo
    i = 0
    while off < hi:
        n = min(CH, hi - off)
        assert n % 128 == 0
        src = bass.AP(wf32, off, [[n // 128, 128], [1, n // 128]])
        dst = bass.AP(wbf, off, [[n // 128, 128], [1, n // 128]])
        P.dma("pool", dst, src, slot=sl[i % 4])
        off += n
        i += 1
        yield


def emit_cast_weights(C, wf32, wbf, lo, hi, tag):
    for _ in cast_gen(C, wf32, wbf, lo, hi, tag):
        pass


def setup_common(C, nrm_d, gn_d=None):
    P, nc = C.P, C.nc
    C.ones = P.sb([128, 128], BF16, name="ones")
    C.b_ones = Buf("ones")
    P.op("pool", lambda: nc.gpsimd.memset(C.ones[:, :], 1.0), writes=[C.b_ones])
    C.G = P.sb([128, 56], F32, name="gains")
    C.bG = Buf("gains")
    P.slot("misc")
    P.dma("sp", C.G[:, :], nrm_d[:, :], writes=[C.bG], slot="misc")
    C.pstack = None
    if gn_d is not None:
        C.GNT = P.sb([128, 20], F32, name="gnt")
        C.bGN = Buf("gn")
        P.dma("sp", C.GNT[:, :], gn_d[:, :], writes=[C.bGN], slot=P.slot("misc3"))
        C.GN = C.GNT[:, 0:4]
        C.RN = C.GNT[:, 4:20]


def tt(C, eng, out, a, b, op, reads, writes):
    h = C.P.E[eng].h
    return C.P.op(eng, lambda: h.tensor_tensor(out, a, b, op), reads=reads, writes=writes)


def ts(C, eng, out, a, s1, s2, op0, op1, reads, writes):
    h = C.P.E[eng].h
    if s2 is None:
        return C.P.op(eng, lambda: h.tensor_scalar(out, a, s1, None, op0), reads=reads, writes=writes)
    return C.P.op(eng, lambda: h.tensor_scalar(out, a, s1, s2, op0, op1), reads=reads, writes=writes)


def cp(C, eng, out, a, reads, writes):
    if eng == "act":
        return C.P.op("act", lambda: C.nc.scalar.copy(out, a), reads=reads, writes=writes)
    h = C.P.E[eng].h
    return C.P.op(eng, lambda: h.tensor_copy(out, a), reads=reads, writes=writes)


def actf(C, out, a, func, reads, writes, bias=0.0, scale=1.0):
    return C.P.op("act", lambda: C.nc.scalar.activation(out, a, func, bias=bias, scale=scale),
                  reads=reads, writes=writes)


def mset(C, eng, ap, val, writes):
    h = C.P.E[eng].h
    return C.P.op(eng, lambda: h.memset(ap, val), writes=writes)


def emit_sin(C, out, x, shift, tmp, tmpi, reads, writes, eng="dve"):
    P = C.P
    h = P.E[eng].h
    PI = math.pi
    ts(C, eng, tmp, x, shift, 1.0 / (2 * PI), ALU.add, ALU.mult, reads, writes)
    cp(C, eng, tmpi, tmp, reads, writes)
    cp(C, eng, tmp, tmpi, reads, writes)
    C1 = 6.28125
    C2 = 2 * PI - C1
    P.op(eng, lambda: h.scalar_tensor_tensor(out, tmp, -C1, x, ALU.mult, ALU.add), reads=reads, writes=writes)
    P.op(eng, lambda: h.scalar_tensor_tensor(tmp, tmp, -C2, out, ALU.mult, ALU.add), reads=reads, writes=writes)
    if shift != 0.0:
        ts(C, eng, tmp, tmp, shift, None, ALU.add, None, reads, writes)
    ts(C, eng, out, tmp, PI, 2 * PI, ALU.is_gt, ALU.mult, reads, writes)
    tt(C, eng, tmp, tmp, out, ALU.subtract, reads, writes)
    ts(C, eng, out, tmp, -PI, 2 * PI, ALU.is_lt, ALU.mult, reads, writes)
    tt(C, eng, tmp, tmp, out, ALU.add, reads, writes)
    ts(C, eng, tmp, tmp, PI, -PI, ALU.min, ALU.max, reads, writes)
    actf(C, out, tmp, AF.Sin, reads, writes)


S5P_COLS = 64 * 3 + 1024 * 4 + 8
T1 = 16


def s5_pack_params(inp):
    def nmaj(a):
        return np.transpose(a, (2, 0, 1)).reshape(64, 64)
    lamr = nmaj(inp["s5_lambda_re"][0])
    lami = nmaj(inp["s5_lambda_im"][0])
    ldt = np.broadcast_to(inp["s5_log_dt"][0].reshape(1, 64), (64, 64))
    br = np.transpose(inp["s5_b_re"][0], (2, 0, 1, 3)).reshape(64, 1024)
    bi = np.transpose(inp["s5_b_im"][0], (2, 0, 1, 3)).reshape(64, 1024)
    cr = np.transpose(inp["s5_c_re"][0], (3, 0, 1, 2)).reshape(64, 1024)
    ci = np.transpose(inp["s5_c_im"][0], (3, 0, 1, 2)).reshape(64, 1024)
    top = np.concatenate([lamr, lami, ldt, br, bi, cr, ci], axis=1)
    top = np.concatenate([top, top], axis=0)
    dp = np.zeros((128, 8), np.float32)
    dsk = inp["s5_d"][0].reshape(8, 4, 16)
    for gl in range(4):
        dp[32 * gl:32 * gl + 16, :] = dsk[:, gl, :].T
    return np.ascontiguousarray(np.concatenate([top, dp], axis=1), dtype=np.float32)


def s5_setup(C, s5p_d, WIN_d, WOUT_d, TOEP_d, bg=None):
    P, nc = C.P, C.nc
    C.s5AR = P.sb([128, 16, 2, 2], F32, name="s5AR")
    C.s5AI = P.sb([128, 16, 2, 2], F32, name="s5AI")
    C.bs5A = Buf("s5A")
    with ExitStack() as st:
        def sb(shape, dt=F32):
            return P.sb(shape, dt, stack=st), Buf()
        PR, bPR = sb([128, S5P_COLS])
        P.dma("sp", PR[:, :], s5p_d[:, :], writes=[bPR], slot="misc")
        lamr = PR[:, 0:64]; lami = PR[:, 64:128]; ldt = PR[:, 128:192]
        Br = PR[:, 192:1216].rearrange("p (g c) -> p g c", c=16)
        Bi = PR[:, 1216:2240].rearrange("p (g c) -> p g c", c=16)
        Cr = PR[:, 2240:3264].rearrange("p (g c) -> p g c", c=16)
        Ci = PR[:, 3264:4288].rearrange("p (g c) -> p g c", c=16)
        dpad = PR[:, 4288:4296]
        T, bT = sb([128, 12, 64])
        lr = T[:, 0, :]; dt = T[:, 1, :]; mag = T[:, 2, :]; th = T[:, 3, :]
        ar = T[:, 4, :]; ai = T[:, 5, :]; den = T[:, 6, :]; crr = T[:, 7, :]; cii = T[:, 8, :]
        t0 = T[:, 9, :]; t1 = T[:, 10, :]; t2 = T[:, 11, :]
        R, W = [bPR, bT], [bT]
        ts(C, "dve", lr, lamr, -1e-4, None, ALU.min, None, R, W)
        actf(C, dt, ldt, AF.Exp, R, W)
        tt(C, "dve", t0, lr, dt, ALU.mult, R, W)
        actf(C, mag, t0, AF.Exp, R, W)
        tt(C, "dve", th, lami, dt, ALU.mult, R, W)
        TI, bTI = sb([128, 64], mybir.dt.int32)
        emit_sin(C, t1, th, 0.0, t0, TI[:, :], R + [bTI], W + [bTI])
        emit_sin(C, t2, th, math.pi / 2, t0, TI[:, :], R + [bTI], W + [bTI])
        tt(C, "dve", ar, mag, t2, ALU.mult, R, W)
        tt(C, "dve", ai, mag, t1, ALU.mult, R, W)
        tt(C, "dve", den, lr, lr, ALU.mult, R, W)
        tt(C, "dve", t0, lami, lami, ALU.mult, R, W)
        tt(C, "dve", den, den, t0, ALU.add, R, W)
        P.op("dve", lambda: nc.vector.reciprocal(den, den), reads=R, writes=W)
        ts(C, "dve", t0, ar, -1.0, None, ALU.add, None, R, W)
        tt(C, "dve", t1, t0, lr, ALU.mult, R, W)
        tt(C, "dve", t2, ai, lami, ALU.mult, R, W)
        tt(C, "dve", t1, t1, t2, ALU.add, R, W)
        tt(C, "dve", crr, t1, den, ALU.mult, R, W)
        tt(C, "dve", t1, ai, lr, ALU.mult, R, W)
        tt(C, "dve", t2, t0, lami, ALU.mult, R, W)
        tt(C, "dve", t1, t1, t2, ALU.subtract, R, W)
        tt(C, "dve", cii, t1, den, ALU.mult, R, W)
        BB, bBB = sb([128, 2, 64, 16])
        TB, bTB = sb([128, 2, 64, 16])

        def bc(x):
            return x.unsqueeze(2).broadcast_to([128, 64, 16])
        R2 = [bPR, bT, bBB, bTB]
        tt(C, "dve", BB[:, 0], Br, bc(crr), ALU.mult, R2, [bBB])
        tt(C, "dve", TB[:, 0], Bi, bc(cii), ALU.mult, R2, [bTB])
        tt(C, "dve", BB[:, 0], BB[:, 0], TB[:, 0], ALU.subtract, R2, [bBB])
        tt(C, "dve", BB[:, 1], Bi, bc(crr), ALU.mult, R2, [bBB])
        tt(C, "dve", TB[:, 1], Br, bc(cii), ALU.mult, R2, [bTB])
        tt(C, "dve", BB[:, 1], BB[:, 1], TB[:, 1], ALU.add, R2, [bBB])
        PW, bPW = sb([128, 2, 17, 64])
        mset(C, "dve", PW[:, 0, 0, :], 1.0, [bPW])
        mset(C, "dve", PW[:, 1, 0, :], 0.0, [bPW])
        R3 = [bPW, bT]
        for e in range(16):
            pr, pi = PW[:, 0, e, :], PW[:, 1, e, :]
            tt(C, "dve", t0, pr, ar, ALU.mult, R3, [bT])
            tt(C, "dve", t1, pi, ai, ALU.mult, R3, [bT])
            tt(C, "dve", PW[:, 0, e + 1, :], t0, t1, ALU.subtract, R3, [bPW])
            tt(C, "dve", t0, pr, ai, ALU.mult, R3, [bT])
            tt(C, "dve", t1, pi, ar, ALU.mult, R3, [bT])
            tt(C, "dve", PW[:, 1, e + 1, :], t0, t1, ALU.add, R3, [bPW])
        for hh in range(2):
            rows = slice(64 * hh, 64 * hh + 64)
            for d in range(2):
                srcr = PW[rows, 0, 16, d * 32:(d + 1) * 32].rearrange("p (m h) -> p m h", h=2)[:, :, hh]
                srci = PW[rows, 1, 16, d * 32:(d + 1) * 32].rearrange("p (m h) -> p m h", h=2)[:, :, hh]
                for ri in range(2):
                    cp(C, "dve", C.s5AR[rows, :, d, ri], srcr, [bPW], [C.bs5A])
                ts(C, "dve", C.s5AI[rows, :, d, 0], srci, -1.0, None, ALU.mult, None, [bPW], [C.bs5A])
                cp(C, "dve", C.s5AI[rows, :, d, 1], srci, [bPW], [C.bs5A])
        ID, bID = sb([128, 128])
        mset(C, "pool", ID[:, :], 1.0, [bID])
        P.op("pool", lambda: nc.gpsimd.affine_select(
            out=ID[:, :], in_=ID[:, :], pattern=[[-1, 128]], compare_op=ALU.is_equal, fill=0.0,
            base=0, channel_multiplier=1), reads=[bID], writes=[bID])
        IDb, bIDb = sb([128, 128], BF16)
        cp(C, "dve", IDb[:, :], ID[:, :], [bID], [bIDb])
        for s_ in range(4):
            P.slot(f"s5st{s_}")
        X1, bX1 = sb([128, 64, 16])
        X2, bX2 = sb([128, 64, 16])
        BPm, bBPm = sb([128, 2, 64, 128], BF16)
        mset(C, "pool", BPm[:, :, :, :], 0.0, [bBPm])
        for ri in range(2):
            for gl in range(4):
                dst = BPm[:, ri, :, 32 * gl:32 * gl + 16].rearrange("p (a b) c -> p a b c", b=4)[:, :, gl, :]
                src = BB[:, ri].rearrange("p (a b) c -> p a b c", b=4)[:, :, gl, :]
                cp(C, "dve", dst, src, [bBB], [bBPm])
        TST = [sb([128, 4, 128], BF16) for _ in range(2)]
        for t_, b_ in TST:
            mset(C, "pool", t_[:, :, :], 0.0, [b_])
        WO = [sb([128, 16, 2, 128], BF16) for _ in range(2)]
        for t_, b_ in WO:
            mset(C, "pool", t_[:, :, :, :], 0.0, [b_])
        nst = 0
        kwo = 0
        CAe = [sb([128, 2, 64, 16], BF16) for _ in range(2)]
        X3, bX3 = sb([128, 64, 16])
        X4, bX4 = sb([128, 64, 16])
        def tick():
            if bg is not None:
                next(bg, None)
        for e in range(17):
            tick()
            ca, bca = CAe[e % 2]
            pr, pi = bc(PW[:, 0, e, :]), bc(PW[:, 1, e, :])
            tt(C, "dve", X1[:, :, :], Cr, pr, ALU.mult, [bPR, bPW, bX1], [bX1])
            tt(C, "dve", X2[:, :, :], Ci, pi, ALU.mult, [bPR, bPW, bX2], [bX2])
            tt(C, "dve", ca[:, 0], X1[:, :, :], X2[:, :, :], ALU.subtract, [bX1, bX2], [bca])
            tt(C, "pool", X3[:, :, :], Cr, pi, ALU.mult, [bPR, bPW, bX3], [bX3])
            tt(C, "pool", X4[:, :, :], Ci, pr, ALU.mult, [bPR, bPW, bX4], [bX4])
            P.op("dve", lambda: nc.vector.scalar_tensor_tensor(ca[:, 1], X3[:, :, :], -1.0, X4[:, :, :],
                                                              ALU.mult, ALU.subtract), reads=[bX3, bX4], writes=[bca])
            if e < 16:
                for d in range(2):
                    for half in range(2):
                        ps, bps = C.psum()
                        for bq in range(4):
                            blk = half * 4 + bq
                            for gl in range(4):
                                g = d * 32 + blk * 4 + gl
                                for ri in range(2):
                                    mm(C, ps[:, bq * 128 + gl * 32: bq * 128 + gl * 32 + 16],
                                       BPm[0:64, ri, g, :], ca[0:64, ri, g, :], ri == 0, ri == 1, [bBPm, bca], [bps])
                        stg, bstg = TST[nst % 2]
                        nst += 1
                        pv = ps[:, :].rearrange("p (a b c) -> p a b c", a=4, b=4)[:, :, :, 0:16]
                        sv = stg[:, :, :].rearrange("p a (b c) -> p a b c", b=4)[:, :, :, 0:16]
                        cp(C, "act", sv, pv, [bps], [bstg])
                        if d == 0 and e == 0:
                            for bq in range(4):
                                blk = half * 4 + bq
                                P.op("dve", lambda: nc.vector.scalar_tensor_tensor(
                                    stg[:, bq, :], ID[:, :], dpad[:, blk:blk + 1], stg[:, bq, :],
                                    ALU.mult, ALU.add), reads=[bID, bPR, bstg], writes=[bstg])
                        dst = TOEP_d[half * 4:half * 4 + 4, :, d, e, :].rearrange("b p c -> p b c")
                        P.dma("sp", dst, stg[:, :, :], reads=[bstg], slot=f"s5st{nst % 4}")
            if e >= 1:
                for d in range(2):
                    s = (e - 1) if d == 0 else (16 - e)
                    wo, bwo = WO[kwo % 2]
                    kwo += 1
                    for hh in range(2):
                        rows = slice(64 * hh, 64 * hh + 64)
                        for ri in range(2):
                            src = ca[rows, ri, d * 32:(d + 1) * 32, :]
                            src = src.rearrange("p (q mp h) c -> p q mp h c", mp=2, h=2)
                            for mp in range(2):
                                co = 64 * mp + 32 * hh
                                dstv = wo[rows, :, ri, co:co + 16].rearrange("p (q mp) c -> p q mp c", mp=2)[:, :, mp, :]
                                cp(C, "dve" if ri == 0 else "pool", dstv, src[:, :, mp, hh, :], [bca], [bwo])
                    for r_ in range(2):
                        dst = WOUT_d[:, :, d, s, r_, :].rearrange("m p c -> p m c")
                        P.dma("sp", dst, wo[:, :, r_, :], reads=[bwo], slot=f"s5st{(2 * kwo + r_) % 4}")
        WP, bWP = sb([128, 2, 64, 32], BF16)
        mset(C, "pool", WP[:, :, :, :], 0.0, [bWP])
        WST = [sb([128, 8, 2, 128], BF16) for _ in range(2)]
        for t_, b_ in WST:
            mset(C, "pool", t_[:, :, :, :], 0.0, [b_])
        nw = 0
        for e in range(16):
            tick()
            pr, pi = bc(PW[:, 0, e, :]), bc(PW[:, 1, e, :])
            tt(C, "dve", X1[:, :, :], BB[:, 0], pr, ALU.mult, [bBB, bPW, bX1], [bX1])
            tt(C, "dve", X2[:, :, :], BB[:, 1], pi, ALU.mult, [bBB, bPW, bX2], [bX2])
            tt(C, "dve", WP[:, 0, :, 0:16], X1[:, :, :], X2[:, :, :], ALU.subtract, [bX1, bX2], [bWP])
            tt(C, "pool", X3[:, :, :], BB[:, 1], pr, ALU.mult, [bBB, bPW, bX3], [bX3])
            tt(C, "dve", X4[:, :, :], BB[:, 0], pi, ALU.mult, [bBB, bPW, bX4], [bX4])
            tt(C, "dve", WP[:, 1, :, 0:16], X3[:, :, :], X4[:, :, :], ALU.add, [bX3, bX4], [bWP])
            for d in range(2):
                s = (15 - e) if d == 0 else e
                for ri in range(2):
                    ps, bps = C.psum()
                    for blk in range(8):
                        g0 = d * 32 + blk * 4
                        lhsT = WP[0:64, ri, g0:g0 + 4, :].rearrange("p a b -> p (a b)")
                        mm(C, ps[:, blk * 64:(blk + 1) * 64], lhsT, IDb[0:64, 0:64], True, True, [bWP, bIDb], [bps])
                    stg, bstg = WST[nw % 2]
                    nw += 1
                    pv = ps[:, :].rearrange("p (b n) -> p b n", n=64)
                    for gl in range(4):
                        r = slice(32 * gl, 32 * gl + 32)
                        cp(C, "act" if gl % 2 == 0 else "dve",
                           stg[r, :, gl // 2, (gl % 2) * 64:(gl % 2) * 64 + 64], pv[r, :, :], [bps], [bstg])
                    for a_ in range(2):
                        dst = WIN_d[:, :, a_, d, s, ri, :].rearrange("b p c -> p b c")
                        P.dma("sp", dst, stg[:, :, a_, :], reads=[bstg], slot=f"s5st{(2 * nw + a_) % 4}")
        if bg is not None:
            for _ in bg:
                pass
        P.barrier()


def s5_main(C, UT_d, WIN_d, WOUT_d, TOEP_d, GY_d, mid=None):
    P, nc = C.P, C.nc
    L = C.L
    NB = L // 512
    NCH = L // 16
    for i in range(2):
        P.slot(f"s5u{i}"); P.slot(f"s5w{i}"); P.slot(f"s5o{i}"); P.slot(f"s5tp{i}"); P.slot(f"s5wo{i}")
    with ExitStack() as st0:
        XC = P.sb([128, 16, 2, 2, NCH], BF16, stack=st0)
        bXC = Buf("XC")
        bXS = Buf("XS")
        with ExitStack() as st:
            U = [(P.sb([128, L], BF16, stack=st), Buf()) for _ in range(2)]
            Wn = [(P.sb([128, 2, 2, 16, 2, 128], BF16, stack=st), Buf()) for _ in range(1)]
            nev = 0
            for blk in range(8):
                u, bu = U[blk % 2]
                w, bw = Wn[0]
                P.dma("sp", u[:, :], UT_d[blk, :, :], writes=[bu], slot=f"s5u{blk % 2}")
                P.dma("sp", w[:, :, :, :, :, :], WIN_d[blk], writes=[bw], slot="s5w0")
                uv = u[:, :].rearrange("p (b s c) -> p b s c", s=16, c=32)
                for pair in range(2):
                    for d in range(2):
                        for ri in range(2):
                            ps, bps = C.psum()
                            ov = ps[:, 0:NCH].rearrange("p (b c) -> p b c", c=32)
                            for s in range(16):
                                mm(C, ov, w[:, pair, d, s, ri, :], uv[:, :, s, :], s == 0, s == 15, [bu, bw], [bps])
                            cp(C, "act" if nev % 2 == 0 else "dve", XC[:, 2 * blk + pair, d, ri, :], ps[:, 0:NCH],
                               [bps], [bXC])
                            nev += 1
            P.barrier()
        S = [(P.sb([128, 16, 2, 2], F32, stack=st0), Buf()) for _ in range(3)]
        TA = [(P.sb([128, 16, 2, 2], F32, stack=st0), Buf()) for _ in range(2)]
        TB = [(P.sb([128, 16, 2, 2], F32, stack=st0), Buf()) for _ in range(2)]

        def scan_gen():
            mset(C, "pool", S[0][0][:, :, :, :], 0.0, [S[0][1]])
            tot = 16 * 2 * 2 * NCH
            for i in range(NCH):
                cur, bcur = S[i % 3]
                nxt, bnxt = S[(i + 1) % 3]
                ta, bta = TA[i % 2]
                tb, btb = TB[i % 2]
                sw = bass.AP(cur, 1, [[64, 128], [2, 32], [-1, 2]])
                cf, cb = i, NCH - 1 - i
                xsel = bass.AP(XC, cf, [[tot, 128], [4 * NCH, 16], [2 * NCH + (cb - cf), 2], [NCH, 2]])
                tt(C, "pool", ta[:, :, :, :], cur[:, :, :, :], C.s5AR[:, :, :, :], ALU.mult, [bcur, C.bs5A], [bta])
                tt(C, "pool", tb[:, :, :, :].rearrange("p a b c -> p (a b) c"), sw,
                   C.s5AI[:, :, :, :].rearrange("p a b c -> p (a b) c"), ALU.mult, [bcur, C.bs5A], [btb])
                tt(C, "pool", ta[:, :, :, :], ta[:, :, :, :], tb[:, :, :, :], ALU.add, [bta, btb], [bta])
                tt(C, "pool", nxt[:, :, :, :], ta[:, :, :, :], xsel, ALU.add, [bta, bXC], [bnxt])
                cp(C, "pool", xsel, cur[:, :, :, :], [bcur, bnxt], [bXS])
                yield

        g = scan_gen()
        if mid is not None:
            mid(g)
        for _ in g:
            pass
        P.barrier()
        with ExitStack() as st:
            U = [(P.sb([128, L], BF16, stack=st), Buf()) for _ in range(2)]
            TP = [(P.sb([128, 2, 16, 128], BF16, stack=st), Buf()) for _ in range(1)]
            WOt = [(P.sb([128, 2, 16, 2, 128], BF16, stack=st), Buf()) for _ in range(2)]
            YI = (P.sb([128, L], F32, stack=st), Buf())
            YS = [(P.sb([128, 512], F32, stack=st), Buf()) for _ in range(2)]
            GS = [(P.sb([128, 512], BF16, stack=st), Buf()) for _ in range(2)]
            nev = 0
            for blk in range(8):
                u, bu = U[blk % 2]
                tp, btp = TP[0]
                yi, byi = YI
                P.dma("sp", u[:, :], UT_d[blk, :, :], writes=[bu], slot=f"s5u{blk % 2}")
                P.dma("sp", tp[:, :, :, :], TOEP_d[blk], writes=[btp], slot="s5tp0")
                for h in range(2):
                    P.dma("sp", WOt[h][0][:, :, :, :, :], WOUT_d[2 * blk + h], writes=[WOt[h][1]], slot=f"s5wo{h}")
                yv = yi[:, :].rearrange("p (b s c) -> p b s c", s=16, c=32)
                for s in range(16):
                    ps, bps = C.psum()
                    k = 0
                    for h in range(2):
                        for d in range(2):
                            for ri in range(2):
                                mm(C, ps[:, 0:NCH], WOt[h][0][:, d, s, ri, :], XC[:, 2 * blk + h, d, ri, :],
                                   k == 0, k == 7, [WOt[h][1], bXS], [bps])
                                k += 1
                    cp(C, "act" if s % 2 == 0 else "dve", yv[:, :, s, :],
                       ps[:, 0:NCH].rearrange("p (b c) -> p b c", c=32), [bps], [byi])
                for bank in range(NB):
                    ps, bps = C.psum()
                    o = bank * 512
                    for tau in range(16):
                        n = (16 - tau) * 32
                        mm(C, ps[:, tau * 32:512], tp[:, 0, tau, :], u[:, o:o + n], tau == 0, False, [btp, bu], [bps],
                           signal=False)
                    for tau in range(16):
                        n = (16 - tau) * 32
                        mm(C, ps[:, 0:n], tp[:, 1, tau, :], u[:, o + tau * 32:o + 512], False, tau == 15, [btp, bu], [bps])
                    ys, bys = YS[nev % 2]
                    gs, bgs = GS[nev % 2]
                    tt(C, "dve", ys[:, :], ps[:, :], yi[:, o:o + 512], ALU.add, [bps, byi], [bys])
                    actf(C, gs[:, :].rearrange("p (c s) -> p s c", s=16), ys[:, :].rearrange("p (s c) -> p s c", c=32),
                         AF.Gelu, [bys], [bgs])
                    for gl in range(4):
                        r0 = (blk % 2) * 64 + gl * 16
                        P.dma("pool", GY_d[blk // 2, r0:r0 + 16, o:o + 512], gs[32 * gl:32 * gl + 16, :], reads=[bgs],
                              slot=P.slot(f"s5st{gl}" if nev % 2 == 0 else f"castA{gl}"))
                    nev += 1
            P.barrier()
    P.barrier()


def s5_dram(nc, kind="Internal"):
    WIN_d = nc.dram_tensor("s5WIN", [8, 128, 2, 2, 16, 2, 128], BF16, kind=kind)
    WOUT_d = nc.dram_tensor("s5WOUT", [16, 128, 2, 16, 2, 128], BF16, kind=kind)
    TOEP_d = nc.dram_tensor("s5TOEP", [8, 128, 2, 16, 128], BF16, kind=kind)
    return WIN_d, WOUT_d, TOEP_d


def build_test_s5(L):
    pk = build_pack(None, True)
    nc = bass.Bass("TRN2", target_bir_lowering=False)
    s5p = nc.dram_tensor("s5p", [128, S5P_COLS], F32, kind="ExternalInput")
    UT = nc.dram_tensor("UT", [8, 128, L], BF16, kind="ExternalInput")
    GY = nc.dram_tensor("GY", [8, 128, L], BF16, kind="ExternalOutput")
    WIN_d, WOUT_d, TOEP_d = s5_dram(nc, "ExternalOutput")
    with ExitStack() as st:
        P = Prog(nc, st)
        C = Ctx(nc, P, L, pk)
        P.slot("misc")
        s5_setup(C, s5p, WIN_d, WOUT_d, TOEP_d)
        s5_main(C, UT, WIN_d, WOUT_d, TOEP_d, GY)
        print("instructions:", P.ninst)
    return nc


GC = 128


def gla_pack_params(inp):
    wg = np.zeros((32, 2, 256), np.float32)
    for s in range(2):
        wg[16 * s:16 * s + 16, s, :] = inp["gla_w_gk"][0, s]
    bg = np.ascontiguousarray(inp["gla_b_gk"][0].reshape(2, 2, 128).transpose(2, 0, 1), dtype=np.float32)
    return wg.reshape(32, 512), bg.reshape(128, 4)


def gla_main(C, GQ_d, GK_d, GV_d, GLO_d, GOG_d, gkw_d, gkb_d, GO_d, bg=None, bg_per_chunk=4):
    P, nc = C.P, C.nc
    L = C.L
    NT = L // TT
    NCg = L // GC
    CPT = TT // GC
    for i in range(2):
        for nm in ("gq", "gk", "gv", "gl", "gg", "go"):
            P.slot(f"{nm}{i}")
    with ExitStack() as st:
        def sb(shape, dt=F32, name=None):
            return P.sb(shape, dt, stack=st), Buf(name or "")
        WG32, bWG32 = sb([32, 512])
        WG, bWG = sb([32, 2, 2, 128], BF16)
        BG, bBG = sb([128, 4])
        P.dma("sp", WG32[:, :], gkw_d[:, :], writes=[bWG32], slot=P.slot("misc1"))
        P.dma("sp", BG[:, :], gkb_d[:, :], writes=[bBG], slot=P.slot("misc2"))
        cp(C, "dve", WG[:, :, :, :].rearrange("p a b c -> p (a b c)"), WG32[:, :], [bWG32], [bWG])
        ts(C, "dve", BG[:, :], BG[:, :], -1.0, None, ALU.mult, None, [bBG], [bBG])
        MK, bMK = sb([128, 256])
        mset(C, "pool", MK[:, :], 1.0, [bMK])
        P.op("pool", lambda: nc.gpsimd.affine_select(out=MK[:, 0:128], in_=MK[:, 0:128], pattern=[[1, 128]],
                                                     compare_op=ALU.is_ge, fill=0.0, base=0, channel_multiplier=-1),
             reads=[bMK], writes=[bMK])
        P.op("pool", lambda: nc.gpsimd.affine_select(out=MK[:, 128:256], in_=MK[:, 128:256], pattern=[[-1, 128]],
                                                     compare_op=ALU.is_gt, fill=0.0, base=0, channel_multiplier=1),
             reads=[bMK], writes=[bMK])
        MSf, bMSf = sb([128, TT])
        MSb, bMSb = sb([128, TT])
        mset(C, "pool", MSf[:, :], 1.0, [bMSf])
        mset(C, "pool", MSb[:, :], 1.0, [bMSb])
        mset(C, "pool", MSf[:, :].rearrange("p (c s) -> p c s", s=GC)[:, :, 0:1], 0.0, [bMSf])
        mset(C, "pool", MSb[:, :].rearrange("p (c s) -> p c s", s=GC)[:, :, GC - 1:GC], 0.0, [bMSb])
        IDb, bIDb = sb([128, 128], BF16)
        ID, bID = sb([128, 128])
        mset(C, "pool", ID[:, :], 1.0, [bID])
        P.op("pool", lambda: nc.gpsimd.affine_select(out=ID[:, :], in_=ID[:, :], pattern=[[-1, 128]],
                                                     compare_op=ALU.is_equal, fill=0.0, base=0, channel_multiplier=1),
             reads=[bID], writes=[bID])
        cp(C, "dve", IDb[:, :], ID[:, :], [bID], [bIDb])
        SBs, bSBs = sb([128, NCg, 2, 128], BF16, "SBs")
        def run_bg(n=1):
            if bg is not None:
                for _ in range(n):
                    next(bg, None)
        Sst = [sb([128, 2, 128]) for _ in range(2)]
        Sbf = [sb([128, 2, 128], BF16) for _ in range(2)]
        rot = {}

        def rb(key, n, shape, dt=F32):
            if key not in rot:
                rot[key] = [[sb(shape, dt, key) for _ in range(n)], 0]
            lst, i = rot[key]
            rot[key][1] = (i + 1) % n
            return lst[i]

        def rev(t, n):
            return bass.AP(t, n - 1, [[n, 128], [-1, n]])

        def gates(t, d, GL, bGL, Q, bQ, Kt, bK, getps=None):
            getps = getps or C.psum
            QD, bQD = rb("QD", 4, [128, 2, TT], BF16)
            KI, bKI = rb("KI", 4, [128, 2, TT], BF16)
            KE, bKE = rb("KE", 2, [128, 2, TT], BF16)
            DEC, bDEC = rb("DEC", 2, [128, 2, CPT])
            for th in range(2):
                ps, bps = getps()
                mm(C, ps[:, :], WG[:, d, th, :], GL[:, :], True, True, [bWG, bGL], [bps])
                E, bE = rb("gE", 1, [128, TT])
                Lg, bLg = E, bE
                Cm, bCm = rb("gC", 1, [128, TT])
                E1, bE1 = rb("gE1", 2, [128, TT])
                E2, bE2 = Cm, bCm
                col = d * 2 + th
                P.op("act", lambda: nc.scalar.activation(E[:, :], ps[:, :], AF.Exp, bias=BG[:, col:col + 1], scale=-1.0),
                     reads=[bps, bBG], writes=[bE])
                actf(C, Lg[:, :], E[:, :], AF.Ln, [bE], [bLg], bias=1.0)
                if d == 0:
                    P.op("dve", lambda: nc.vector.tensor_tensor_scan(Cm[:, :], MSf[:, :], Lg[:, :], 0.0, ALU.mult, ALU.add),
                         reads=[bMSf, bLg], writes=[bCm])
                else:
                    P.op("dve", lambda: nc.vector.tensor_tensor_scan(rev(Cm, TT), rev(MSb, TT), rev(Lg, TT), 0.0,
                                                                     ALU.mult, ALU.add),
                         reads=[bMSb, bLg], writes=[bCm])
                actf(C, E1[:, :], Cm[:, :], AF.Exp, [bCm], [bE1], scale=-1.0 / 16)
                actf(C, E2[:, :], Cm[:, :], AF.Exp, [bCm], [bE2], scale=1.0 / 16)
                P.op("dve", lambda: nc.vector.scalar_tensor_tensor(QD[:, th, :], Q[:, th, :], 0.125, E1[:, :],
                                                                  ALU.mult, ALU.mult), reads=[bQ, bE1], writes=[bQD])
                tt(C, "dve", KI[:, th, :], Kt[:, th, :], E2[:, :], ALU.mult, [bK, bE2], [bKI])
                e1c = E1[:, :].rearrange("p (c s) -> p c s", s=GC)
                dcol = e1c[:, :, GC - 1] if d == 0 else e1c[:, :, 0]
                cp(C, "dve", DEC[:, th, :], dcol, [bE1], [bDEC])
                tt(C, "dve", KE[:, th, :].rearrange("p (c s) -> p c s", s=GC),
                   KI[:, th, :].rearrange("p (c s) -> p c s", s=GC),
                   DEC[:, th, :].unsqueeze(2).broadcast_to([128, CPT, GC]), ALU.mult, [bKI, bDEC], [bKE])
            return (QD, bQD), (KI, bKI), (KE, bKE), (DEC, bDEC)

        def load_tile(t, want_q):
            i = t % 2
            sl = slice(t * TT, (t + 1) * TT)
            GL, bGL = rb("GL", 2, [32, TT], BF16)
            Kt, bK = rb("Kt", 2, [128, 2, TT], BF16)
            V, bV = rb("V", 2, [128, CPT, 512], BF16)
            Q, bQ = rb("Q", 2, [128, 2, TT], BF16)
            P.dma("sp", GL[:, :], GLO_d[:, sl], writes=[bGL], slot=f"gl{i}")
            P.dma("sp", Kt[:, :, :], GK_d[:, :, sl].rearrange("a p t -> p a t"), writes=[bK], slot=f"gk{i}")
            P.dma("sp", V[:, :, :], GV_d[sl, :].rearrange("(c p) f -> p c f", p=128), writes=[bV], slot=f"gv{i}")
            P.dma("sp", Q[:, :, :], GQ_d[:, :, sl].rearrange("a p t -> p a t"), writes=[bQ], slot=f"gq{i}")
            return (GL, bGL), (Kt, bK), (V, bV), (Q, bQ)

        def kv_update(d, c_in_tile, th, KE, bKE, V, bV, DEC, bDEC, getps=None):
            getps = getps or C.psum
            S, bS = Sst[d]
            cs = slice(c_in_tile * GC, (c_in_tile + 1) * GC)
            pt, bpt = getps()
            mm(C, pt[:, 0:128], KE[:, th, cs], IDb[:, :], True, True, [bKE, bIDb], [bpt])
            KEt, bKEt = rb("KEt", 2, [128, 128], BF16)
            cp(C, "act", KEt[:, :], pt[:, 0:128], [bpt], [bKEt])
            pk_, bpk = getps()
            for hh in range(2):
                h = th * 2 + hh
                mm(C, pk_[:, hh * 128:(hh + 1) * 128], KEt[:, :], V[:, c_in_tile, h * 128:(h + 1) * 128],
                   True, True, [bKEt, bV], [bpk], signal=(hh == 1))
            for hh in range(2):
                r = slice(64 * hh, 64 * hh + 64)
                P.op("dve", lambda: nc.vector.scalar_tensor_tensor(
                    S[r, th, :], S[r, th, :], DEC[r, th, c_in_tile:c_in_tile + 1], pk_[r, hh * 128:(hh + 1) * 128],
                    ALU.mult, ALU.add), reads=[bS, bDEC, bpk], writes=[bS])

        for d in range(2):
            mset(C, "dve", Sst[d][0][:, :, :], 0.0, [Sst[d][1]])
        dbg = getattr(C, "dbg", 9)
        for t in range(NT - 1, -1, -1):
            (GL, bGL), (Kt, bK), (V, bV), (Q, bQ) = load_tile(t, True)
            (QD, bQD), (KI, bKI), (KE, bKE), (DEC, bDEC) = gates(t, 1, GL, bGL, Q, bQ, Kt, bK)
            for ci in range(CPT - 1, -1, -1):
                if dbg < 2:
                    break
                c = t * CPT + ci
                cp(C, "act", SBs[:, c, :, :], Sst[1][0][:, :, :], [Sst[1][1]], [bSBs])
                for th in range(2):
                    kv_update(1, ci, th, KE, bKE, V, bV, DEC, bDEC)
                    run_bg(max(1, bg_per_chunk // 4))
        getR = lambda: C.psum_pool("glaR", [2, 3, 4, 5, 6])
        pos = [C.psum_fixed(0), C.psum_fixed(1)]

        def post(c, OG, bOG, cs):
            gs = slice(c * GC, (c + 1) * GC)
            GO, bGO = rb("GO", 2, [128, 4, GC], BF16)
            for hh in range(2):
                po, bpo = pos[hh]
                SQ, bSQ = rb("SQ", 2, [128, 256], BF16)
                actf(C, SQ[:, :], po[:, 0:256], AF.Square, [bpo], [bSQ])
                pst, bpst = C.psum_stat()
                mm(C, pst[:, 0:256], C.ones[:, :], SQ[:, :], True, True, [bSQ, C.b_ones], [bpst])
                RS, bRS = rb("RS", 2, [128, 256])
                actf(C, RS[:, :], pst[:, 0:256], AF.Ln, [bpst], [bRS], bias=EPS, scale=1.0 / 128)
                actf(C, RS[:, :], RS[:, :], AF.Exp, [bRS], [bRS], scale=-0.5)
                ON, bON = rb("ON", 2, [128, 256])
                tt(C, "dve", ON[:, :], po[:, 0:256], RS[:, :], ALU.mult, [bpo, bRS], [bON])
                gov = GO[:, :, :].rearrange("p (a b) i -> p a b i", b=2)[:, :, hh, :]
                ogv = OG[:, :, cs].rearrange("p (a b) i -> p a b i", b=2)[:, :, hh, :]
                tt(C, "dve", gov, ON[:, :].rearrange("p (a i) -> p a i", i=GC), ogv, ALU.mult, [bON, bOG], [bGO])
            P.dma("pool", GO_d[:, :, gs].rearrange("h p t -> p h t"), GO[:, :, :], reads=[bGO], slot=f"go{c % 2}")

        prev = None
        for t in range(NT if dbg >= 3 else 0):
            (GL, bGL), (Kt, bK), (V, bV), (Q, bQ) = load_tile(t, True)
            (QDb, bQDb), (KIb, bKIb), _, _ = gates(t, 1, GL, bGL, Q, bQ, Kt, bK, getR)
            (QD, bQD), (KI, bKI), (KE, bKE), (DEC, bDEC) = gates(t, 0, GL, bGL, Q, bQ, Kt, bK, getR)
            OG, bOG = rb("OG", 2, [128, 4, TT], BF16)
            P.dma("sp", OG[:, :, :], GOG_d[:, :, t * TT:(t + 1) * TT].rearrange("h p t -> p h t"), writes=[bOG],
                  slot=f"gg{t % 2}")
            for ci in range(CPT):
                c = t * CPT + ci
                cs = slice(ci * GC, (ci + 1) * GC)
                gs = slice(c * GC, (c + 1) * GC)
                sf, bsf = Sbf[c % 2]
                cp(C, "act", sf[:, :, :], Sst[0][0][:, :, :], [Sst[0][1]], [bsf])
                for th in range(2):
                    kv_update(0, ci, th, KE, bKE, V, bV, DEC, bDEC, getR)
                    run_bg(max(1, bg_per_chunk // 4))
                if prev is not None:
                    post(*prev)
                run_bg(max(1, bg_per_chunk // 2))
                Ats = []
                for h in range(4):
                    th, hh = h // 2, h % 2
                    r = slice(64 * hh, 64 * hh + 64)
                    psc, bpsc = getR()
                    mm(C, psc[:, 0:128], KI[r, th, cs], QD[r, th, cs], True, True, [bKI, bQD], [bpsc], signal=False)
                    mm(C, psc[:, 128:256], KIb[r, th, cs], QDb[r, th, cs], True, True, [bKIb, bQDb], [bpsc])
                    At, bAt = rb("At", 4, [128, 256], BF16)
                    tt(C, "dve", At[:, :], psc[:, 0:256], MK[:, :], ALU.mult, [bpsc, bMK], [bAt])
                    Ats.append((At, bAt))
                for h in range(4):
                    th, hh = h // 2, h % 2
                    r = slice(64 * hh, 64 * hh + 64)
                    po, bpo = pos[hh]
                    At, bAt = Ats[h]
                    oh = po[:, th * 128:(th + 1) * 128]
                    vh = V[:, ci, h * 128:(h + 1) * 128]
                    mm(C, oh, vh, At[:, 0:128], True, False, [bV, bAt], [bpo], signal=False)
                    mm(C, oh, vh, At[:, 128:256], False, False, [bV, bAt], [bpo], signal=False)
                    mm(C, oh, SBs[r, c, th, :], QDb[r, th, cs], False, False, [bSBs, bQDb], [bpo], signal=False)
                    mm(C, oh, sf[r, th, :], QD[r, th, cs], False, True, [bsf, bQD], [bpo], signal=(th == 1))
                run_bg(max(1, bg_per_chunk // 2))
                prev = (c, OG, bOG, cs)
        if prev is not None:
            post(*prev)
        P.barrier()


def build_test_gla(L):
    pk = build_pack(None, True)
    nc = bass.Bass("TRN2", target_bir_lowering=False)
    GQ = nc.dram_tensor("GQ", [2, 128, L], BF16, kind="ExternalInput")
    GK = nc.dram_tensor("GK", [2, 128, L], BF16, kind="ExternalInput")
    GV = nc.dram_tensor("GV", [L, 512], BF16, kind="ExternalInput")
    GLO = nc.dram_tensor("GLO", [32, L], BF16, kind="ExternalInput")
    GOG = nc.dram_tensor("GOG", [4, 128, L], BF16, kind="ExternalInput")
    gkw = nc.dram_tensor("gkw", [32, 512], F32, kind="ExternalInput")
    gkb = nc.dram_tensor("gkb", [128, 4], F32, kind="ExternalInput")
    nrm = nc.dram_tensor("nrm", [128, 56], F32, kind="ExternalInput")
    GO = nc.dram_tensor("GO", [4, 128, L], BF16, kind="ExternalOutput")
    with ExitStack() as st:
        P = Prog(nc, st)
        C = Ctx(nc, P, L, pk)
        setup_common(C, nrm)
        gla_main(C, GQ, GK, GV, GLO, GOG, gkw, gkb, GO)
        print("instructions:", P.ninst)
    return nc


def proj_fm(C, key, ntiles, nk, rhs_fn, rhs_bufs, evac_fn, perm_out=False):
    wt, bw = C.W.get(key)
    wv = wt[:, 0:nk * ntiles * 128].rearrange("p (k c) -> p k c", k=nk)
    for j in range(ntiles):
        ps, bps = C.psum()
        out = ps[:, :].rearrange("p (s c) -> p s c", c=32) if perm_out else ps[:, :]
        for kt in range(nk):
            rb_ = rhs_bufs(kt) if callable(rhs_bufs) else rhs_bufs
            mm(C, out, wv[:, kt, j * 128:(j + 1) * 128], rhs_fn(kt), kt == 0, kt == nk - 1, [bw] + rb_, [bps])
        evac_fn(j, ps, bps)


def proj_tm(C, key, H, bH, nk, evac_fn):
    wt, bw = C.W.get(key)
    wv = wt[:, 0:nk * 512].rearrange("p (k c) -> p k c", k=nk)
    for i in range(TT // 128):
        ps, bps = C.psum()
        for kt in range(nk):
            mm(C, ps[:, :], H[:, kt, i * 128:(i + 1) * 128], wv[:, kt, :], kt == 0, kt == nk - 1, [bw, bH[kt]], [bps])
        evac_fn(i, ps, bps)


def resid_proj(C, tag, nchunks, nk, rhs_fn, rhs_bufs, X, bX, nxt, resident=None):
    P, nc = C.P, C.nc
    st = rms_begin(C)
    for c in range(nchunks):
        if resident is not None:
            wt, bw = resident[0][:, c, :], resident[1][c]
        else:
            wt, bw = C.W.get((tag, c))
        wv = wt[:, 0:nk * 256].rearrange("p (k c) -> p k c", k=nk)
        for j in range(2):
            m = 2 * c + j
            ps, bps = C.psum()
            for kt in range(nk):
                mm(C, ps[:, :], wv[:, kt, j * 128:(j + 1) * 128], rhs_fn(kt), kt == 0, kt == nk - 1,
                   [bw] + rhs_bufs(kt), [bps])
            if m >= 1:
                rms_stat(C, st, m - 1)
            tt(C, "dve", X[:, m, :], ps[:, :], X[:, m, :], ALU.add, [bps, bX[m]], [bX[m]])
            rms_square(C, st, X, bX, m)
    rms_stat(C, st, 7)
    G, gcol, Hn, bHn = nxt
    rms_finish(C, st, X, bX, G, gcol, Hn, bHn)


def p1_keys():
    ks = ffn_keys("f10")
    ks += [("abu", c) for c in range(4)] + [("abq", 0), ("abk", 0), ("abog", 0), ("abog", 1), ("abglo", 0), ("abv", 0)]
    return ks


def pack_p1(pk, inp, meta):
    if meta:
        pack_fm(pk, "abu", Shape(1024, 1024), 256, True)
        pack_fm(pk, "abq", Shape(1024, 256), 256, True)
        pack_fm(pk, "abk", Shape(1024, 256), 256, True)
        pack_fm(pk, "abog", Shape(1024, 512), 256, True)
        pk.add_meta(("abglo", 0), 8 * 128)
        pk.add_meta(("abv", 0), 8 * 512)
        return
    w = inp["ab_w_in"][0]
    wu = np.zeros((1024, 32, 32), np.float32)
    wu[:, :, 0:16] = w[:, 0:512].reshape(1024, 32, 16)
    pack_fm(pk, "abu", wu.reshape(1024, 1024), 256)
    pack_fm(pk, "abq", w[:, 512:768], 256)
    pack_fm(pk, "abk", w[:, 768:1024], 256)
    pack_fm(pk, "abog", w[:, 1536:2048], 256)
    wg = np.zeros((1024, 128), np.float32)
    wg[:, 0:32] = w[:, 2048:2080]
    pk.add(("abglo", 0), kt_split(wg))
    pk.add(("abv", 0), kt_split(w[:, 1024:1536]))


class Shape:
    def __init__(self, *s):
        self.shape = s


def pack_fm(pk, tag, w, ncols_chunk, meta_only=False):
    k, n = w.shape
    assert n % ncols_chunk == 0
    for c in range(n // ncols_chunk):
        key = (tag, c)
        if meta_only:
            pk.add_meta(key, (k // 128) * ncols_chunk)
        else:
            pk.add(key, kt_split(w[:, c * ncols_chunk:(c + 1) * ncols_chunk]))


def phase1(C, xT_d, X1_d, UT_d, GQ_d, GK_d, GV_d, GLO_d, GOG_d):
    P, nc = C.P, C.nc
    xv = xT_d[:, :].rearrange("(k p) t -> p k t", p=128)
    x1v = X1_d[:, :].rearrange("(k p) t -> p k t", p=128)
    for i in range(2):
        for nm in ("xin", "xout", "p1u", "p1q", "p1k", "p1g", "p1l", "p1v"):
            P.slot(f"{nm}{i}")
    for t in range(C.NT):
        C.W.schedule(p1_keys())
    with ExitStack() as st:
        C.pstack = st
        C.rot = {}
        def load_x(t):
            X, bX = C.rotbuf("X", 2, [128, 8, TT], F32, nb=8)
            P.dma("sp", X[:, :, :], xv[:, :, t * TT:(t + 1) * TT], writes=bX, slot=f"xin{t % 2}")
            return X, bX
        nxt_x = load_x(0)
        nxt_h = C.rotbuf("H", 2, [128, 8, TT], BF16, nb=8)
        emit_rmsnorm(C, nxt_x[0], nxt_x[1], C.G, 0, nxt_h[0], nxt_h[1])
        for t in range(C.NT):
            sl = slice(t * TT, (t + 1) * TT)
            i2 = t % 2
            X, bX = nxt_x
            H, bH = nxt_h
            if t + 1 < C.NT:
                nxt_x = load_x(t + 1)
            Hp, bHp = C.rotbuf("Hp", 1, [128, 8, TT], BF16, nb=8)
            emit_ffn(C, "f10", X, bX, H, bH, nxt=(C.G, 8, H, bH, (Hp, bHp)))
            if t + 1 < C.NT:
                nxt_h = C.rotbuf("H", 2, [128, 8, TT], BF16, nb=8)
                emit_rmsnorm(C, nxt_x[0], nxt_x[1], C.G, 0, nxt_h[0], nxt_h[1])
            P.dma("pool", x1v[:, :, sl], X[:, :, :], reads=bX, slot=f"xout{i2}")
            US, bUS = C.rotbuf("US", 2, [128, 8, TT], BF16)
            for c in range(4):
                def ev(j, ps, bps, c=c):
                    cp(C, "act" if j == 0 else "dve", US[:, 2 * c + j, :], ps[:, :], [bps], [bUS])
                proj_fm(C, ("abu", c), 2, 8, lambda kt: Hp[:, kt, :], lambda kt: [bHp[kt]], ev)
            P.dma("pool", UT_d[:, :, sl].rearrange("b p t -> p b t"), US[:, :, :], reads=[bUS], slot=f"p1u{i2}")
            hnat = lambda kt: H[:, kt, :]
            QS, bQS = C.rotbuf("QS", 2, [128, 2, TT], BF16)
            KS, bKS = C.rotbuf("KS", 2, [128, 2, TT], BF16)
            proj_fm(C, ("abq", 0), 2, 8, hnat, lambda kt: [bH[kt]],
                    lambda j, ps, bps: cp(C, "act" if j == 0 else "dve", QS[:, j, :], ps[:, :], [bps], [bQS]))
            P.dma("pool", GQ_d[:, :, sl].rearrange("a p t -> p a t"), QS[:, :, :], reads=[bQS], slot=f"p1q{i2}")
            proj_fm(C, ("abk", 0), 2, 8, hnat, lambda kt: [bH[kt]],
                    lambda j, ps, bps: cp(C, "act" if j == 0 else "dve", KS[:, j, :], ps[:, :], [bps], [bKS]))
            P.dma("pool", GK_d[:, :, sl].rearrange("a p t -> p a t"), KS[:, :, :], reads=[bKS], slot=f"p1k{i2}")
            OGS, bOGS = C.rotbuf("OGS", 2, [128, 4, TT], BF16)
            for c in range(2):
                def ev(j, ps, bps, c=c):
                    h = 2 * c + j
                    SG, bSG = C.rotbuf("SG", 2, [128, TT], F32)
                    actf(C, SG[:, :], ps[:, :], AF.Silu, [bps], [bSG])
                    ts(C, "dve", OGS[:, h, :], SG[:, :], C.GN[:, h:h + 1], None, ALU.mult, None, [bSG, C.bGN], [bOGS])
                proj_fm(C, ("abog", c), 2, 8, hnat, lambda kt: [bH[kt]], ev)
            P.dma("pool", GOG_d[:, :, sl].rearrange("a p t -> p a t"), OGS[:, :, :], reads=[bOGS], slot=f"p1g{i2}")
            LS, bLS = C.rotbuf("LS", 2, [32, TT], BF16)
            proj_fm(C, ("abglo", 0), 1, 8, hnat, lambda kt: [bH[kt]],
                    lambda j, ps, bps: cp(C, "act", LS[:, :], ps[0:32, :], [bps], [bLS]))
            P.dma("pool", GLO_d[:, sl], LS[:, :], reads=[bLS], slot=f"p1l{i2}")
            VS, bVS = C.rotbuf("VS", 2, [128, 4, 512], BF16)
            proj_tm(C, ("abv", 0), H, bH, 8,
                    lambda i, ps, bps: cp(C, "act" if i % 2 == 0 else "dve", VS[:, i, :], ps[:, :], [bps], [bVS]))
            P.dma("pool", GV_d[sl, :].rearrange("(c p) f -> p c f", p=128), VS[:, :, :], reads=[bVS], slot=f"p1v{i2}")
        if getattr(C.W, "bg", None) is not None:
            for _ in C.W.bg:
                pass
            C.W.bg = None
        P.barrier()
    C.pstack = None


RC = 128
LGF = [math.log1p(-2.0 ** (-5 - h)) for h in range(8)]
LGB = LGF[::-1]


def ret_main(C, RQ_d, RK_d, RV_d, ROG_d, SBD_d, RO_d):
    P, nc = C.P, C.nc
    L = C.L
    NCr = L // RC
    for i in range(2):
        for nm in ("rq", "rk", "rv", "rg", "rs", "ro"):
            P.slot(f"{nm}{i}")
    P.slot("rg2")
    with ExitStack() as st:
        def sb(shape, dt=F32, name=None):
            return P.sb(shape, dt, stack=st), Buf(name or "")
        rot = {}

        def rb(key, n, shape, dt=F32):
            if key not in rot:
                rot[key] = [[sb(shape, dt, key) for _ in range(n)], 0]
            lst, i = rot[key]
            rot[key][1] = (i + 1) % n
            return lst[i]
        ID, bID = sb([128, 128])
        IDb, bIDb = sb([128, 128], BF16)
        mset(C, "pool", ID[:, :], 1.0, [bID])
        P.op("pool", lambda: nc.gpsimd.affine_select(out=ID[:, :], in_=ID[:, :], pattern=[[-1, 128]],
                                                     compare_op=ALU.is_equal, fill=0.0, base=0, channel_multiplier=1),
             reads=[bID], writes=[bID])
        cp(C, "dve", IDb[:, :], ID[:, :], [bID], [bIDb])
        EI, bEI = sb([128, 128], mybir.dt.int32)
        E, bE = sb([128, 128])
        Ep, bEp = sb([128, 128])
        En, bEn = sb([128, 128])
        P.op("pool", lambda: nc.gpsimd.iota(EI[:, :], [[1, 128]], base=0, channel_multiplier=-1), writes=[bEI])
        cp(C, "dve", E[:, :], EI[:, :], [bEI], [bE])
        ts(C, "dve", Ep[:, :], E[:, :], 0.0, None, ALU.max, None, [bE], [bEp])
        ts(C, "dve", En[:, :], E[:, :], -1.0, 0.0, ALU.mult, ALU.max, [bE], [bEn])
        DT, bDT = sb([128, 8, 128])
        ARG, bARG = sb([128, 128])
        for h in range(8):
            ts(C, "dve", ARG[:, :], Ep[:, :], LGF[h], None, ALU.mult, None, [bEp], [bARG])
            P.op("dve", lambda: nc.vector.scalar_tensor_tensor(ARG[:, :], En[:, :], LGB[h], ARG[:, :], ALU.mult, ALU.add),
                 reads=[bEn, bARG], writes=[bARG])
            actf(C, DT[:, h, :], ARG[:, :], AF.Exp, [bARG], [bDT])
        IRI, bIRI = sb([128, 128], mybir.dt.int32)
        IR, bIR = sb([128, 128])
        P.op("pool", lambda: nc.gpsimd.iota(IRI[:, :], [[1, 128]], base=0, channel_multiplier=0), writes=[bIRI])
        cp(C, "dve", IR[:, :], IRI[:, :], [bIRI], [bIR])
        IPI, bIPI = sb([128, 1], mybir.dt.int32)
        IP, bIP = sb([128, 1])
        P.op("pool", lambda: nc.gpsimd.iota(IPI[:, :], [[0, 1]], base=0, channel_multiplier=1), writes=[bIPI])
        cp(C, "dve", IP[:, :], IPI[:, :], [bIPI], [bIP])
        XIf, bXIf = sb([128, 8, 128], BF16)
        XIb, bXIb = sb([128, 8, 128], BF16)
        ZF, bZF = sb([128, 8])
        ZB, bZB = sb([128, 8])
        for h in range(8):
            actf(C, XIf[:, h, :], IR[:, :], AF.Exp, [bIR], [bXIf], bias=LGF[h], scale=LGF[h])
            actf(C, XIb[:, h, :], IR[:, :], AF.Exp, [bIR], [bXIb], bias=RC * LGB[h], scale=-LGB[h])
            actf(C, ZF[:, h:h + 1], IP[:, :], AF.Exp, [bIP], [bZF], bias=(RC - 1) * LGF[h], scale=-LGF[h])
            actf(C, ZB[:, h:h + 1], IP[:, :], AF.Exp, [bIP], [bZB], scale=LGB[h])
        GF = [math.exp(RC * LGF[h]) for h in range(8)]
        GB = [math.exp(RC * LGB[h]) for h in range(8)]
        Sp = [sb([128, 8, 256], F32, "Sstate0")[0], sb([128, 8, 256], F32, "Sstate1")[0]]
        bSp = [[Buf() for _ in range(8)], [Buf() for _ in range(8)]]

        def load_kv(c):
            i = c % 2
            cs = slice(c * RC, (c + 1) * RC)
            Kt, bK = rb("Kt", 2, [128, 8, RC], BF16)
            V, bV = rb("V", 2, [128, 2048], BF16)
            P.dma("sp", Kt[:, :, :], RK_d[:, :, cs].rearrange("h p t -> p h t"), writes=[bK], slot=f"rk{i}")
            P.dma("sp", V[:, :], RV_d[cs, :], writes=[bV], slot=f"rv{i}")
            return Kt, bK, V, bV

        def kv_update(k, Kt, bK, V, bV, Z, bZ, GAM, getps=None):
            getps = getps or C.psum
            So, bSo = Sp[k % 2], bSp[k % 2]
            Sn, bSn = Sp[(k + 1) % 2], bSp[(k + 1) % 2]
            for g in range(2):
                pt, bpt = getps()
                for hl in range(4):
                    h = 4 * g + hl
                    mm(C, pt[:, hl * 128:(hl + 1) * 128], Kt[:, h, :], IDb[:, :], True, True, [bK, bIDb], [bpt],
                       signal=(hl == 3))
                Kz, bKz = rb("Kz", 2, [128, 4, 128], BF16)
                tt(C, "dve", Kz[:, :, :], pt[:, :].rearrange("p (a b) -> p a b", b=128),
                   Z[:, 4 * g:4 * g + 4].unsqueeze(2).broadcast_to([128, 4, 128]), ALU.mult, [bpt, bZ], [bKz])
                for hp in range(2):
                    pk_, bpk = getps()
                    for hq in range(2):
                        hl = 2 * hp + hq
                        h = 4 * g + hl
                        mm(C, pk_[:, hq * 256:(hq + 1) * 256], Kz[:, hl, :], V[:, h * 256:(h + 1) * 256], True, True,
                           [bKz, bV], [bpk], signal=(hq == 1))
                    for hq in range(2):
                        h = 4 * g + 2 * hp + hq
                        P.op("dve", lambda: nc.vector.scalar_tensor_tensor(
                            Sn[:, h, :], So[:, h, :], GAM[h], pk_[:, hq * 256:(hq + 1) * 256], ALU.mult, ALU.add),
                            reads=[bSo[h], bpk], writes=[bSn[h]])

        mset(C, "dve", Sp[0][:, :, :], 0.0, bSp[0])
        k = 0
        for c in range(NCr - 1, -1, -1):
            Kt, bK, V, bV = load_kv(c)
            Sb16, bSb16 = rb("Sb16", 2, [128, 8, 256], BF16)
            cp(C, "act", Sb16[:, :, :], Sp[k % 2][:, :, :], bSp[k % 2], [bSb16])
            P.dma("act", SBD_d[c], Sb16[:, :, :], reads=[bSb16], slot=f"rs{c % 2}")
            kv_update(k, Kt, bK, V, bV, ZB, bZB, GB)
            k += 1
        P.barrier()
        mset(C, "dve", Sp[k % 2][:, :, :], 0.0, bSp[k % 2])
        RP = [4, 5, 6]
        getR = lambda: C.psum_pool("retR", RP)
        OB = [[C.psum_fixed(0), C.psum_fixed(1)], [C.psum_fixed(2), C.psum_fixed(3)]]
        for c in range(NCr):
            i = c % 2
            cs = slice(c * RC, (c + 1) * RC)
            Kt, bK, V, bV = load_kv(c)
            Q, bQ = rb("Q", 2, [128, 8, RC], BF16)
            SB, bSB = rb("SB", 2, [128, 8, 256], BF16)
            OG, bOG = rb("OG", 2, [128, 16, RC], BF16)
            P.dma("sp", Q[:, :, :], RQ_d[:, :, cs].rearrange("h p t -> p h t"), writes=[bQ], slot=f"rq{i}")
            P.dma("sp", SB[:, :, :], SBD_d[c], writes=[bSB], slot=f"rs{i}")
            P.dma("sp", OG[:, :, :], ROG_d[:, :, cs].rearrange("h p t -> p h t"), writes=[bOG], slot=f"rg{i}")
            Sf16, bSf16 = rb("Sf16", 2, [128, 8, 256], BF16)
            cp(C, "act", Sf16[:, :, :], Sp[k % 2][:, :, :], bSp[k % 2], [bSf16])
            Qf, bQf = rb("Qf", 2, [128, 8, RC], BF16)
            Qb, bQb = rb("Qb", 2, [128, 8, RC], BF16)
            tt(C, "pool", Qf[:, :, :], Q[:, :, :], XIf[:, :, :], ALU.mult, [bQ, bXIf], [bQf])
            tt(C, "pool", Qb[:, :, :], Q[:, :, :], XIb[:, :, :], ALU.mult, [bQ, bXIb], [bQb])
            kv_update(k, Kt, bK, V, bV, ZF, bZF, GF, getR)
            k += 1
            At, bAt = rb("At", 2, [128, 8, RC], BF16, )
            bAtg = [Buf(), Buf()]
            for g in range(2):
                psc, bpsc = getR()
                for hl in range(4):
                    h = 4 * g + hl
                    mm(C, psc[:, hl * 128:(hl + 1) * 128], Kt[:, h, :], Q[:, h, :], True, True, [bK, bQ], [bpsc],
                       signal=(hl == 3))
                tt(C, "dve", At[:, 4 * g:4 * g + 4, :], psc[:, :].rearrange("p (a b) -> p a b", b=128),
                   DT[:, 4 * g:4 * g + 4, :], ALU.mult, [bpsc, bDT, bAt], [bAtg[g]])
            RO, bRO = rb("RO", 2, [128, 16, RC], BF16)
            sqs = []
            for g in range(2):
                for dvt in range(2):
                    pb, bpb = OB[g][dvt]
                    for hl in range(4):
                        h = 4 * g + hl
                        oh = pb[:, hl * 128:(hl + 1) * 128]
                        vs = slice(h * 256 + dvt * 128, h * 256 + dvt * 128 + 128)
                        ss = slice(dvt * 128, dvt * 128 + 128)
                        mm(C, oh, V[:, vs], At[:, h, :], True, False, [bV, bAtg[g]], [bpb], signal=False)
                        mm(C, oh, SB[:, h, ss], Qb[:, h, :], False, False, [bSB, bQb], [bpb], signal=False)
                        mm(C, oh, Sf16[:, h, ss], Qf[:, h, :], False, True, [bSf16, bQf], [bpb], signal=(hl == 3))
                for dvt in range(2):
                    pb, bpb = OB[g][dvt]
                    SQ, bSQ = rb("SQ", 4, [128, 512], BF16)
                    actf(C, SQ[:, :], pb[:, :], AF.Square, [bpb], [bSQ])
                    sqs.append((g, dvt, SQ, bSQ))
            for g in range(2):
                pst, bpst = C.psum_stat()
                for (g_, dvt, SQ, bSQ) in sqs:
                    if g_ == g:
                        mm(C, pst[:, :], C.ones[:, :], SQ[:, :], dvt == 0, dvt == 1, [bSQ, C.b_ones], [bpst])
                RS, bRS = rb("RS", 2, [128, 512])
                actf(C, RS[:, :], pst[:, :], AF.Ln, [bpst], [bRS], bias=EPS, scale=1.0 / 256)
                actf(C, RS[:, :], RS[:, :], AF.Exp, [bRS], [bRS], scale=-0.5)
                for dvt in range(2):
                    pb, bpb = OB[g][dvt]
                    ON, bON = rb("ON", 2, [128, 512])
                    tt(C, "dve", ON[:, :], pb[:, :], RS[:, :], ALU.mult, [bpb, bRS], [bON])
                    rov = RO[:, 8 * g:8 * g + 8, :].rearrange("p (a b) i -> p a b i", b=2)[:, :, dvt, :]
                    ogv = OG[:, 8 * g:8 * g + 8, :].rearrange("p (a b) i -> p a b i", b=2)[:, :, dvt, :]
                    tt(C, "pool", rov, ON[:, :].rearrange("p (a i) -> p a i", i=RC), ogv, ALU.mult, [bON, bOG], [bRO])
            P.dma("pool", RO_d[:, :, cs].rearrange("h p t -> p h t"), RO[:, :, :], reads=[bRO], slot=f"ro{c % 2}")
        P.barrier()


def build_test_ret(L):
    pk = build_pack(None, True)
    nc = bass.Bass("TRN2", target_bir_lowering=False)
    RQ = nc.dram_tensor("RQ", [8, 128, L], BF16, kind="ExternalInput")
    RK = nc.dram_tensor("RK", [8, 128, L], BF16, kind="ExternalInput")
    RV = nc.dram_tensor("RV", [L, 2048], BF16, kind="ExternalInput")
    ROG = nc.dram_tensor("ROG", [16, 128, L], BF16, kind="ExternalInput")
    nrm = nc.dram_tensor("nrm", [128, 56], F32, kind="ExternalInput")
    SBD = nc.dram_tensor("SBD", [L // RC, 128, 8, 256], BF16, kind="Internal")
    RO = nc.dram_tensor("RO", [16, 128, L], BF16, kind="ExternalOutput")
    with ExitStack() as st:
        P = Prog(nc, st)
        C = Ctx(nc, P, L, pk)
        setup_common(C, nrm)
        ret_main(C, RQ, RK, RV, ROG, SBD, RO)
        print("instructions:", P.ninst)
    return nc


def pad_rows_s5(w):
    out = np.zeros((32, 32, w.shape[1]), np.float32)
    out[:, 0:16, :] = w.reshape(32, 16, w.shape[1])
    return out.reshape(1024, w.shape[1])


def pack_p3(pk, inp, meta):
    if meta:
        pack_fm(pk, "wglu", Shape(512, 512), 256, True)
        pack_fm(pk, "abo", Shape(1024, 1024), 256, True)
        for h in range(8):
            pk.add_meta(("rqk", h), 8 * 256)
        pack_fm(pk, "rog", Shape(1024, 2048), 256, True)
        pack_fm(pk, "rv", Shape(1024, 2048), 512, True)
        return
    pack_fm(pk, "wglu", inp["s5_w_glu"][0], 256)
    pack_fm(pk, "abo", inp["ab_w_out"][0], 256)
    w = inp["ret_w_in"][0]
    for h in range(8):
        q = w[:, h * 128:(h + 1) * 128]
        k = w[:, 1024 + h * 128:1024 + (h + 1) * 128]
        pk.add(("rqk", h), kt_split(np.concatenate([q, k], axis=1)))
    pack_fm(pk, "rog", w[:, 4096:6144], 256)
    pack_fm(pk, "rv", w[:, 2048:4096], 512)


def pack_p5(pk, inp, meta):
    if meta:
        pack_fm(pk, "reto", Shape(2048, 1024), 256, True)
    else:
        pack_fm(pk, "reto", inp["ret_w_out"][0], 256)


def p3_keys():
    ks = [("wglu", c) for c in range(2)] + [("abo", c) for c in range(4)]
    ks += ffn_keys("f20") + ffn_keys("f11")
    ks += [("rqk", h) for h in range(8)] + [("rog", c) for c in range(8)] + [("rv", c) for c in range(4)]
    return ks


def p5_keys():
    return ffn_keys("f21")


def rotary_setup(C):
    P, nc = C.P, C.nc
    C.ROT = P.sb([128, 4], F32, name="rotc")
    C.bROT = Buf("rotc")
    IPI = P.sb([128, 1], mybir.dt.int32, name="rot_ipi")
    bI = Buf()
    P.op("pool", lambda: nc.gpsimd.iota(IPI[:, :], [[0, 1]], base=0, channel_multiplier=1), writes=[bI])
    R = [bI, C.bROT]
    cp(C, "dve", C.ROT[:, 2:3], IPI[:, :], R, [C.bROT])
    ts(C, "dve", C.ROT[:, 3:4], C.ROT[:, 2:3], 64.0, None, ALU.is_ge, None, R, [C.bROT])
    ts(C, "dve", C.ROT[:, 1:2], C.ROT[:, 3:4], 2.0, -1.0, ALU.mult, ALU.add, R, [C.bROT])
    P.op("dve", lambda: nc.vector.scalar_tensor_tensor(C.ROT[:, 2:3], C.ROT[:, 3:4], -64.0, C.ROT[:, 2:3],
                                                      ALU.mult, ALU.add), reads=R, writes=[C.bROT])
    actf(C, C.ROT[:, 0:1], C.ROT[:, 2:3], AF.Exp, R, [C.bROT], scale=-math.log(10000.0) / 64.0)


def phase3(C, X1_d, GY_d, GO_d, X4_d, RQ_d, RK_d, RV_d, ROG_d):
    P, nc = C.P, C.nc
    x1v = X1_d[:, :].rearrange("(k p) t -> p k t", p=128)
    x4v = X4_d[:, :].rearrange("(k p) t -> p k t", p=128)
    for i in range(2):
        for nm in ("xin", "xout", "p3y", "p3o", "p3q", "p3k", "p3g", "p3v"):
            P.slot(f"{nm}{i}")
    for i in range(8):
        P.slot(f"rsw{i}")
    for t in range(C.NT):
        C.W.schedule(p3_keys())
    with ExitStack() as st:
        C.pstack = st
        C.rot = {}
        POSI = P.sb([128, TT], mybir.dt.int32, stack=st)
        bPOSI = Buf()
        TI = P.sb([128, TT], mybir.dt.int32, stack=st)

        def load_in(t):
            sl = slice(t * TT, (t + 1) * TT)
            X, bX = C.rotbuf("X", 2, [128, 8, TT], F32, nb=8)
            GY, bGY = C.rotbuf("GY", 2, [128, 4, TT], BF16)
            GO, bGO = C.rotbuf("GO", 2, [128, 4, TT], BF16)
            P.dma("sp", X[:, :, :], x1v[:, :, sl], writes=bX, slot=f"xin{t % 2}")
            P.dma("sp", GY[:, :, :], GY_d[:, :, sl].rearrange("b p t -> p b t"), writes=[bGY], slot=f"p3y{t % 2}")
            P.dma("sp", GO[:, :, :], GO_d[:, :, sl].rearrange("h p t -> p h t"), writes=[bGO], slot=f"p3o{t % 2}")
            return (X, bX), (GY, bGY), (GO, bGO)
        for t in range(C.NT):
            sl = slice(t * TT, (t + 1) * TT)
            i2 = t % 2
            if t == 0:
                nxt_in = load_in(0)
            (X, bX), (GY, bGY), (GO, bGO) = nxt_in
            if t + 1 < C.NT:
                nxt_in = load_in(t + 1)
            H, bH = C.rotbuf("H", 1, [128, 8, TT], BF16, nb=8)
            S5O, bS5O = C.rotbuf("S5O", 1, [128, 4, TT], BF16)
            for c in range(2):
                def ev(j, ps, bps, c=c):
                    m = 2 * c + j
                    SG, bSG = C.rotbuf("SG", 2, [128, TT], F32)
                    actf(C, SG[:, :], ps[:, :], AF.Sigmoid, [bps], [bSG])
                    tt(C, "dve", S5O[:, m, :], GY[:, m, :], SG[:, :], ALU.mult, [bGY, bSG], [bS5O])
                proj_fm(C, ("wglu", c), 2, 4, lambda kt: GY[:, kt, :], [bGY], ev)
            resid_proj(C, "abo", 4, 8, lambda kt: S5O[:, kt, :] if kt < 4 else GO[:, kt - 4, :],
                       lambda kt: [bS5O, bGO], X, bX, (C.G, 16, H, bH))
            emit_ffn(C, "f20", X, bX, H, bH, nxt=(C.G, 24, H, bH))
            emit_ffn(C, "f11", X, bX, H, bH, nxt=(C.G, 32, H, bH))
            P.dma("pool", x4v[:, :, sl], X[:, :, :], reads=bX, slot=f"xout{i2}")
            TB, bTB = C.rotbuf("rtab", 1, [128, 6, TT], F32)
            RT = [bTB, C.bROT, bPOSI]
            P.op("pool", lambda: nc.gpsimd.iota(POSI[:, :], [[1, TT]], base=t * TT, channel_multiplier=0),
                 reads=[bPOSI], writes=[bPOSI])
            cp(C, "dve", TB[:, 0, :], POSI[:, :], RT, [bTB])
            ts(C, "dve", TB[:, 0, :], TB[:, 0, :], C.ROT[:, 0:1], None, ALU.mult, None, RT, [bTB])
            emit_sin(C, TB[:, 3, :], TB[:, 0, :], 0.0, TB[:, 1, :], TI[:, :], RT, [bTB])
            emit_sin(C, TB[:, 2, :], TB[:, 0, :], math.pi / 2, TB[:, 1, :], TI[:, :], RT, [bTB])
            ts(C, "dve", TB[:, 3, :], TB[:, 3, :], C.ROT[:, 1:2], None, ALU.mult, None, RT, [bTB])
            ksc = 128.0 ** -0.5
            ts(C, "dve", TB[:, 4, :], TB[:, 2, :], ksc, None, ALU.mult, None, RT, [bTB])
            ts(C, "dve", TB[:, 5, :], TB[:, 3, :], ksc, None, ALU.mult, None, RT, [bTB])
            hnat = lambda kt: H[:, kt, :]
            for h in range(8):
                def ev(j, ps, bps, h=h):
                    nsw = C.nsw = getattr(C, "nsw", 0) + 1
                    r4 = nsw % 4
                    QF, bQF = C.rotbuf("rQF", 4, [128, TT], F32)
                    QS, bQS = C.rotbuf("rQS", 4, [128, TT], F32, nb=2)
                    T1, bT1 = C.rotbuf("rT1", 2, [128, TT], F32)
                    T2, bT2 = C.rotbuf("rT2", 2, [128, TT], F32)
                    O_, bO_ = C.rotbuf("rOQ", 4, [128, TT], BF16)
                    cp(C, "act", QF[:, :], ps[:, :], [bps], [bQF])
                    P.dma("act", QS[0:64, :], QF[64:128, :], reads=[bQF], writes=[bQS[0]], slot=f"rsw{2 * r4}")
                    P.dma("act", QS[64:128, :], QF[0:64, :], reads=[bQF], writes=[bQS[1]], slot=f"rsw{2 * r4 + 1}")
                    base = 2 if j == 0 else 4
                    tt(C, "dve", T1[:, :], QF[:, :], TB[:, base, :], ALU.mult, [bQF, bTB], [bT1])
                    tt(C, "pool" if j == 0 else "dve", T2[:, :], QS[:, :], TB[:, base + 1, :], ALU.mult, bQS + [bTB], [bT2])
                    tt(C, "dve", O_[:, :], T1[:, :], T2[:, :], ALU.add, [bT1, bT2], [bO_])
                    dst = (RQ_d if j == 0 else RK_d)[h, :, sl]
                    P.dma("pool", dst, O_[:, :], reads=[bO_], slot=f"p3q{h % 2}" if j == 0 else f"p3k{h % 2}")
                proj_fm(C, ("rqk", h), 2, 8, hnat, lambda kt: [bH[kt]], ev)
            for c in range(8):
                OGS, bOGS = C.rotbuf("rOGS", 2, [128, 2, TT], BF16)

                def ev(j, ps, bps, c=c, OGS=OGS, bOGS=bOGS):
                    idx = 2 * c + j
                    SG, bSG = C.rotbuf("SG", 2, [128, TT], F32)
                    actf(C, SG[:, :], ps[:, :], AF.Silu, [bps], [bSG])
                    ts(C, "dve", OGS[:, j, :], SG[:, :], C.RN[:, idx:idx + 1], None, ALU.mult, None, [bSG, C.bGN], [bOGS])
                proj_fm(C, ("rog", c), 2, 8, hnat, lambda kt: [bH[kt]], ev)
                P.dma("pool", ROG_d[2 * c:2 * c + 2, :, sl].rearrange("a p t -> p a t"), OGS[:, :, :], reads=[bOGS],
                      slot=f"p3g{c % 2}")
            for c in range(4):
                VS, bVS = C.rotbuf("rVS", 2, [128, 4, 512], BF16)
                proj_tm(C, ("rv", c), H, bH, 8,
                        lambda i, ps, bps, VS=VS, bVS=bVS: cp(C, "act" if i % 2 == 0 else "dve", VS[:, i, :], ps[:, :],
                                                              [bps], [bVS]))
                P.dma("pool", RV_d[sl, c * 512:(c + 1) * 512].rearrange("(c p) f -> p c f", p=128), VS[:, :, :],
                      reads=[bVS], slot=f"p3v{c % 2}")
        P.barrier()
    C.pstack = None


def phase5(C, X4_d, RO_d, outT_d):
    P, nc = C.P, C.nc
    x4v = X4_d[:, :].rearrange("(k p) t -> p k t", p=128)
    ov = outT_d[:, :].rearrange("(k p) t -> p k t", p=128)
    for i in range(2):
        for nm in ("xin", "xout", "p5o"):
            P.slot(f"{nm}{i}")
    for t in range(C.NT):
        C.W.schedule(p5_keys())
    with ExitStack() as st:
        C.pstack = st
        C.rot = {}

        RW = P.sb([128, 4, 4096], BF16, stack=st, name="reto_res")
        bRW = [Buf(f"reto{c}") for c in range(4)]
        for c in range(4):
            off, n = C.pk.chunks[("reto", c)]
            P.dma("sp", RW[:, c, 0:n], bass.AP(C.W.wbf, off, [[n, 128], [1, n]]), writes=[bRW[c]],
                  slot=P.slot(f"s5st{c}"))

        def load_in(t):
            sl = slice(t * TT, (t + 1) * TT)
            X, bX = C.rotbuf("X", 2, [128, 8, TT], F32, nb=8)
            RO, bRO = C.rotbuf("RO", 2, [128, 16, TT], BF16)
            P.dma("sp", X[:, :, :], x4v[:, :, sl], writes=bX, slot=f"xin{t % 2}")
            P.dma("sp", RO[:, :, :], RO_d[:, :, sl].rearrange("h p t -> p h t"), writes=[bRO], slot=f"p5o{t % 2}")
            return (X, bX), (RO, bRO)
        for t in range(C.NT):
            sl = slice(t * TT, (t + 1) * TT)
            i2 = t % 2
            if t == 0:
                nxt_in = load_in(0)
            (X, bX), (RO, bRO) = nxt_in
            if t + 1 < C.NT:
                nxt_in = load_in(t + 1)
            H, bH = C.rotbuf("H", 1, [128, 8, TT], BF16, nb=8)
            resid_proj(C, "reto", 4, 16, lambda kt: RO[:, kt, :], lambda kt: [bRO], X, bX, (C.G, 40, H, bH),
                       resident=(RW, bRW))
            HO, bHO = C.rotbuf("HO", 2, [128, 8, TT], F32, nb=8)
            emit_ffn(C, "f21", X, bX, H, bH, nxt=(C.G, 48, HO, bHO))
            P.dma("pool", ov[:, :, sl], HO[:, :, :], reads=bHO, slot=f"xout{i2}")
        P.barrier()
    C.pstack = None


def full_pack(inp, meta):
    pk = WPack()
    def ffn(tag, a, b):
        if meta:
            pack_ffn(pk, tag, None, None, True)
        else:
            pack_ffn(pk, tag, inp[a + "_w1"][b], inp[a + "_w2"][b])
    ffn("f10", "ffn1", 0)
    pack_p1(pk, inp, meta)
    pk.first = pk.off
    ffn("f20", "ffn2", 0)
    ffn("f11", "ffn1", 1)
    pack_p3(pk, inp, meta)
    pack_p5(pk, inp, meta)
    ffn("f21", "ffn2", 1)
    return pk


def build_full(L, dbg_outputs=()):
    pk = full_pack(None, True)
    nc = bass.Bass("TRN2", target_bir_lowering=False)

    def dram(name, shape, dt, kind="Internal"):
        if name in dbg_outputs:
            kind = "ExternalOutput"
        return nc.dram_tensor(name, list(shape), dt, kind=kind)
    xT = dram("xT", [D, L], F32, "ExternalInput")
    nrm = dram("nrm", [128, 56], F32, "ExternalInput")
    gn = dram("gn", [128, 20], F32, "ExternalInput")
    wf32 = dram("wf32", [pk.off], F32, "ExternalInput")
    s5p = dram("s5p", [128, S5P_COLS], F32, "ExternalInput")
    gkw = dram("gkw", [32, 512], F32, "ExternalInput")
    gkb = dram("gkb", [128, 4], F32, "ExternalInput")
    outT = dram("outT", [D, L], F32, "ExternalOutput")
    wbf = dram("wbf", [pk.off], BF16)
    X1 = dram("X1", [D, L], F32)
    X4 = dram("X4", [D, L], F32)
    UT = dram("UT", [8, 128, L], BF16)
    GQ = dram("GQ", [2, 128, L], BF16)
    GK = dram("GK", [2, 128, L], BF16)
    GV = dram("GV", [L, 512], BF16)
    GLO = dram("GLO", [32, L], BF16)
    GOG = dram("GOG", [4, 128, L], BF16)
    GY = dram("GY", [4, 128, L], BF16)
    GO = dram("GO", [4, 128, L], BF16)
    RQ = dram("RQ", [8, 128, L], BF16)
    RK = dram("RK", [8, 128, L], BF16)
    RV = dram("RV", [L, 2048], BF16)
    ROG = dram("ROG", [16, 128, L], BF16)
    SBD = dram("SBD", [L // RC, 128, 8, 256], BF16)
    RO = dram("RO", [16, 128, L], BF16)
    WIN_d, WOUT_d, TOEP_d = s5_dram(nc)
    with ExitStack() as st:
        P = Prog(nc, st)
        C = Ctx(nc, P, L, pk)
        setup_common(C, nrm, gn)
        rotary_setup(C)
        castA = cast_gen(C, wf32, wbf, 0, pk.first, "A")
        for _ in range(4):
            next(castA, None)
        C.W = WStream(C, wbf)
        s5_setup(C, s5p, WIN_d, WOUT_d, TOEP_d, bg=castA)
        castB = cast_gen(C, wf32, wbf, pk.first, pk.off, "B")
        phase1(C, xT, X1, UT, GQ, GK, GV, GLO, GOG)
        NCH = L // 16
        per = max(4, -(-NCH // (2 * (L // GC))))
        def both():
            k = 0
            while True:
                if k % per == 0:
                    next(castB, None)
                k += 1
                yield

        def mid(g):
            def merged():
                b = both()
                for _ in g:
                    next(b)
                    yield
            gla_main(C, GQ, GK, GV, GLO, GOG, gkw, gkb, GO, bg=merged(), bg_per_chunk=per)
        s5_main(C, UT, WIN_d, WOUT_d, TOEP_d, GY, mid=mid)
        for _ in castB:
            pass
        P.barrier()
        phase3(C, X1, GY, GO, X4, RQ, RK, RV, ROG)
        ret_main(C, RQ, RK, RV, ROG, SBD, RO)
        phase5(C, X4, RO, outT)
        P.barrier()
        C.ninst = P.ninst
    return nc, pk


def host_inputs(inp):
    pk = full_pack(inp, False)
    wimg = pk.image()
    names = ["ffn1_norm", "mix_norm", "ffn2_norm"]
    nrm = np.zeros((128, 56), np.float32)
    col = 0
    for layer in range(2):
        order = [inp["ffn1_norm"][layer], inp["mix_norm"][layer], inp["ffn2_norm"][layer]]
        if layer == 1:
            pass
        for j, g in enumerate(order):
            pass
    cols = [inp["ffn1_norm"][0], inp["mix_norm"][0], inp["ffn2_norm"][0],
            inp["ffn1_norm"][1], inp["mix_norm"][1], inp["ffn2_norm"][1], inp["final_norm"]]
    for i, g in enumerate(cols):
        nrm[:, 8 * i:8 * i + 8] = np.asarray(g, np.float32).reshape(8, 128).T
    gn = np.zeros((128, 20), np.float32)
    gn[:, 0:4] = inp["gla_norm"][0].reshape(4, 128).T
    gn[:, 4:20] = inp["ret_norm"][0].reshape(16, 128).T
    gkw, gkb = gla_pack_params(inp)
    return dict(nrm=nrm, gn=gn, wf32=wimg, s5p=s5_pack_params(inp), gkw=gkw, gkb=gkb)


_CACHE = {}


def kernel(**inputs):
    inp = {k: np.asarray(v) for k, v in inputs.items()}
    x = inp["x"]
    B, L, _ = x.shape
    shared = host_inputs(inp)
    if L not in _CACHE:
        _CACHE[L] = build_full(L)
    nc, pk = _CACHE[L]
    in_maps = []
    for b in range(B):
        m = dict(shared)
        m["xT"] = np.ascontiguousarray(x[b].T)
        in_maps.append(m)
    res = run_bass_kernel_spmd(nc, in_maps, core_ids=list(range(B)))
    out = np.stack([np.ascontiguousarray(r["outT"].T) for r in res.results], axis=0)
    return out.astype(np.float32)
```

```python
import math
from contextlib import ExitStack
import numpy as np
import concourse.bass as bass
import concourse.mybir as mybir
from concourse.bass_utils import run_bass_kernel_spmd

F32 = mybir.dt.float32
BF16 = mybir.dt.bfloat16
ALU = mybir.AluOpType
AF = mybir.ActivationFunctionType

D = 1024
DFF = 2816
NJT = DFF // 128
EPS = 1e-6
TT = 512


class Buf:
    __slots__ = ("name", "w", "r")

    def __init__(self, name=""):
        self.name = name
        self.w = None
        self.r = {}


class Eng:
    def __init__(self, name, h, sem):
        self.name = name
        self.h = h
        self.sem = sem
        self.cnt = 0
        self.pending = False
        self.seen = {}


class Prog:
    def __init__(self, nc, stack):
        self.nc = nc
        self.stack = stack
        self.sems = {}
        self.dmaval = {}
        self.E = {}
        for name, h in (("pe", nc.tensor), ("act", nc.scalar), ("dve", nc.vector),
                        ("pool", nc.gpsimd), ("sp", nc.sync)):
            sem = stack.enter_context(nc.semaphore("sem_" + name))
            self.sems[name] = sem
            self.E[name] = Eng(name, h, sem)
        self.nuid = 0
        self.ninst = 0

    def sb(self, shape, dtype, name=None, stack=None):
        self.nuid += 1
        return (stack or self.stack).enter_context(
            self.nc.sbuf_tensor(name or f"sb{self.nuid}", list(shape), dtype))

    def slot(self, name):
        if name not in self.sems:
            self.sems[name] = self.stack.enter_context(self.nc.semaphore("dq_" + name))
            self.dmaval[name] = 0
        return name

    def _deps(self, reads, writes):
        deps = {}

        def add(k, v):
            if deps.get(k, 0) < v:
                deps[k] = v
        for b in reads:
            if b.w is not None:
                add(*b.w)
        for b in writes:
            if b.w is not None:
                add(*b.w)
            for k, v in b.r.items():
                add(k, v)
        return deps

    def _wait(self, e, deps, skip_self=False):
        for k, v in deps.items():
            if skip_self and k == e.name:
                continue
            if e.seen.get(k, 0) >= v:
                continue
            e.h.wait_ge(self.sems[k], v)
            e.seen[k] = v

    def op(self, eng, fn, reads=(), writes=(), signal=True):
        e = self.E[eng]
        self._wait(e, self._deps(reads, writes), skip_self=(eng == "pe"))
        ins = fn()
        self.ninst += 1
        if signal:
            e.cnt += 1
            ins.then_inc(e.sem, 1)
            e.pending = False
            val = e.cnt
        else:
            e.pending = True
            val = e.cnt + 1
        for b in writes:
            b.w = (eng, val)
            b.r = {}
        for b in reads:
            if b.r.get(eng, 0) < val:
                b.r[eng] = val
        return ins

    def dma(self, q, out, in_, reads=(), writes=(), slot=None, **kw):
        e = self.E[q]
        deps = self._deps(reads, writes)
        prev = self.dmaval[slot]
        if prev and deps.get(slot, 0) < prev:
            deps[slot] = prev
        self._wait(e, deps)
        ins = e.h.dma_start(out=out, in_=in_, **kw)
        self.ninst += 1
        self.dmaval[slot] += 16
        val = self.dmaval[slot]
        ins.then_inc(self.sems[slot], 16)
        for b in writes:
            b.w = (slot, val)
            b.r = {}
        for b in reads:
            if b.r.get(slot, 0) < val:
                b.r[slot] = val
        return ins

    def barrier(self):
        targets = {}
        for name, e in self.E.items():
            assert not e.pending, f"{name} has unsignalled instructions at barrier"
            if e.cnt:
                targets[name] = e.cnt
        for s, v in self.dmaval.items():
            if v and s not in getattr(self, "bar_exclude", ()):
                targets[s] = v
        for name, e in self.E.items():
            self._wait(e, {k: v for k, v in targets.items() if k != name})


class WPack:
    def __init__(self):
        self.chunks = {}
        self.arrays = []
        self.off = 0

    def add(self, key, arr):
        arr = np.ascontiguousarray(arr, dtype=np.float32)
        assert arr.shape[0] == 128
        n = arr.size // 128
        self.chunks[key] = (self.off, n)
        self.arrays.append(arr.reshape(-1))
        self.off += arr.size

    def add_meta(self, key, n):
        self.chunks[key] = (self.off, n)
        self.off += 128 * n

    def image(self):
        return np.concatenate(self.arrays)


def kt_split(w):
    k, n = w.shape
    return w.reshape(k // 128, 128, n).transpose(1, 0, 2)


def pack_ffn(pk, tag, w1, w2, meta_only=False):
    for f in range(NJT):
        key = (tag, "w1", f)
        if meta_only:
            pk.add_meta(key, 2 * 8 * 128)
            continue
        g = kt_split(w1[:, f * 128:(f + 1) * 128])
        u = kt_split(w1[:, DFF + f * 128:DFF + (f + 1) * 128])
        pk.add(key, np.stack([g, u], axis=1))
    for m in range(8):
        key = (tag, "w2", m)
        if meta_only:
            pk.add_meta(key, NJT * 128)
            continue
        pk.add(key, kt_split(w2[:, m * 128:(m + 1) * 128]))


def pack_fm(pk, tag, w, ncols_chunk, meta_only=False):
    k, n = w.shape
    assert n % ncols_chunk == 0
    for c in range(n // ncols_chunk):
        key = (tag, c)
        if meta_only:
            pk.add_meta(key, (k // 128) * ncols_chunk)
        else:
            pk.add(key, kt_split(w[:, c * ncols_chunk:(c + 1) * ncols_chunk]))


class Shape:
    def __init__(self, *s):
        self.shape = s


def build_pack(inp, meta_only):
    pk = WPack()
    g = (lambda k: inp[k]) if not meta_only else None
    for i, (tag, a, b) in enumerate([("f10", "ffn1", 0), ("f20", "ffn2", 0), ("f11", "ffn1", 1), ("f21", "ffn2", 1)]):
        if meta_only:
            pack_ffn(pk, tag, None, None, True)
        else:
            pack_ffn(pk, tag, g(a + "_w1")[b], g(a + "_w2")[b])
    return pk


class Ctx:
    def __init__(self, nc, P, L, pk):
        self.nc = nc
        self.P = P
        self.L = L
        self.NT = L // TT
        self.pk = pk
        self.ps = []
        self.psb = []
        for i in range(8):
            self.ps.append(P.stack.enter_context(nc.psum_tensor(f"psb{i}", [128, 512], F32)))
            self.psb.append(Buf(f"ps{i}"))
        self.psi = 0
        self.rot = {}

    def psum(self):
        i = self.psi
        self.psi = (self.psi + 1) % 7
        return self.ps[i], self.psb[i]

    def psum_stat(self):
        return self.ps[7], self.psb[7]

    def psum_fixed(self, i):
        return self.ps[i], self.psb[i]

    def psum_pool(self, name, banks):
        if not hasattr(self, "_pools"):
            self._pools = {}
        st = self._pools.setdefault(name, [0])
        i = banks[st[0] % len(banks)]
        st[0] += 1
        return self.ps[i], self.psb[i]

    def rotbuf(self, key, n, shape, dtype, nb=None):
        if key not in self.rot:
            self.nrot = getattr(self, "nrot", 0) + 1
            stk = getattr(self, "pstack", None)
            mk = (lambda i: Buf(f"{key}{i}")) if nb is None else (lambda i: [Buf(f"{key}{i}_{j}") for j in range(nb)])
            self.rot[key] = [[(self.P.sb(shape, dtype, name=f"{key}{i}_{self.nrot}", stack=stk), mk(i))
                              for i in range(n)], 0]
        lst, i = self.rot[key]
        self.rot[key][1] = (i + 1) % n
        return lst[i]


class WStream:
    def __init__(self, C, wbf, nslots=3, slot_elems=4096, q="sp", name="w"):
        P = C.P
        self.C = C
        self.wbf = wbf
        self.q = q
        self.ns = nslots
        self.tiles = [P.sb([128, slot_elems], BF16, name=f"{name}slot{i}") for i in range(nslots)]
        self.bufs = [Buf(f"{name}slot{i}") for i in range(nslots)]
        self.slots = [P.slot(f"{name}{i}") for i in range(nslots)]
        self.queue = []
        self.issued = 0
        self.consumed = 0

    def schedule(self, keys):
        self.queue.extend(keys)

    def _issue(self):
        P = self.C.P
        while self.issued < min(self.consumed + self.ns, len(self.queue)):
            i = self.issued
            off, n = self.C.pk.chunks[self.queue[i]]
            s = i % self.ns
            src = bass.AP(self.wbf, off, [[n, 128], [1, n]])
            P.dma(self.q, self.tiles[s][:, 0:n], src, writes=[self.bufs[s]], slot=self.slots[s])
            self.issued += 1

    def get(self, key):
        assert self.queue[self.consumed] == key, (self.queue[self.consumed], key)
        bg = getattr(self, "bg", None)
        if bg is not None and self.consumed % self.bg_every == 0:
            if next(bg, "done") == "done":
                self.bg = None
        self._issue()
        i = self.consumed
        self.consumed += 1
        s = i % self.ns
        return self.tiles[s], self.bufs[s]


def mm(C, out, lhsT, rhs, start, stop, reads, writes, signal=None):
    if signal is None:
        signal = stop
    return C.P.op("pe", lambda: C.nc.tensor.matmul(out, lhsT, rhs, start=start, stop=stop),
                  reads=reads, writes=writes, signal=signal)


class RmsState:
    pass


def rms_begin(C):
    st = RmsState()
    st.SQ, st.bSQ = C.rotbuf("rms_sq", 1, [128, 8, TT], BF16, nb=8)
    st.R, st.bR = C.rotbuf("rms_r", 2, [128, TT], F32)
    st.ps, st.bps = C.psum_stat()
    return st


def rms_square(C, st, X, bX, m):
    C.P.op("act", lambda: C.nc.scalar.activation(st.SQ[:, m, :], X[:, m, :], AF.Square), reads=[bX[m]], writes=[st.bSQ[m]])


def rms_stat(C, st, m):
    mm(C, st.ps[:, :], C.ones[:, :], st.SQ[:, m, :], m == 0, m == 7, [st.bSQ[m], C.b_ones], [st.bps])


def rms_finish(C, st, X, bX, G, gcol, H, bH, perm=None):
    P, nc = C.P, C.nc
    R, bR = st.R, st.bR
    P.op("act", lambda: nc.scalar.activation(R[:, :], st.ps[:, :], AF.Ln, bias=EPS, scale=1.0 / D),
         reads=[st.bps], writes=[bR])
    P.op("act", lambda: nc.scalar.activation(R[:, :], R[:, :], AF.Exp, scale=-0.5), reads=[bR], writes=[bR])
    for kt in range(8):
        P.op("dve", lambda kt=kt: nc.vector.scalar_tensor_tensor(
            H[:, kt, :], X[:, kt, :], G[:, gcol + kt:gcol + kt + 1], R[:, :], ALU.mult, ALU.mult),
            reads=[bX[kt], bR, C.bG], writes=[bH[kt]])
        if perm is not None:
            Hp, bHp = perm
            P.op("act", lambda kt=kt: nc.scalar.copy(
                Hp[:, kt, :].rearrange("p (s c) -> p c s", c=32), H[:, kt, :].rearrange("p (c s) -> p c s", s=16)),
                reads=[bH[kt]], writes=[bHp[kt]])


def emit_rmsnorm(C, X, bX, G, gcol, H, bH):
    st = rms_begin(C)
    for m in range(8):
        rms_square(C, st, X, bX, m)
        rms_stat(C, st, m)
    rms_finish(C, st, X, bX, G, gcol, H, bH)


def ffn_keys(tag):
    return [(tag, "w1", f) for f in range(NJT)] + [(tag, "w2", m) for m in range(8)]


def emit_ffn(C, tag, X, bX, H, bH, nxt=None):
    P, nc = C.P, C.nc
    A, bA = C.rotbuf("ffn_a", 1, [128, NJT, TT], BF16, nb=NJT)
    for f in range(NJT):
        wt, bw = C.W.get((tag, "w1", f))
        wv = wt[:, 0:2048].rearrange("p (g k c) -> p g k c", g=2, k=8)
        pg, bpg = C.psum()
        pu, bpu = C.psum()
        for kt in range(8):
            mm(C, pg[:, :], wv[:, 0, kt, :], H[:, kt, :], kt == 0, kt == 7, [bw, bH[kt]], [bpg])
        for kt in range(8):
            mm(C, pu[:, :], wv[:, 1, kt, :], H[:, kt, :], kt == 0, kt == 7, [bw, bH[kt]], [bpu])
        S, bS = C.rotbuf("ffn_s", 2, [128, TT], F32)
        P.op("act", lambda: nc.scalar.activation(S[:, :], pg[:, :], AF.Silu), reads=[bpg], writes=[bS])
        P.op("dve", lambda: nc.vector.tensor_tensor(A[:, f, :], S[:, :], pu[:, :], ALU.mult),
             reads=[bS, bpu], writes=[bA[f]])
    st = rms_begin(C) if nxt is not None else None
    for m in range(8):
        wt, bw = C.W.get((tag, "w2", m))
        wv = wt[:, 0:NJT * 128].rearrange("p (j c) -> p j c", j=NJT)
        py, bpy = C.psum()
        for j in range(NJT):
            mm(C, py[:, :], wv[:, j, :], A[:, j, :], j == 0, j == NJT - 1, [bw, bA[j]], [bpy])
        if st is not None and m >= 1:
            rms_stat(C, st, m - 1)
        P.op("dve", lambda: nc.vector.scalar_tensor_tensor(X[:, m, :], py[:, :], 0.5, X[:, m, :], ALU.mult, ALU.add),
             reads=[bpy, bX[m]], writes=[bX[m]])
        if st is not None:
            rms_square(C, st, X, bX, m)
    if st is not None:
        rms_stat(C, st, 7)
        G, gcol, Hn, bHn = nxt[:4]
        rms_finish(C, st, X, bX, G, gcol, Hn, bHn, perm=(nxt[4] if len(nxt) > 4 else None))


def cast_gen(C, wf32, wbf, lo, hi, tag):
    P = C.P
    CH = 128 * 4096
    sl = [P.slot(f"cast{tag}{i}") for i in range(4)]
    off = lo
    i = 0
    while off < hi:
        n = min(CH, hi - off)
        assert n % 128 == 0
        src = bass.AP(wf32, off, [[n // 128, 128], [1, n // 128]])
        dst = bass.AP(wbf, off, [[n // 128, 128], [1, n // 128]])
        P.dma("pool", dst, src, slot=sl[i % 4])
        off += n
        i += 1
        yield


def emit_cast_weights(C, wf32, wbf, lo, hi, tag):
    for _ in cast_gen(C, wf32, wbf, lo, hi, tag):
        pass


def setup_common(C, nrm_d, gn_d=None):
    P, nc = C.P, C.nc
    C.ones = P.sb([128, 128], BF16, name="ones")
    C.b_ones = Buf("ones")
    P.op("pool", lambda: nc.gpsimd.memset(C.ones[:, :], 1.0), writes=[C.b_ones])
    C.G = P.sb([128, 56], F32, name="gains")
    C.bG = Buf("gains")
    P.slot("misc")
    P.dma("sp", C.G[:, :], nrm_d[:, :], writes=[C.bG], slot="misc")
    C.pstack = None
    if gn_d is not None:
        C.GNT = P.sb([128, 20], F32, name="gnt")
        C.bGN = Buf("gn")
        P.dma("sp", C.GNT[:, :], gn_d[:, :], writes=[C.bGN], slot=P.slot("misc3"))
        C.GN = C.GNT[:, 0:4]
        C.RN = C.GNT[:, 4:20]


def tt(C, eng, out, a, b, op, reads, writes):
    h = C.P.E[eng].h
    return C.P.op(eng, lambda: h.tensor_tensor(out, a, b, op), reads=reads, writes=writes)


def ts(C, eng, out, a, s1, s2, op0, op1, reads, writes):
    h = C.P.E[eng].h
    if s2 is None:
        return C.P.op(eng, lambda: h.tensor_scalar(out, a, s1, None, op0), reads=reads, writes=writes)
    return C.P.op(eng, lambda: h.tensor_scalar(out, a, s1, s2, op0, op1), reads=reads, writes=writes)


def cp(C, eng, out, a, reads, writes):
    if eng == "act":
        return C.P.op("act", lambda: C.nc.scalar.copy(out, a), reads=reads, writes=writes)
    h = C.P.E[eng].h
    return C.P.op(eng, lambda: h.tensor_copy(out, a), reads=reads, writes=writes)


def actf(C, out, a, func, reads, writes, bias=0.0, scale=1.0):
    return C.P.op("act", lambda: C.nc.scalar.activation(out, a, func, bias=bias, scale=scale),
                  reads=reads, writes=writes)


def mset(C, eng, ap, val, writes):
    h = C.P.E[eng].h
    return C.P.op(eng, lambda: h.memset(ap, val), writes=writes)


def emit_sin(C, out, x, shift, tmp, tmpi, reads, writes, eng="dve"):
    P = C.P
    h = P.E[eng].h
    PI = math.pi
    ts(C, eng, tmp, x, shift, 1.0 / (2 * PI), ALU.add, ALU.mult, reads, writes)
    cp(C, eng, tmpi, tmp, reads, writes)
    cp(C, eng, tmp, tmpi, reads, writes)
    C1 = 6.28125
    C2 = 2 * PI - C1
    P.op(eng, lambda: h.scalar_tensor_tensor(out, tmp, -C1, x, ALU.mult, ALU.add), reads=reads, writes=writes)
    P.op(eng, lambda: h.scalar_tensor_tensor(tmp, tmp, -C2, out, ALU.mult, ALU.add), reads=reads, writes=writes)
    if shift != 0.0:
        ts(C, eng, tmp, tmp, shift, None, ALU.add, None, reads, writes)
    ts(C, eng, out, tmp, PI, 2 * PI, ALU.is_gt, ALU.mult, reads, writes)
    tt(C, eng, tmp, tmp, out, ALU.subtract, reads, writes)
    ts(C, eng, out, tmp, -PI, 2 * PI, ALU.is_lt, ALU.mult, reads, writes)
    tt(C, eng, tmp, tmp, out, ALU.add, reads, writes)
    ts(C, eng, tmp, tmp, PI, -PI, ALU.min, ALU.max, reads, writes)
    actf(C, out, tmp, AF.Sin, reads, writes)


S5P_COLS = 64 * 3 + 1024 * 4 + 8
T1 = 16


def s5_pack_params(inp):
    def nmaj(a):
        return np.transpose(a, (2, 0, 1)).reshape(64, 64)
    lamr = nmaj(inp["s5_lambda_re"][0])
    lami = nmaj(inp["s5_lambda_im"][0])
    ldt = np.broadcast_to(inp["s5_log_dt"][0].reshape(1, 64), (64, 64))
    br = np.transpose(inp["s5_b_re"][0], (2, 0, 1, 3)).reshape(64, 1024)
    bi = np.transpose(inp["s5_b_im"][0], (2, 0, 1, 3)).reshape(64, 1024)
    cr = np.transpose(inp["s5_c_re"][0], (3, 0, 1, 2)).reshape(64, 1024)
    ci = np.transpose(inp["s5_c_im"][0], (3, 0, 1, 2)).reshape(64, 1024)
    top = np.concatenate([lamr, lami, ldt, br, bi, cr, ci], axis=1)
    top = np.concatenate([top, top], axis=0)
    dp = np.zeros((128, 8), np.float32)
    dsk = inp["s5_d"][0].reshape(8, 4, 16)
    for gl in range(4):
        dp[32 * gl:32 * gl + 16, :] = dsk[:, gl, :].T
    return np.ascontiguousarray(np.concatenate([top, dp], axis=1), dtype=np.float32)


def s5_setup(C, s5p_d, WIN_d, WOUT_d, TOEP_d, bg=None):
    P, nc = C.P, C.nc
    C.s5AR = P.sb([128, 16, 2, 2], F32, name="s5AR")
    C.s5AI = P.sb([128, 16, 2, 2], F32, name="s5AI")
    C.bs5A = Buf("s5A")
    with ExitStack() as st:
        def sb(shape, dt=F32):
            return P.sb(shape, dt, stack=st), Buf()
        PR, bPR = sb([128, S5P_COLS])
        P.dma("sp", PR[:, :], s5p_d[:, :], writes=[bPR], slot="misc")
        lamr = PR[:, 0:64]; lami = PR[:, 64:128]; ldt = PR[:, 128:192]
        Br = PR[:, 192:1216].rearrange("p (g c) -> p g c", c=16)
        Bi = PR[:, 1216:2240].rearrange("p (g c) -> p g c", c=16)
        Cr = PR[:, 2240:3264].rearrange("p (g c) -> p g c", c=16)
        Ci = PR[:, 3264:4288].rearrange("p (g c) -> p g c", c=16)
        dpad = PR[:, 4288:4296]
        T, bT = sb([128, 12, 64])
        lr = T[:, 0, :]; dt = T[:, 1, :]; mag = T[:, 2, :]; th = T[:, 3, :]
        ar = T[:, 4, :]; ai = T[:, 5, :]; den = T[:, 6, :]; crr = T[:, 7, :]; cii = T[:, 8, :]
        t0 = T[:, 9, :]; t1 = T[:, 10, :]; t2 = T[:, 11, :]
        R, W = [bPR, bT], [bT]
        ts(C, "dve", lr, lamr, -1e-4, None, ALU.min, None, R, W)
        actf(C, dt, ldt, AF.Exp, R, W)
        tt(C, "dve", t0, lr, dt, ALU.mult, R, W)
        actf(C, mag, t0, AF.Exp, R, W)
        tt(C, "dve", th, lami, dt, ALU.mult, R, W)
        TI, bTI = sb([128, 64], mybir.dt.int32)
        emit_sin(C, t1, th, 0.0, t0, TI[:, :], R + [bTI], W + [bTI])
        emit_sin(C, t2, th, math.pi / 2, t0, TI[:, :], R + [bTI], W + [bTI])
        tt(C, "dve", ar, mag, t2, ALU.mult, R, W)
        tt(C, "dve", ai, mag, t1, ALU.mult, R, W)
        tt(C, "dve", den, lr, lr, ALU.mult, R, W)
        tt(C, "dve", t0, lami, lami, ALU.mult, R, W)
        tt(C, "dve", den, den, t0, ALU.add, R, W)
        P.op("dve", lambda: nc.vector.reciprocal(den, den), reads=R, writes=W)
        ts(C, "dve", t0, ar, -1.0, None, ALU.add, None, R, W)
        tt(C, "dve", t1, t0, lr, ALU.mult, R, W)
        tt(C, "dve", t2, ai, lami, ALU.mult, R, W)
        tt(C, "dve", t1, t1, t2, ALU.add, R, W)
        tt(C, "dve", crr, t1, den, ALU.mult, R, W)
        tt(C, "dve", t1, ai, lr, ALU.mult, R, W)
        tt(C, "dve", t2, t0, lami, ALU.mult, R, W)
        tt(C, "dve", t1, t1, t2, ALU.subtract, R, W)
        tt(C, "dve", cii, t1, den, ALU.mult, R, W)
        BB, bBB = sb([128, 2, 64, 16])
        TB, bTB = sb([128, 2, 64, 16])

        def bc(x):
            return x.unsqueeze(2).broadcast_to([128, 64, 16])
        R2 = [bPR, bT, bBB, bTB]
        tt(C, "dve", BB[:, 0], Br, bc(crr), ALU.mult, R2, [bBB])
        tt(C, "dve", TB[:, 0], Bi, bc(cii), ALU.mult, R2, [bTB])
        tt(C, "dve", BB[:, 0], BB[:, 0], TB[:, 0], ALU.subtract, R2, [bBB])
        tt(C, "dve", BB[:, 1], Bi, bc(crr), ALU.mult, R2, [bBB])
        tt(C, "dve", TB[:, 1], Br, bc(cii), ALU.mult, R2, [bTB])
        tt(C, "dve", BB[:, 1], BB[:, 1], TB[:, 1], ALU.add, R2, [bBB])
        PW, bPW = sb([128, 2, 17, 64])
        mset(C, "dve", PW[:, 0, 0, :], 1.0, [bPW])
        mset(C, "dve", PW[:, 1, 0, :], 0.0, [bPW])
        R3 = [bPW, bT]
        for e in range(16):
            pr, pi = PW[:, 0, e, :], PW[:, 1, e, :]
            tt(C, "dve", t0, pr, ar, ALU.mult, R3, [bT])
            tt(C, "dve", t1, pi, ai, ALU.mult, R3, [bT])
            tt(C, "dve", PW[:, 0, e + 1, :], t0, t1, ALU.subtract, R3, [bPW])
            tt(C, "dve", t0, pr, ai, ALU.mult, R3, [bT])
            tt(C, "dve", t1, pi, ar, ALU.mult, R3, [bT])
            tt(C, "dve", PW[:, 1, e + 1, :], t0, t1, ALU.add, R3, [bPW])
        for hh in range(2):
            rows = slice(64 * hh, 64 * hh + 64)
            for d in range(2):
                srcr = PW[rows, 0, 16, d * 32:(d + 1) * 32].rearrange("p (m h) -> p m h", h=2)[:, :, hh]
                srci = PW[rows, 1, 16, d * 32:(d + 1) * 32].rearrange("p (m h) -> p m h", h=2)[:, :, hh]
                for ri in range(2):
                    cp(C, "dve", C.s5AR[rows, :, d, ri], srcr, [bPW], [C.bs5A])
                ts(C, "dve", C.s5AI[rows, :, d, 0], srci, -1.0, None, ALU.mult, None, [bPW], [C.bs5A])
                cp(C, "dve", C.s5AI[rows, :, d, 1], srci, [bPW], [C.bs5A])
        ID, bID = sb([128, 128])
        mset(C, "pool", ID[:, :], 1.0, [bID])
        P.op("pool", lambda: nc.gpsimd.affine_select(
            out=ID[:, :], in_=ID[:, :], pattern=[[-1, 128]], compare_op=ALU.is_equal, fill=0.0,
            base=0, channel_multiplier=1), reads=[bID], writes=[bID])
        IDb, bIDb = sb([128, 128], BF16)
        cp(C, "dve", IDb[:, :], ID[:, :], [bID], [bIDb])
        for s_ in range(4):
            P.slot(f"s5st{s_}")
        X1, bX1 = sb([128, 64, 16])
        X2, bX2 = sb([128, 64, 16])
        BPm, bBPm = sb([128, 2, 64, 128], BF16)
        mset(C, "pool", BPm[:, :, :, :], 0.0, [bBPm])
        for ri in range(2):
            for gl in range(4):
                dst = BPm[:, ri, :, 32 * gl:32 * gl + 16].rearrange("p (a b) c -> p a b c", b=4)[:, :, gl, :]
                src = BB[:, ri].rearrange("p (a b) c -> p a b c", b=4)[:, :, gl, :]
                cp(C, "dve", dst, src, [bBB], [bBPm])
        TST = [sb([128, 4, 128], BF16) for _ in range(2)]
        for t_, b_ in TST:
            mset(C, "pool", t_[:, :, :], 0.0, [b_])
        WO = [sb([128, 16, 2, 128], BF16) for _ in range(2)]
        for t_, b_ in WO:
            mset(C, "pool", t_[:, :, :, :], 0.0, [b_])
        nst = 0
        kwo = 0
        CAe = [sb([128, 2, 64, 16], BF16) for _ in range(2)]
        X3, bX3 = sb([128, 64, 16])
        X4, bX4 = sb([128, 64, 16])
        def tick():
            if bg is not None:
                next(bg, None)
        for e in range(17):
            tick()
            ca, bca = CAe[e % 2]
            pr, pi = bc(PW[:, 0, e, :]), bc(PW[:, 1, e, :])
            tt(C, "dve", X1[:, :, :], Cr, pr, ALU.mult, [bPR, bPW, bX1], [bX1])
            tt(C, "dve", X2[:, :, :], Ci, pi, ALU.mult, [bPR, bPW, bX2], [bX2])
            tt(C, "dve", ca[:, 0], X1[:, :, :], X2[:, :, :], ALU.subtract, [bX1, bX2], [bca])
            tt(C, "pool", X3[:, :, :], Cr, pi, ALU.mult, [bPR, bPW, bX3], [bX3])
            tt(C, "pool", X4[:, :, :], Ci, pr, ALU.mult, [bPR, bPW, bX4], [bX4])
            P.op("dve", lambda: nc.vector.scalar_tensor_tensor(ca[:, 1], X3[:, :, :], -1.0, X4[:, :, :],
                                                              ALU.mult, ALU.subtract), reads=[bX3, bX4], writes=[bca])
            if e < 16:
                for d in range(2):
                    for half in range(2):
                        ps, bps = C.psum()
                        for bq in range(4):
                            blk = half * 4 + bq
                            for gl in range(4):
                                g = d * 32 + blk * 4 + gl
                                for ri in range(2):
                                    mm(C, ps[:, bq * 128 + gl * 32: bq * 128 + gl * 32 + 16],
                                       BPm[0:64, ri, g, :], ca[0:64, ri, g, :], ri == 0, ri == 1, [bBPm, bca], [bps])
                        stg, bstg = TST[nst % 2]
                        nst += 1
                        pv = ps[:, :].rearrange("p (a b c) -> p a b c", a=4, b=4)[:, :, :, 0:16]
                        sv = stg[:, :, :].rearrange("p a (b c) -> p a b c", b=4)[:, :, :, 0:16]
                        cp(C, "act", sv, pv, [bps], [bstg])
                        if d == 0 and e == 0:
                            for bq in range(4):
                                blk = half * 4 + bq
                                P.op("dve", lambda: nc.vector.scalar_tensor_tensor(
                                    stg[:, bq, :], ID[:, :], dpad[:, blk:blk + 1], stg[:, bq, :],
                                    ALU.mult, ALU.add), reads=[bID, bPR, bstg], writes=[bstg])
                        dst = TOEP_d[half * 4:half * 4 + 4, :, d, e, :].rearrange("b p c -> p b c")
                        P.dma("sp", dst, stg[:, :, :], reads=[bstg], slot=f"s5st{nst % 4}")
            if e >= 1:
                for d in range(2):
                    s = (e - 1) if d == 0 else (16 - e)
                    wo, bwo = WO[kwo % 2]
                    kwo += 1
                    for hh in range(2):
                        rows = slice(64 * hh, 64 * hh + 64)
                        for ri in range(2):
                            src = ca[rows, ri, d * 32:(d + 1) * 32, :]
                            src = src.rearrange("p (q mp h) c -> p q mp h c", mp=2, h=2)
                            for mp in range(2):
                                co = 64 * mp + 32 * hh
                                dstv = wo[rows, :, ri, co:co + 16].rearrange("p (q mp) c -> p q mp c", mp=2)[:, :, mp, :]
                                cp(C, "dve" if ri == 0 else "pool", dstv, src[:, :, mp, hh, :], [bca], [bwo])
                    for r_ in range(2):
                        dst = WOUT_d[:, :, d, s, r_, :].rearrange("m p c -> p m c")
                        P.dma("sp", dst, wo[:, :, r_, :], reads=[bwo], slot=f"s5st{(2 * kwo + r_) % 4}")
        WP, bWP = sb([128, 2, 64, 32], BF16)
        mset(C, "pool", WP[:, :, :, :], 0.0, [bWP])
        WST = [sb([128, 8, 2, 128], BF16) for _ in range(2)]
        for t_, b_ in WST:
            mset(C, "pool", t_[:, :, :, :], 0.0, [b_])
        nw = 0
        for e in range(16):
            tick()
            pr, pi = bc(PW[:, 0, e, :]), bc(PW[:, 1, e, :])
            tt(C, "dve", X1[:, :, :], BB[:, 0], pr, ALU.mult, [bBB, bPW, bX1], [bX1])
            tt(C, "dve", X2[:, :, :], BB[:, 1], pi, ALU.mult, [bBB, bPW, bX2], [bX2])
            tt(C, "dve", WP[:, 0, :, 0:16], X1[:, :, :], X2[:, :, :], ALU.subtract, [bX1, bX2], [bWP])
            tt(C, "pool", X3[:, :, :], BB[:, 1], pr, ALU.mult, [bBB, bPW, bX3], [bX3])
            tt(C, "dve", X4[:, :, :], BB[:, 0], pi, ALU.mult, [bBB, bPW, bX4], [bX4])
            tt(C, "dve", WP[:, 1, :, 0:16], X3[:, :, :], X4[:, :, :], ALU.add, [bX3, bX4], [bWP])
            for d in range(2):
                s = (15 - e) if d == 0 else e
                for ri in range(2):
                    ps, bps = C.psum()
                    for blk in range(8):
                        g0 = d * 32 + blk * 4
                        lhsT = WP[0:64, ri, g0:g0 + 4, :].rearrange("p a b -> p (a b)")
                        mm(C, ps[:, blk * 64:(blk + 1) * 64], lhsT, IDb[0:64, 0:64], True, True, [bWP, bIDb], [bps])
                    stg, bstg = WST[nw % 2]
                    nw += 1
                    pv = ps[:, :].rearrange("p (b n) -> p b n", n=64)
                    for gl in range(4):
                        r = slice(32 * gl, 32 * gl + 32)
                        cp(C, "act" if gl % 2 == 0 else "dve",
                           stg[r, :, gl // 2, (gl % 2) * 64:(gl % 2) * 64 + 64], pv[r, :, :], [bps], [bstg])
                    for a_ in range(2):
                        dst = WIN_d[:, :, a_, d, s, ri, :].rearrange("b p c -> p b c")
                        P.dma("sp", dst, stg[:, :, a_, :], reads=[bstg], slot=f"s5st{(2 * nw + a_) % 4}")
        if bg is not None:
            for _ in bg:
                pass
        P.barrier()


def s5_main(C, UT_d, WIN_d, WOUT_d, TOEP_d, GY_d, mid=None):
    P, nc = C.P, C.nc
    L = C.L
    NB = L // 512
    NCH = L // 16
    for i in range(2):
        P.slot(f"s5u{i}"); P.slot(f"s5w{i}"); P.slot(f"s5o{i}"); P.slot(f"s5tp{i}"); P.slot(f"s5wo{i}")
    with ExitStack() as st0:
        XC = P.sb([128, 16, 2, 2, NCH], BF16, stack=st0)
        bXC = Buf("XC")
        bXS = Buf("XS")
        with ExitStack() as st:
            U = [(P.sb([128, L], BF16, stack=st), Buf()) for _ in range(2)]
            Wn = [(P.sb([128, 2, 2, 16, 2, 128], BF16, stack=st), Buf()) for _ in range(1)]
            nev = 0
            for blk in range(8):
                u, bu = U[blk % 2]
                w, bw = Wn[0]
                P.dma("sp", u[:, :], UT_d[blk, :, :], writes=[bu], slot=f"s5u{blk % 2}")
                P.dma("sp", w[:, :, :, :, :, :], WIN_d[blk], writes=[bw], slot="s5w0")
                uv = u[:, :].rearrange("p (b s c) -> p b s c", s=16, c=32)
                for pair in range(2):
                    for d in range(2):
                        for ri in range(2):
                            ps, bps = C.psum()
                            ov = ps[:, 0:NCH].rearrange("p (b c) -> p b c", c=32)
                            for s in range(16):
                                mm(C, ov, w[:, pair, d, s, ri, :], uv[:, :, s, :], s == 0, s == 15, [bu, bw], [bps])
                            cp(C, "act" if nev % 2 == 0 else "dve", XC[:, 2 * blk + pair, d, ri, :], ps[:, 0:NCH],
                               [bps], [bXC])
                            nev += 1
            P.barrier()
        S = [(P.sb([128, 16, 2, 2], F32, stack=st0), Buf()) for _ in range(3)]
        TA = [(P.sb([128, 16, 2, 2], F32, stack=st0), Buf()) for _ in range(2)]
        TB = [(P.sb([128, 16, 2, 2], F32, stack=st0), Buf()) for _ in range(2)]

        def scan_gen():
            mset(C, "pool", S[0][0][:, :, :, :], 0.0, [S[0][1]])
            tot = 16 * 2 * 2 * NCH
            for i in range(NCH):
                cur, bcur = S[i % 3]
                nxt, bnxt = S[(i + 1) % 3]
                ta, bta = TA[i % 2]
                tb, btb = TB[i % 2]
                sw = bass.AP(cur, 1, [[64, 128], [2, 32], [-1, 2]])
                cf, cb = i, NCH - 1 - i
                xsel = bass.AP(XC, cf, [[tot, 128], [4 * NCH, 16], [2 * NCH + (cb - cf), 2], [NCH, 2]])
                tt(C, "pool", ta[:, :, :, :], cur[:, :, :, :], C.s5AR[:, :, :, :], ALU.mult, [bcur, C.bs5A], [bta])
                tt(C, "pool", tb[:, :, :, :].rearrange("p a b c -> p (a b) c"), sw,
                   C.s5AI[:, :, :, :].rearrange("p a b c -> p (a b) c"), ALU.mult, [bcur, C.bs5A], [btb])
                tt(C, "pool", ta[:, :, :, :], ta[:, :, :, :], tb[:, :, :, :], ALU.add, [bta, btb], [bta])
                tt(C, "pool", nxt[:, :, :, :], ta[:, :, :, :], xsel, ALU.add, [bta, bXC], [bnxt])
                cp(C, "pool", xsel, cur[:, :, :, :], [bcur, bnxt], [bXS])
                yield

        g = scan_gen()
        if mid is not None:
            mid(g)
        for _ in g:
            pass
        P.barrier()
        with ExitStack() as st:
            U = [(P.sb([128, L], BF16, stack=st), Buf()) for _ in range(2)]
            TP = [(P.sb([128, 2, 16, 128], BF16, stack=st), Buf()) for _ in range(1)]
            WOt = [(P.sb([128, 2, 16, 2, 128], BF16, stack=st), Buf()) for _ in range(2)]
            YI = (P.sb([128, L], F32, stack=st), Buf())
            YS = [(P.sb([128, 512], F32, stack=st), Buf()) for _ in range(2)]
            GS = [(P.sb([128, 512], BF16, stack=st), Buf()) for _ in range(2)]
            nev = 0
            for blk in range(8):
                u, bu = U[blk % 2]
                tp, btp = TP[0]
                yi, byi = YI
                P.dma("sp", u[:, :], UT_d[blk, :, :], writes=[bu], slot=f"s5u{blk % 2}")
                P.dma("sp", tp[:, :, :, :], TOEP_d[blk], writes=[btp], slot="s5tp0")
                for h in range(2):
                    P.dma("sp", WOt[h][0][:, :, :, :, :], WOUT_d[2 * blk + h], writes=[WOt[h][1]], slot=f"s5wo{h}")
                yv = yi[:, :].rearrange("p (b s c) -> p b s c", s=16, c=32)
                for s in range(16):
                    ps, bps = C.psum()
                    k = 0
                    for h in range(2):
                        for d in range(2):
                            for ri in range(2):
                                mm(C, ps[:, 0:NCH], WOt[h][0][:, d, s, ri, :], XC[:, 2 * blk + h, d, ri, :],
                                   k == 0, k == 7, [WOt[h][1], bXS], [bps])
                                k += 1
                    cp(C, "act" if s % 2 == 0 else "dve", yv[:, :, s, :],
                       ps[:, 0:NCH].rearrange("p (b c) -> p b c", c=32), [bps], [byi])
                for bank in range(NB):
                    ps, bps = C.psum()
                    o = bank * 512
                    for tau in range(16):
                        n = (16 - tau) * 32
                        mm(C, ps[:, tau * 32:512], tp[:, 0, tau, :], u[:, o:o + n], tau == 0, False, [btp, bu], [bps],
                           signal=False)
                    for tau in range(16):
                        n = (16 - tau) * 32
                        mm(C, ps[:, 0:n], tp[:, 1, tau, :], u[:, o + tau * 32:o + 512], False, tau == 15, [btp, bu], [bps])
                    ys, bys = YS[nev % 2]
                    gs, bgs = GS[nev % 2]
                    tt(C, "dve", ys[:, :], ps[:, :], yi[:, o:o + 512], ALU.add, [bps, byi], [bys])
                    actf(C, gs[:, :].rearrange("p (c s) -> p s c", s=16), ys[:, :].rearrange("p (s c) -> p s c", c=32),
                         AF.Gelu, [bys], [bgs])
                    for gl in range(4):
                        r0 = (blk % 2) * 64 + gl * 16
                        P.dma("pool", GY_d[blk // 2, r0:r0 + 16, o:o + 512], gs[32 * gl:32 * gl + 16, :], reads=[bgs],
                              slot=P.slot(f"s5st{gl}" if nev % 2 == 0 else f"castA{gl}"))
                    nev += 1
            P.barrier()
    P.barrier()


def s5_dram(nc, kind="Internal"):
    WIN_d = nc.dram_tensor("s5WIN", [8, 128, 2, 2, 16, 2, 128], BF16, kind=kind)
    WOUT_d = nc.dram_tensor("s5WOUT", [16, 128, 2, 16, 2, 128], BF16, kind=kind)
    TOEP_d = nc.dram_tensor("s5TOEP", [8, 128, 2, 16, 128], BF16, kind=kind)
    return WIN_d, WOUT_d, TOEP_d


def build_test_s5(L):
    pk = build_pack(None, True)
    nc = bass.Bass("TRN2", target_bir_lowering=False)
    s5p = nc.dram_tensor("s5p", [128, S5P_COLS], F32, kind="ExternalInput")
    UT = nc.dram_tensor("UT", [8, 128, L], BF16, kind="ExternalInput")
    GY = nc.dram_tensor("GY", [8, 128, L], BF16, kind="ExternalOutput")
    WIN_d, WOUT_d, TOEP_d = s5_dram(nc, "ExternalOutput")
    with ExitStack() as st:
        P = Prog(nc, st)
        C = Ctx(nc, P, L, pk)
        P.slot("misc")
        s5_setup(C, s5p, WIN_d, WOUT_d, TOEP_d)
        s5_main(C, UT, WIN_d, WOUT_d, TOEP_d, GY)
        print("instructions:", P.ninst)
    return nc


GC = 128


def gla_pack_params(inp):
    wg = np.zeros((32, 2, 256), np.float32)
    for s in range(2):
        wg[16 * s:16 * s + 16, s, :] = inp["gla_w_gk"][0, s]
    bg = np.ascontiguousarray(inp["gla_b_gk"][0].reshape(2, 2, 128).transpose(2, 0, 1), dtype=np.float32)
    return wg.reshape(32, 512), bg.reshape(128, 4)


def gla_main(C, GQ_d, GK_d, GV_d, GLO_d, GOG_d, gkw_d, gkb_d, GO_d, bg=None, bg_per_chunk=4):
    P, nc = C.P, C.nc
    L = C.L
    NT = L // TT
    NCg = L // GC
    CPT = TT // GC
    for i in range(2):
        for nm in ("gq", "gk", "gv", "gl", "gg", "go"):
            P.slot(f"{nm}{i}")
    with ExitStack() as st:
        def sb(shape, dt=F32, name=None):
            return P.sb(shape, dt, stack=st), Buf(name or "")
        WG32, bWG32 = sb([32, 512])
        WG, bWG = sb([32, 2, 2, 128], BF16)
        BG, bBG = sb([128, 4])
        P.dma("sp", WG32[:, :], gkw_d[:, :], writes=[bWG32], slot=P.slot("misc1"))
        P.dma("sp", BG[:, :], gkb_d[:, :], writes=[bBG], slot=P.slot("misc2"))
        cp(C, "dve", WG[:, :, :, :].rearrange("p a b c -> p (a b c)"), WG32[:, :], [bWG32], [bWG])
        ts(C, "dve", BG[:, :], BG[:, :], -1.0, None, ALU.mult, None, [bBG], [bBG])
        MK, bMK = sb([128, 256])
        mset(C, "pool", MK[:, :], 1.0, [bMK])
        P.op("pool", lambda: nc.gpsimd.affine_select(out=MK[:, 0:128], in_=MK[:, 0:128], pattern=[[1, 128]],
                                                     compare_op=ALU.is_ge, fill=0.0, base=0, channel_multiplier=-1),
             reads=[bMK], writes=[bMK])
        P.op("pool", lambda: nc.gpsimd.affine_select(out=MK[:, 128:256], in_=MK[:, 128:256], pattern=[[-1, 128]],
                                                     compare_op=ALU.is_gt, fill=0.0, base=0, channel_multiplier=1),
             reads=[bMK], writes=[bMK])
        MSf, bMSf = sb([128, TT])
        MSb, bMSb = sb([128, TT])
        mset(C, "pool", MSf[:, :], 1.0, [bMSf])
        mset(C, "pool", MSb[:, :], 1.0, [bMSb])
        mset(C, "pool", MSf[:, :].rearrange("p (c s) -> p c s", s=GC)[:, :, 0:1], 0.0, [bMSf])
        mset(C, "pool", MSb[:, :].rearrange("p (c s) -> p c s", s=GC)[:, :, GC - 1:GC], 0.0, [bMSb])
        IDb, bIDb = sb([128, 128], BF16)
        ID, bID = sb([128, 128])
        mset(C, "pool", ID[:, :], 1.0, [bID])
        P.op("pool", lambda: nc.gpsimd.affine_select(out=ID[:, :], in_=ID[:, :], pattern=[[-1, 128]],
                                                     compare_op=ALU.is_equal, fill=0.0, base=0, channel_multiplier=1),
             reads=[bID], writes=[bID])
        cp(C, "dve", IDb[:, :], ID[:, :], [bID], [bIDb])
        SBs, bSBs = sb([128, NCg, 2, 128], BF16, "SBs")
        def run_bg(n=1):
            if bg is not None:
                for _ in range(n):
                    next(bg, None)
        Sst = [sb([128, 2, 128]) for _ in range(2)]
        Sbf = [sb([128, 2, 128], BF16) for _ in range(2)]
        rot = {}

        def rb(key, n, shape, dt=F32):
            if key not in rot:
                rot[key] = [[sb(shape, dt, key) for _ in range(n)], 0]
            lst, i = rot[key]
            rot[key][1] = (i + 1) % n
            return lst[i]

        def rev(t, n):
            return bass.AP(t, n - 1, [[n, 128], [-1, n]])

        def gates(t, d, GL, bGL, Q, bQ, Kt, bK, getps=None):
            getps = getps or C.psum
            QD, bQD = rb("QD", 4, [128, 2, TT], BF16)
            KI, bKI = rb("KI", 4, [128, 2, TT], BF16)
            KE, bKE = rb("KE", 2, [128, 2, TT], BF16)
            DEC, bDEC = rb("DEC", 2, [128, 2, CPT])
            for th in range(2):
                ps, bps = getps()
                mm(C, ps[:, :], WG[:, d, th, :], GL[:, :], True, True, [bWG, bGL], [bps])
                E, bE = rb("gE", 1, [128, TT])
                Lg, bLg = E, bE
                Cm, bCm = rb("gC", 1, [128, TT])
                E1, bE1 = rb("gE1", 2, [128, TT])
                E2, bE2 = Cm, bCm
                col = d * 2 + th
                P.op("act", lambda: nc.scalar.activation(E[:, :], ps[:, :], AF.Exp, bias=BG[:, col:col + 1], scale=-1.0),
                     reads=[bps, bBG], writes=[bE])
                actf(C, Lg[:, :], E[:, :], AF.Ln, [bE], [bLg], bias=1.0)
                if d == 0:
                    P.op("dve", lambda: nc.vector.tensor_tensor_scan(Cm[:, :], MSf[:, :], Lg[:, :], 0.0, ALU.mult, ALU.add),
                         reads=[bMSf, bLg], writes=[bCm])
                else:
                    P.op("dve", lambda: nc.vector.tensor_tensor_scan(rev(Cm, TT), rev(MSb, TT), rev(Lg, TT), 0.0,
                                                                     ALU.mult, ALU.add),
                         reads=[bMSb, bLg], writes=[bCm])
                actf(C, E1[:, :], Cm[:, :], AF.Exp, [bCm], [bE1], scale=-1.0 / 16)
                actf(C, E2[:, :], Cm[:, :], AF.Exp, [bCm], [bE2], scale=1.0 / 16)
                P.op("dve", lambda: nc.vector.scalar_tensor_tensor(QD[:, th, :], Q[:, th, :], 0.125, E1[:, :],
                                                                  ALU.mult, ALU.mult), reads=[bQ, bE1], writes=[bQD])
                tt(C, "dve", KI[:, th, :], Kt[:, th, :], E2[:, :], ALU.mult, [bK, bE2], [bKI])
                e1c = E1[:, :].rearrange("p (c s) -> p c s", s=GC)
                dcol = e1c[:, :, GC - 1] if d == 0 else e1c[:, :, 0]
                cp(C, "dve", DEC[:, th, :], dcol, [bE1], [bDEC])
                tt(C, "dve", KE[:, th, :].rearrange("p (c s) -> p c s", s=GC),
                   KI[:, th, :].rearrange("p (c s) -> p c s", s=GC),
                   DEC[:, th, :].unsqueeze(2).broadcast_to([128, CPT, GC]), ALU.mult, [bKI, bDEC], [bKE])
            return (QD, bQD), (KI, bKI), (KE, bKE), (DEC, bDEC)

        def load_tile(t, want_q):
            i = t % 2
            sl = slice(t * TT, (t + 1) * TT)
            GL, bGL = rb("GL", 2, [32, TT], BF16)
            Kt, bK = rb("Kt", 2, [128, 2, TT], BF16)
            V, bV = rb("V", 2, [128, CPT, 512], BF16)
            Q, bQ = rb("Q", 2, [128, 2, TT], BF16)
            P.dma("sp", GL[:, :], GLO_d[:, sl], writes=[bGL], slot=f"gl{i}")
            P.dma("sp", Kt[:, :, :], GK_d[:, :, sl].rearrange("a p t -> p a t"), writes=[bK], slot=f"gk{i}")
            P.dma("sp", V[:, :, :], GV_d[sl, :].rearrange("(c p) f -> p c f", p=128), writes=[bV], slot=f"gv{i}")
            P.dma("sp", Q[:, :, :], GQ_d[:, :, sl].rearrange("a p t -> p a t"), writes=[bQ], slot=f"gq{i}")
            return (GL, bGL), (Kt, bK), (V, bV), (Q, bQ)

        def kv_update(d, c_in_tile, th, KE, bKE, V, bV, DEC, bDEC, getps=None):
            getps = getps or C.psum
            S, bS = Sst[d]
            cs = slice(c_in_tile * GC, (c_in_tile + 1) * GC)
            pt, bpt = getps()
            mm(C, pt[:, 0:128], KE[:, th, cs], IDb[:, :], True, True, [bKE, bIDb], [bpt])
            KEt, bKEt = rb("KEt", 2, [128, 128], BF16)
            cp(C, "act", KEt[:, :], pt[:, 0:128], [bpt], [bKEt])
            pk_, bpk = getps()
            for hh in range(2):
                h = th * 2 + hh
                mm(C, pk_[:, hh * 128:(hh + 1) * 128], KEt[:, :], V[:, c_in_tile, h * 128:(h + 1) * 128],
                   True, True, [bKEt, bV], [bpk], signal=(hh == 1))
            for hh in range(2):
                r = slice(64 * hh, 64 * hh + 64)
                P.op("dve", lambda: nc.vector.scalar_tensor_tensor(
                    S[r, th, :], S[r, th, :], DEC[r, th, c_in_tile:c_in_tile + 1], pk_[r, hh * 128:(hh + 1) * 128],
                    ALU.mult, ALU.add), reads=[bS, bDEC, bpk], writes=[bS])

        for d in range(2):
            mset(C, "dve", Sst[d][0][:, :, :], 0.0, [Sst[d][1]])
        dbg = getattr(C, "dbg", 9)
        for t in range(NT - 1, -1, -1):
            (GL, bGL), (Kt, bK), (V, bV), (Q, bQ) = load_tile(t, True)
            (QD, bQD), (KI, bKI), (KE, bKE), (DEC, bDEC) = gates(t, 1, GL, bGL, Q, bQ, Kt, bK)
            for ci in range(CPT - 1, -1, -1):
                if dbg < 2:
                    break
                c = t * CPT + ci
                cp(C, "act", SBs[:, c, :, :], Sst[1][0][:, :, :], [Sst[1][1]], [bSBs])
                for th in range(2):
                    kv_update(1, ci, th, KE, bKE, V, bV, DEC, bDEC)
                    run_bg(max(1, bg_per_chunk // 4))
        getR = lambda: C.psum_pool("glaR", [2, 3, 4, 5, 6])
        pos = [C.psum_fixed(0), C.psum_fixed(1)]

        def post(c, OG, bOG, cs):
            gs = slice(c * GC, (c + 1) * GC)
            GO, bGO = rb("GO", 2, [128, 4, GC], BF16)
            for hh in range(2):
                po, bpo = pos[hh]
                SQ, bSQ = rb("SQ", 2, [128, 256], BF16)
                actf(C, SQ[:, :], po[:, 0:256], AF.Square, [bpo], [bSQ])
                pst, bpst = C.psum_stat()
                mm(C, pst[:, 0:256], C.ones[:, :], SQ[:, :], True, True, [bSQ, C.b_ones], [bpst])
                RS, bRS = rb("RS", 2, [128, 256])
                actf(C, RS[:, :], pst[:, 0:256], AF.Ln, [bpst], [bRS], bias=EPS, scale=1.0 / 128)
                actf(C, RS[:, :], RS[:, :], AF.Exp, [bRS], [bRS], scale=-0.5)
                ON, bON = rb("ON", 2, [128, 256])
                tt(C, "dve", ON[:, :], po[:, 0:256], RS[:, :], ALU.mult, [bpo, bRS], [bON])
                gov = GO[:, :, :].rearrange("p (a b) i -> p a b i", b=2)[:, :, hh, :]
                ogv = OG[:, :, cs].rearrange("p (a b) i -> p a b i", b=2)[:, :, hh, :]
                tt(C, "dve", gov, ON[:, :].rearrange("p (a i) -> p a i", i=GC), ogv, ALU.mult, [bON, bOG], [bGO])
            P.dma("pool", GO_d[:, :, gs].rearrange("h p t -> p h t"), GO[:, :, :], reads=[bGO], slot=f"go{c % 2}")

        prev = None
        for t in range(NT if dbg >= 3 else 0):
            (GL, bGL), (Kt, bK), (V, bV), (Q, bQ) = load_tile(t, True)
            (QDb, bQDb), (KIb, bKIb), _, _ = gates(t, 1, GL, bGL, Q, bQ, Kt, bK, getR)
            (QD, bQD), (KI, bKI), (KE, bKE), (DEC, bDEC) = gates(t, 0, GL, bGL, Q, bQ, Kt, bK, getR)
            OG, bOG = rb("OG", 2, [128, 4, TT], BF16)
            P.dma("sp", OG[:, :, :], GOG_d[:, :, t * TT:(t + 1) * TT].rearrange("h p t -> p h t"), writes=[bOG],
                  slot=f"gg{t % 2}")
            for ci in range(CPT):
                c = t * CPT + ci
                cs = slice(ci * GC, (ci + 1) * GC)
                gs = slice(c * GC, (c + 1) * GC)
                sf, bsf = Sbf[c % 2]
                cp(C, "act", sf[:, :, :], Sst[0][0][:, :, :], [Sst[0][1]], [bsf])
                for th in range(2):
                    kv_update(0, ci, th, KE, bKE, V, bV, DEC, bDEC, getR)
                    run_bg(max(1, bg_per_chunk // 4))
                if prev is not None:
                    post(*prev)
                run_bg(max(1, bg_per_chunk // 2))
                Ats = []
                for h in range(4):
                    th, hh = h // 2, h % 2
                    r = slice(64 * hh, 64 * hh + 64)
                    psc, bpsc = getR()
                    mm(C, psc[:, 0:128], KI[r, th, cs], QD[r, th, cs], True, True, [bKI, bQD], [bpsc], signal=False)
                    mm(C, psc[:, 128:256], KIb[r, th, cs], QDb[r, th, cs], True, True, [bKIb, bQDb], [bpsc])
                    At, bAt = rb("At", 4, [128, 256], BF16)
                    tt(C, "dve", At[:, :], psc[:, 0:256], MK[:, :], ALU.mult, [bpsc, bMK], [bAt])
                    Ats.append((At, bAt))
                for h in range(4):
                    th, hh = h // 2, h % 2
                    r = slice(64 * hh, 64 * hh + 64)
                    po, bpo = pos[hh]
                    At, bAt = Ats[h]
                    oh = po[:, th * 128:(th + 1) * 128]
                    vh = V[:, ci, h * 128:(h + 1) * 128]
                    mm(C, oh, vh, At[:, 0:128], True, False, [bV, bAt], [bpo], signal=False)
                    mm(C, oh, vh, At[:, 128:256], False, False, [bV, bAt], [bpo], signal=False)
                    mm(C, oh, SBs[r, c, th, :], QDb[r, th, cs], False, False, [bSBs, bQDb], [bpo], signal=False)
                    mm(C, oh, sf[r, th, :], QD[r, th, cs], False, True, [bsf, bQD], [bpo], signal=(th == 1))
                run_bg(max(1, bg_per_chunk // 2))
                prev = (c, OG, bOG, cs)
        if prev is not None:
            post(*prev)
        P.barrier()


def build_test_gla(L):
    pk = build_pack(None, True)
    nc = bass.Bass("TRN2", target_bir_lowering=False)
    GQ = nc.dram_tensor("GQ", [2, 128, L], BF16, kind="ExternalInput")
    GK = nc.dram_tensor("GK", [2, 128, L], BF16, kind="ExternalInput")
    GV = nc.dram_tensor("GV", [L, 512], BF16, kind="ExternalInput")
    GLO = nc.dram_tensor("GLO", [32, L], BF16, kind="ExternalInput")
    GOG = nc.dram_tensor("GOG", [4, 128, L], BF16, kind="ExternalInput")
    gkw = nc.dram_tensor("gkw", [32, 512], F32, kind="ExternalInput")
    gkb = nc.dram_tensor("gkb", [128, 4], F32, kind="ExternalInput")
    nrm = nc.dram_tensor("nrm", [128, 56], F32, kind="ExternalInput")
    GO = nc.dram_tensor("GO", [4, 128, L], BF16, kind="ExternalOutput")
    with ExitStack() as st:
        P = Prog(nc, st)
        C = Ctx(nc, P, L, pk)
        setup_common(C, nrm)
        gla_main(C, GQ, GK, GV, GLO, GOG, gkw, gkb, GO)
        print("instructions:", P.ninst)
    return nc


def proj_fm(C, key, ntiles, nk, rhs_fn, rhs_bufs, evac_fn, perm_out=False):
    wt, bw = C.W.get(key)
    wv = wt[:, 0:nk * ntiles * 128].rearrange("p (k c) -> p k c", k=nk)
    for j in range(ntiles):
        ps, bps = C.psum()
        out = ps[:, :].rearrange("p (s c) -> p s c", c=32) if perm_out else ps[:, :]
        for kt in range(nk):
            rb_ = rhs_bufs(kt) if callable(rhs_bufs) else rhs_bufs
            mm(C, out, wv[:, kt, j * 128:(j + 1) * 128], rhs_fn(kt), kt == 0, kt == nk - 1, [bw] + rb_, [bps])
        evac_fn(j, ps, bps)


def proj_tm(C, key, H, bH, nk, evac_fn, resident=None):
    wt, bw = resident if resident is not None else C.W.get(key)
    wv = wt[:, 0:nk * 512].rearrange("p (k c) -> p k c", k=nk)
    for i in range(TT // 128):
        ps, bps = C.psum()
        for kt in range(nk):
            mm(C, ps[:, :], H[:, kt, i * 128:(i + 1) * 128], wv[:, kt, :], kt == 0, kt == nk - 1, [bw, bH[kt]], [bps])
        evac_fn(i, ps, bps)


def resid_proj(C, tag, nchunks, nk, rhs_fn, rhs_bufs, X, bX, nxt, resident=None):
    P, nc = C.P, C.nc
    st = rms_begin(C)
    for c in range(nchunks):
        if resident is not None:
            wt, bw = resident[0][:, c, :], resident[1][c]
        else:
            wt, bw = C.W.get((tag, c))
        wv = wt[:, 0:nk * 256].rearrange("p (k c) -> p k c", k=nk)
        for j in range(2):
            m = 2 * c + j
            ps, bps = C.psum()
            for kt in range(nk):
                mm(C, ps[:, :], wv[:, kt, j * 128:(j + 1) * 128], rhs_fn(kt), kt == 0, kt == nk - 1,
                   [bw] + rhs_bufs(kt), [bps])
            if m >= 1:
                rms_stat(C, st, m - 1)
            tt(C, "dve", X[:, m, :], ps[:, :], X[:, m, :], ALU.add, [bps, bX[m]], [bX[m]])
            rms_square(C, st, X, bX, m)
    rms_stat(C, st, 7)
    G, gcol, Hn, bHn = nxt
    rms_finish(C, st, X, bX, G, gcol, Hn, bHn)


def p1_keys():
    ks = ffn_keys("f10")
    ks += [("abu", c) for c in range(4)] + [("abq", 0), ("abk", 0), ("abog", 0), ("abog", 1), ("abglo", 0), ("abv", 0)]
    return ks


def pack_p1(pk, inp, meta):
    if meta:
        pack_fm(pk, "abu", Shape(1024, 1024), 256, True)
        pack_fm(pk, "abq", Shape(1024, 256), 256, True)
        pack_fm(pk, "abk", Shape(1024, 256), 256, True)
        pack_fm(pk, "abog", Shape(1024, 512), 256, True)
        pk.add_meta(("abglo", 0), 8 * 128)
        pk.add_meta(("abv", 0), 8 * 512)
        return
    w = inp["ab_w_in"][0]
    wu = np.zeros((1024, 32, 32), np.float32)
    wu[:, :, 0:16] = w[:, 0:512].reshape(1024, 32, 16)
    pack_fm(pk, "abu", wu.reshape(1024, 1024), 256)
    pack_fm(pk, "abq", w[:, 512:768], 256)
    pack_fm(pk, "abk", w[:, 768:1024], 256)
    pack_fm(pk, "abog", w[:, 1536:2048], 256)
    wg = np.zeros((1024, 128), np.float32)
    wg[:, 0:32] = w[:, 2048:2080]
    pk.add(("abglo", 0), kt_split(wg))
    pk.add(("abv", 0), kt_split(w[:, 1024:1536]))


class Shape:
    def __init__(self, *s):
        self.shape = s


def pack_fm(pk, tag, w, ncols_chunk, meta_only=False):
    k, n = w.shape
    assert n % ncols_chunk == 0
    for c in range(n // ncols_chunk):
        key = (tag, c)
        if meta_only:
            pk.add_meta(key, (k // 128) * ncols_chunk)
        else:
            pk.add(key, kt_split(w[:, c * ncols_chunk:(c + 1) * ncols_chunk]))


def phase1(C, xT_d, X1_d, UT_d, GQ_d, GK_d, GV_d, GLO_d, GOG_d):
    P, nc = C.P, C.nc
    xv = xT_d[:, :].rearrange("(k p) t -> p k t", p=128)
    x1v = X1_d[:, :].rearrange("(k p) t -> p k t", p=128)
    for i in range(2):
        for nm in ("xin", "xout", "p1u", "p1q", "p1k", "p1g", "p1l", "p1v"):
            P.slot(f"{nm}{i}")
    for t in range(C.NT):
        C.W.schedule(p1_keys())
    with ExitStack() as st:
        C.pstack = st
        C.rot = {}
        def load_x(t):
            X, bX = C.rotbuf("X", 2, [128, 8, TT], F32, nb=8)
            P.dma("sp", X[:, :, :], xv[:, :, t * TT:(t + 1) * TT], writes=bX, slot=f"xin{t % 2}")
            return X, bX
        nxt_x = load_x(0)
        nxt_h = C.rotbuf("H", 2, [128, 8, TT], BF16, nb=8)
        emit_rmsnorm(C, nxt_x[0], nxt_x[1], C.G, 0, nxt_h[0], nxt_h[1])
        for t in range(C.NT):
            sl = slice(t * TT, (t + 1) * TT)
            i2 = t % 2
            X, bX = nxt_x
            H, bH = nxt_h
            if t + 1 < C.NT:
                nxt_x = load_x(t + 1)
            Hp, bHp = C.rotbuf("Hp", 1, [128, 8, TT], BF16, nb=8)
            emit_ffn(C, "f10", X, bX, H, bH, nxt=(C.G, 8, H, bH, (Hp, bHp)))
            if t + 1 < C.NT:
                nxt_h = C.rotbuf("H", 2, [128, 8, TT], BF16, nb=8)
                emit_rmsnorm(C, nxt_x[0], nxt_x[1], C.G, 0, nxt_h[0], nxt_h[1])
            P.dma("pool", x1v[:, :, sl], X[:, :, :], reads=bX, slot=f"xout{i2}")
            US, bUS = C.rotbuf("US", 2, [128, 8, TT], BF16)
            for c in range(4):
                def ev(j, ps, bps, c=c):
                    cp(C, "act" if j == 0 else "dve", US[:, 2 * c + j, :], ps[:, :], [bps], [bUS])
                proj_fm(C, ("abu", c), 2, 8, lambda kt: Hp[:, kt, :], lambda kt: [bHp[kt]], ev)
            P.dma("pool", UT_d[:, :, sl].rearrange("b p t -> p b t"), US[:, :, :], reads=[bUS], slot=f"p1u{i2}")
            hnat = lambda kt: H[:, kt, :]
            QS, bQS = C.rotbuf("QS", 2, [128, 2, TT], BF16)
            KS, bKS = C.rotbuf("KS", 2, [128, 2, TT], BF16)
            proj_fm(C, ("abq", 0), 2, 8, hnat, lambda kt: [bH[kt]],
                    lambda j, ps, bps: cp(C, "act" if j == 0 else "dve", QS[:, j, :], ps[:, :], [bps], [bQS]))
            P.dma("pool", GQ_d[:, :, sl].rearrange("a p t -> p a t"), QS[:, :, :], reads=[bQS], slot=f"p1q{i2}")
            proj_fm(C, ("abk", 0), 2, 8, hnat, lambda kt: [bH[kt]],
                    lambda j, ps, bps: cp(C, "act" if j == 0 else "dve", KS[:, j, :], ps[:, :], [bps], [bKS]))
            P.dma("pool", GK_d[:, :, sl].rearrange("a p t -> p a t"), KS[:, :, :], reads=[bKS], slot=f"p1k{i2}")
            OGS, bOGS = C.rotbuf("OGS", 2, [128, 4, TT], BF16)
            for c in range(2):
                def ev(j, ps, bps, c=c):
                    h = 2 * c + j
                    SG, bSG = C.rotbuf("SG", 2, [128, TT], F32)
                    actf(C, SG[:, :], ps[:, :], AF.Silu, [bps], [bSG])
                    ts(C, "dve", OGS[:, h, :], SG[:, :], C.GN[:, h:h + 1], None, ALU.mult, None, [bSG, C.bGN], [bOGS])
                proj_fm(C, ("abog", c), 2, 8, hnat, lambda kt: [bH[kt]], ev)
            P.dma("pool", GOG_d[:, :, sl].rearrange("a p t -> p a t"), OGS[:, :, :], reads=[bOGS], slot=f"p1g{i2}")
            LS, bLS = C.rotbuf("LS", 2, [32, TT], BF16)
            proj_fm(C, ("abglo", 0), 1, 8, hnat, lambda kt: [bH[kt]],
                    lambda j, ps, bps: cp(C, "act", LS[:, :], ps[0:32, :], [bps], [bLS]))
            P.dma("pool", GLO_d[:, sl], LS[:, :], reads=[bLS], slot=f"p1l{i2}")
            VS, bVS = C.rotbuf("VS", 2, [128, 4, 512], BF16)
            proj_tm(C, ("abv", 0), H, bH, 8,
                    lambda i, ps, bps: cp(C, "act" if i % 2 == 0 else "dve", VS[:, i, :], ps[:, :], [bps], [bVS]))
            P.dma("pool", GV_d[sl, :].rearrange("(c p) f -> p c f", p=128), VS[:, :, :], reads=[bVS], slot=f"p1v{i2}")
        if getattr(C.W, "bg", None) is not None:
            for _ in C.W.bg:
                pass
            C.W.bg = None
        P.barrier()
    C.pstack = None


RC = 128
LGF = [math.log1p(-2.0 ** (-5 - h)) for h in range(8)]
LGB = LGF[::-1]


def ret_main(C, RQ_d, RK_d, RV_d, ROG_d, SBD_d, RO_d):
    P, nc = C.P, C.nc
    L = C.L
    NCr = L // RC
    for i in range(2):
        for nm in ("rq", "rk", "rv", "rg", "rs", "ro"):
            P.slot(f"{nm}{i}")
    P.slot("rg2")
    with ExitStack() as st:
        def sb(shape, dt=F32, name=None):
            return P.sb(shape, dt, stack=st), Buf(name or "")
        rot = {}

        def rb(key, n, shape, dt=F32):
            if key not in rot:
                rot[key] = [[sb(shape, dt, key) for _ in range(n)], 0]
            lst, i = rot[key]
            rot[key][1] = (i + 1) % n
            return lst[i]
        ID, bID = sb([128, 128])
        IDb, bIDb = sb([128, 128], BF16)
        mset(C, "pool", ID[:, :], 1.0, [bID])
        P.op("pool", lambda: nc.gpsimd.affine_select(out=ID[:, :], in_=ID[:, :], pattern=[[-1, 128]],
                                                     compare_op=ALU.is_equal, fill=0.0, base=0, channel_multiplier=1),
             reads=[bID], writes=[bID])
        cp(C, "dve", IDb[:, :], ID[:, :], [bID], [bIDb])
        EI, bEI = sb([128, 128], mybir.dt.int32)
        E, bE = sb([128, 128])
        Ep, bEp = sb([128, 128])
        En, bEn = sb([128, 128])
        P.op("pool", lambda: nc.gpsimd.iota(EI[:, :], [[1, 128]], base=0, channel_multiplier=-1), writes=[bEI])
        cp(C, "dve", E[:, :], EI[:, :], [bEI], [bE])
        ts(C, "dve", Ep[:, :], E[:, :], 0.0, None, ALU.max, None, [bE], [bEp])
        ts(C, "dve", En[:, :], E[:, :], -1.0, 0.0, ALU.mult, ALU.max, [bE], [bEn])
        DT, bDT = sb([128, 8, 128])
        ARG, bARG = sb([128, 128])
        for h in range(8):
            ts(C, "dve", ARG[:, :], Ep[:, :], LGF[h], None, ALU.mult, None, [bEp], [bARG])
            P.op("dve", lambda: nc.vector.scalar_tensor_tensor(ARG[:, :], En[:, :], LGB[h], ARG[:, :], ALU.mult, ALU.add),
                 reads=[bEn, bARG], writes=[bARG])
            actf(C, DT[:, h, :], ARG[:, :], AF.Exp, [bARG], [bDT])
        IRI, bIRI = sb([128, 128], mybir.dt.int32)
        IR, bIR = sb([128, 128])
        P.op("pool", lambda: nc.gpsimd.iota(IRI[:, :], [[1, 128]], base=0, channel_multiplier=0), writes=[bIRI])
        cp(C, "dve", IR[:, :], IRI[:, :], [bIRI], [bIR])
        IPI, bIPI = sb([128, 1], mybir.dt.int32)
        IP, bIP = sb([128, 1])
        P.op("pool", lambda: nc.gpsimd.iota(IPI[:, :], [[0, 1]], base=0, channel_multiplier=1), writes=[bIPI])
        cp(C, "dve", IP[:, :], IPI[:, :], [bIPI], [bIP])
        XIf, bXIf = sb([128, 8, 128], BF16)
        XIb, bXIb = sb([128, 8, 128], BF16)
        ZF, bZF = sb([128, 8])
        ZB, bZB = sb([128, 8])
        for h in range(8):
            actf(C, XIf[:, h, :], IR[:, :], AF.Exp, [bIR], [bXIf], bias=LGF[h], scale=LGF[h])
            actf(C, XIb[:, h, :], IR[:, :], AF.Exp, [bIR], [bXIb], bias=RC * LGB[h], scale=-LGB[h])
            actf(C, ZF[:, h:h + 1], IP[:, :], AF.Exp, [bIP], [bZF], bias=(RC - 1) * LGF[h], scale=-LGF[h])
            actf(C, ZB[:, h:h + 1], IP[:, :], AF.Exp, [bIP], [bZB], scale=LGB[h])
        GF = [math.exp(RC * LGF[h]) for h in range(8)]
        GB = [math.exp(RC * LGB[h]) for h in range(8)]
        Sp = [sb([128, 8, 256], F32, "Sstate0")[0], sb([128, 8, 256], F32, "Sstate1")[0]]
        bSp = [[Buf() for _ in range(8)], [Buf() for _ in range(8)]]

        def load_kv(c):
            i = c % 2
            cs = slice(c * RC, (c + 1) * RC)
            Kt, bK = rb("Kt", 2, [128, 8, RC], BF16)
            V, bV = rb("V", 2, [128, 2048], BF16)
            P.dma("sp", Kt[:, :, :], RK_d[:, :, cs].rearrange("h p t -> p h t"), writes=[bK], slot=f"rk{i}")
            P.dma("sp", V[:, :], RV_d[cs, :], writes=[bV], slot=f"rv{i}")
            return Kt, bK, V, bV

        def kv_update(k, Kt, bK, V, bV, Z, bZ, GAM, getps=None):
            getps = getps or C.psum
            So, bSo = Sp[k % 2], bSp[k % 2]
            Sn, bSn = Sp[(k + 1) % 2], bSp[(k + 1) % 2]
            for g in range(2):
                pt, bpt = getps()
                for hl in range(4):
                    h = 4 * g + hl
                    mm(C, pt[:, hl * 128:(hl + 1) * 128], Kt[:, h, :], IDb[:, :], True, True, [bK, bIDb], [bpt],
                       signal=(hl == 3))
                Kz, bKz = rb("Kz", 2, [128, 4, 128], BF16)
                tt(C, "dve", Kz[:, :, :], pt[:, :].rearrange("p (a b) -> p a b", b=128),
                   Z[:, 4 * g:4 * g + 4].unsqueeze(2).broadcast_to([128, 4, 128]), ALU.mult, [bpt, bZ], [bKz])
                for hp in range(2):
                    pk_, bpk = getps()
                    for hq in range(2):
                        hl = 2 * hp + hq
                        h = 4 * g + hl
                        mm(C, pk_[:, hq * 256:(hq + 1) * 256], Kz[:, hl, :], V[:, h * 256:(h + 1) * 256], True, True,
                           [bKz, bV], [bpk], signal=(hq == 1))
                    for hq in range(2):
                        h = 4 * g + 2 * hp + hq
                        P.op("dve", lambda: nc.vector.scalar_tensor_tensor(
                            Sn[:, h, :], So[:, h, :], GAM[h], pk_[:, hq * 256:(hq + 1) * 256], ALU.mult, ALU.add),
                            reads=[bSo[h], bpk], writes=[bSn[h]])

        mset(C, "dve", Sp[0][:, :, :], 0.0, bSp[0])
        k = 0
        for c in range(NCr - 1, -1, -1):
            Kt, bK, V, bV = load_kv(c)
            Sb16, bSb16 = rb("Sb16", 2, [128, 8, 256], BF16)
            cp(C, "act", Sb16[:, :, :], Sp[k % 2][:, :, :], bSp[k % 2], [bSb16])
            P.dma("act", SBD_d[c], Sb16[:, :, :], reads=[bSb16], slot=f"rs{c % 2}")
            kv_update(k, Kt, bK, V, bV, ZB, bZB, GB)
            k += 1
        P.barrier()
        mset(C, "dve", Sp[k % 2][:, :, :], 0.0, bSp[k % 2])
        RP = [4, 5, 6]
        getR = lambda: C.psum_pool("retR", RP)
        OB = [[C.psum_fixed(0), C.psum_fixed(1)], [C.psum_fixed(2), C.psum_fixed(3)]]
        for c in range(NCr):
            i = c % 2
            cs = slice(c * RC, (c + 1) * RC)
            Kt, bK, V, bV = load_kv(c)
            Q, bQ = rb("Q", 2, [128, 8, RC], BF16)
            SB, bSB = rb("SB", 2, [128, 8, 256], BF16)
            OG, bOG = rb("OG", 2, [128, 16, RC], BF16)
            P.dma("sp", Q[:, :, :], RQ_d[:, :, cs].rearrange("h p t -> p h t"), writes=[bQ], slot=f"rq{i}")
            P.dma("sp", SB[:, :, :], SBD_d[c], writes=[bSB], slot=f"rs{i}")
            P.dma("sp", OG[:, :, :], ROG_d[:, :, cs].rearrange("h p t -> p h t"), writes=[bOG], slot=f"rg{i}")
            Sf16, bSf16 = rb("Sf16", 2, [128, 8, 256], BF16)
            cp(C, "act", Sf16[:, :, :], Sp[k % 2][:, :, :], bSp[k % 2], [bSf16])
            Qf, bQf = rb("Qf", 2, [128, 8, RC], BF16)
            Qb, bQb = rb("Qb", 2, [128, 8, RC], BF16)
            tt(C, "pool", Qf[:, :, :], Q[:, :, :], XIf[:, :, :], ALU.mult, [bQ, bXIf], [bQf])
            tt(C, "pool", Qb[:, :, :], Q[:, :, :], XIb[:, :, :], ALU.mult, [bQ, bXIb], [bQb])
            kv_update(k, Kt, bK, V, bV, ZF, bZF, GF, getR)
            k += 1
            At, bAt = rb("At", 2, [128, 8, RC], BF16, )
            bAtg = [Buf(), Buf()]
            for g in range(2):
                psc, bpsc = getR()
                for hl in range(4):
                    h = 4 * g + hl
                    mm(C, psc[:, hl * 128:(hl + 1) * 128], Kt[:, h, :], Q[:, h, :], True, True, [bK, bQ], [bpsc],
                       signal=(hl == 3))
                tt(C, "dve", At[:, 4 * g:4 * g + 4, :], psc[:, :].rearrange("p (a b) -> p a b", b=128),
                   DT[:, 4 * g:4 * g + 4, :], ALU.mult, [bpsc, bDT, bAt], [bAtg[g]])
            RO, bRO = rb("RO", 2, [128, 16, RC], BF16)
            sqs = []
            for g in range(2):
                for dvt in range(2):
                    pb, bpb = OB[g][dvt]
                    for hl in range(4):
                        h = 4 * g + hl
                        oh = pb[:, hl * 128:(hl + 1) * 128]
                        vs = slice(h * 256 + dvt * 128, h * 256 + dvt * 128 + 128)
                        ss = slice(dvt * 128, dvt * 128 + 128)
                        mm(C, oh, V[:, vs], At[:, h, :], True, False, [bV, bAtg[g]], [bpb], signal=False)
                        mm(C, oh, SB[:, h, ss], Qb[:, h, :], False, False, [bSB, bQb], [bpb], signal=False)
                        mm(C, oh, Sf16[:, h, ss], Qf[:, h, :], False, True, [bSf16, bQf], [bpb], signal=(hl == 3))
                for dvt in range(2):
                    pb, bpb = OB[g][dvt]
                    SQ, bSQ = rb("SQ", 4, [128, 512], BF16)
                    actf(C, SQ[:, :], pb[:, :], AF.Square, [bpb], [bSQ])
                    sqs.append((g, dvt, SQ, bSQ))
            for g in range(2):
                pst, bpst = C.psum_stat()
                for (g_, dvt, SQ, bSQ) in sqs:
                    if g_ == g:
                        mm(C, pst[:, :], C.ones[:, :], SQ[:, :], dvt == 0, dvt == 1, [bSQ, C.b_ones], [bpst])
                RS, bRS = rb("RS", 2, [128, 512])
                actf(C, RS[:, :], pst[:, :], AF.Ln, [bpst], [bRS], bias=EPS, scale=1.0 / 256)
                actf(C, RS[:, :], RS[:, :], AF.Exp, [bRS], [bRS], scale=-0.5)
                for dvt in range(2):
                    pb, bpb = OB[g][dvt]
                    ON, bON = rb("ON", 2, [128, 512])
                    tt(C, "dve", ON[:, :], pb[:, :], RS[:, :], ALU.mult, [bpb, bRS], [bON])
                    rov = RO[:, 8 * g:8 * g + 8, :].rearrange("p (a b) i -> p a b i", b=2)[:, :, dvt, :]
                    ogv = OG[:, 8 * g:8 * g + 8, :].rearrange("p (a b) i -> p a b i", b=2)[:, :, dvt, :]
                    tt(C, "pool", rov, ON[:, :].rearrange("p (a i) -> p a i", i=RC), ogv, ALU.mult, [bON, bOG], [bRO])
            P.dma("pool", RO_d[:, :, cs].rearrange("h p t -> p h t"), RO[:, :, :], reads=[bRO], slot=f"ro{c % 2}")
        P.barrier()


def build_test_ret(L):
    pk = build_pack(None, True)
    nc = bass.Bass("TRN2", target_bir_lowering=False)
    RQ = nc.dram_tensor("RQ", [8, 128, L], BF16, kind="ExternalInput")
    RK = nc.dram_tensor("RK", [8, 128, L], BF16, kind="ExternalInput")
    RV = nc.dram_tensor("RV", [L, 2048], BF16, kind="ExternalInput")
    ROG = nc.dram_tensor("ROG", [16, 128, L], BF16, kind="ExternalInput")
    nrm = nc.dram_tensor("nrm", [128, 56], F32, kind="ExternalInput")
    SBD = nc.dram_tensor("SBD", [L // RC, 128, 8, 256], BF16, kind="Internal")
    RO = nc.dram_tensor("RO", [16, 128, L], BF16, kind="ExternalOutput")
    with ExitStack() as st:
        P = Prog(nc, st)
        C = Ctx(nc, P, L, pk)
        setup_common(C, nrm)
        ret_main(C, RQ, RK, RV, ROG, SBD, RO)
        print("instructions:", P.ninst)
    return nc


def pad_rows_s5(w):
    out = np.zeros((32, 32, w.shape[1]), np.float32)
    out[:, 0:16, :] = w.reshape(32, 16, w.shape[1])
    return out.reshape(1024, w.shape[1])


def pack_p3(pk, inp, meta):
    if meta:
        pack_fm(pk, "wglu", Shape(512, 512), 256, True)
        pack_fm(pk, "abo", Shape(1024, 1024), 256, True)
        for h in range(8):
            pk.add_meta(("rqk", h), 8 * 256)
        pack_fm(pk, "rog", Shape(1024, 2048), 256, True)
        pack_fm(pk, "rv", Shape(1024, 2048), 512, True)
        return
    pack_fm(pk, "wglu", inp["s5_w_glu"][0], 256)
    pack_fm(pk, "abo", inp["ab_w_out"][0], 256)
    w = inp["ret_w_in"][0]
    for h in range(8):
        q = w[:, h * 128:(h + 1) * 128]
        k = w[:, 1024 + h * 128:1024 + (h + 1) * 128]
        pk.add(("rqk", h), kt_split(np.concatenate([q, k], axis=1)))
    pack_fm(pk, "rog", w[:, 4096:6144], 256)
    pack_fm(pk, "rv", w[:, 2048:4096], 512)


def pack_p5(pk, inp, meta):
    if meta:
        pack_fm(pk, "reto", Shape(2048, 1024), 256, True)
    else:
        pack_fm(pk, "reto", inp["ret_w_out"][0], 256)


def p3_keys():
    ks = [("wglu", c) for c in range(2)] + [("abo", c) for c in range(4)]
    ks += ffn_keys("f20") + ffn_keys("f11")
    ks += [("rqk", h) for h in range(8)] + [("rog", c) for c in range(8)]
    return ks


def p5_keys():
    return ffn_keys("f21")


def rotary_setup(C):
    P, nc = C.P, C.nc
    C.ROT = P.sb([128, 4], F32, name="rotc")
    C.bROT = Buf("rotc")
    IPI = P.sb([128, 1], mybir.dt.int32, name="rot_ipi")
    bI = Buf()
    P.op("pool", lambda: nc.gpsimd.iota(IPI[:, :], [[0, 1]], base=0, channel_multiplier=1), writes=[bI])
    R = [bI, C.bROT]
    cp(C, "dve", C.ROT[:, 2:3], IPI[:, :], R, [C.bROT])
    ts(C, "dve", C.ROT[:, 3:4], C.ROT[:, 2:3], 64.0, None, ALU.is_ge, None, R, [C.bROT])
    ts(C, "dve", C.ROT[:, 1:2], C.ROT[:, 3:4], 2.0, -1.0, ALU.mult, ALU.add, R, [C.bROT])
    P.op("dve", lambda: nc.vector.scalar_tensor_tensor(C.ROT[:, 2:3], C.ROT[:, 3:4], -64.0, C.ROT[:, 2:3],
                                                      ALU.mult, ALU.add), reads=R, writes=[C.bROT])
    actf(C, C.ROT[:, 0:1], C.ROT[:, 2:3], AF.Exp, R, [C.bROT], scale=-math.log(10000.0) / 64.0)


def phase3(C, X1_d, GY_d, GO_d, X4_d, RQ_d, RK_d, RV_d, ROG_d):
    P, nc = C.P, C.nc
    x1v = X1_d[:, :].rearrange("(k p) t -> p k t", p=128)
    x4v = X4_d[:, :].rearrange("(k p) t -> p k t", p=128)
    for i in range(2):
        for nm in ("xin", "xout", "p3y", "p3o", "p3q", "p3k", "p3g", "p3v"):
            P.slot(f"{nm}{i}")
    for i in range(8):
        P.slot(f"rsw{i}")
    for t in range(C.NT):
        C.W.schedule(p3_keys())
    with ExitStack() as st:
        C.pstack = st
        C.rot = {}
        POSI = P.sb([128, TT], mybir.dt.int32, stack=st)
        bPOSI = Buf()
        TI = P.sb([128, TT], mybir.dt.int32, stack=st)

        RVW = P.sb([128, 4, 4096], BF16, stack=st, name="rv_res")
        bRVW = [Buf(f"rvw{c}") for c in range(4)]
        for c in range(4):
            off, n = C.pk.chunks[("rv", c)]
            P.dma("sp", RVW[:, c, 0:n], bass.AP(C.W.wbf, off, [[n, 128], [1, n]]), writes=[bRVW[c]],
                  slot=P.slot(f"s5st{c}"))

        def load_in(t):
            sl = slice(t * TT, (t + 1) * TT)
            X, bX = C.rotbuf("X", 2, [128, 8, TT], F32, nb=8)
            GY, bGY = C.rotbuf("GY", 2, [128, 4, TT], BF16)
            GO, bGO = C.rotbuf("GO", 2, [128, 4, TT], BF16)
            P.dma("sp", X[:, :, :], x1v[:, :, sl], writes=bX, slot=f"xin{t % 2}")
            P.dma("sp", GY[:, :, :], GY_d[:, :, sl].rearrange("b p t -> p b t"), writes=[bGY], slot=f"p3y{t % 2}")
            P.dma("sp", GO[:, :, :], GO_d[:, :, sl].rearrange("h p t -> p h t"), writes=[bGO], slot=f"p3o{t % 2}")
            return (X, bX), (GY, bGY), (GO, bGO)
        for t in range(C.NT):
            sl = slice(t * TT, (t + 1) * TT)
            i2 = t % 2
            if t == 0:
                nxt_in = load_in(0)
            (X, bX), (GY, bGY), (GO, bGO) = nxt_in
            if t + 1 < C.NT:
                nxt_in = load_in(t + 1)
            H, bH = C.rotbuf("H", 1, [128, 8, TT], BF16, nb=8)
            S5O, bS5O = C.rotbuf("S5O", 1, [128, 4, TT], BF16)
            for c in range(2):
                def ev(j, ps, bps, c=c):
                    m = 2 * c + j
                    SG, bSG = C.rotbuf("SG", 2, [128, TT], F32)
                    actf(C, SG[:, :], ps[:, :], AF.Sigmoid, [bps], [bSG])
                    tt(C, "dve", S5O[:, m, :], GY[:, m, :], SG[:, :], ALU.mult, [bGY, bSG], [bS5O])
                proj_fm(C, ("wglu", c), 2, 4, lambda kt: GY[:, kt, :], [bGY], ev)
            resid_proj(C, "abo", 4, 8, lambda kt: S5O[:, kt, :] if kt < 4 else GO[:, kt - 4, :],
                       lambda kt: [bS5O, bGO], X, bX, (C.G, 16, H, bH))
            emit_ffn(C, "f20", X, bX, H, bH, nxt=(C.G, 24, H, bH))
            emit_ffn(C, "f11", X, bX, H, bH, nxt=(C.G, 32, H, bH))
            P.dma("pool", x4v[:, :, sl], X[:, :, :], reads=bX, slot=f"xout{i2}")
            TB, bTB = C.rotbuf("rtab", 1, [128, 6, TT], F32)
            RT = [bTB, C.bROT, bPOSI]
            P.op("pool", lambda: nc.gpsimd.iota(POSI[:, :], [[1, TT]], base=t * TT, channel_multiplier=0),
                 reads=[bPOSI], writes=[bPOSI])
            cp(C, "dve", TB[:, 0, :], POSI[:, :], RT, [bTB])
            ts(C, "dve", TB[:, 0, :], TB[:, 0, :], C.ROT[:, 0:1], None, ALU.mult, None, RT, [bTB])
            emit_sin(C, TB[:, 3, :], TB[:, 0, :], 0.0, TB[:, 1, :], TI[:, :], RT, [bTB])
            emit_sin(C, TB[:, 2, :], TB[:, 0, :], math.pi / 2, TB[:, 1, :], TI[:, :], RT, [bTB])
            ts(C, "dve", TB[:, 3, :], TB[:, 3, :], C.ROT[:, 1:2], None, ALU.mult, None, RT, [bTB])
            ksc = 128.0 ** -0.5
            ts(C, "dve", TB[:, 4, :], TB[:, 2, :], ksc, None, ALU.mult, None, RT, [bTB])
            ts(C, "dve", TB[:, 5, :], TB[:, 3, :], ksc, None, ALU.mult, None, RT, [bTB])
            hnat = lambda kt: H[:, kt, :]
            for h in range(8):
                def ev(j, ps, bps, h=h):
                    nsw = C.nsw = getattr(C, "nsw", 0) + 1
                    r4 = nsw % 4
                    QF, bQF = C.rotbuf("rQF", 3, [128, TT], F32)
                    QS, bQS = C.rotbuf("rQS", 3, [128, TT], F32, nb=2)
                    T1, bT1 = C.rotbuf("rT1", 2, [128, TT], F32)
                    T2, bT2 = C.rotbuf("rT2", 2, [128, TT], F32)
                    O_, bO_ = C.rotbuf("rOQ", 4, [128, TT], BF16)
                    cp(C, "act", QF[:, :], ps[:, :], [bps], [bQF])
                    P.dma("act", QS[0:64, :], QF[64:128, :], reads=[bQF], writes=[bQS[0]], slot=f"rsw{2 * r4}")
                    P.dma("act", QS[64:128, :], QF[0:64, :], reads=[bQF], writes=[bQS[1]], slot=f"rsw{2 * r4 + 1}")
                    base = 2 if j == 0 else 4
                    tt(C, "dve", T1[:, :], QF[:, :], TB[:, base, :], ALU.mult, [bQF, bTB], [bT1])
                    tt(C, "pool" if j == 0 else "dve", T2[:, :], QS[:, :], TB[:, base + 1, :], ALU.mult, bQS + [bTB], [bT2])
                    tt(C, "dve", O_[:, :], T1[:, :], T2[:, :], ALU.add, [bT1, bT2], [bO_])
                    dst = (RQ_d if j == 0 else RK_d)[h, :, sl]
                    P.dma("pool", dst, O_[:, :], reads=[bO_], slot=f"p3q{h % 2}" if j == 0 else f"p3k{h % 2}")
                proj_fm(C, ("rqk", h), 2, 8, hnat, lambda kt: [bH[kt]], ev)
            for c in range(8):
                OGS, bOGS = C.rotbuf("rOGS", 2, [128, 2, TT], BF16)

                def ev(j, ps, bps, c=c, OGS=OGS, bOGS=bOGS):
                    idx = 2 * c + j
                    SG, bSG = C.rotbuf("SG", 2, [128, TT], F32)
                    actf(C, SG[:, :], ps[:, :], AF.Silu, [bps], [bSG])
                    ts(C, "dve", OGS[:, j, :], SG[:, :], C.RN[:, idx:idx + 1], None, ALU.mult, None, [bSG, C.bGN], [bOGS])
                proj_fm(C, ("rog", c), 2, 8, hnat, lambda kt: [bH[kt]], ev)
                P.dma("pool", ROG_d[2 * c:2 * c + 2, :, sl].rearrange("a p t -> p a t"), OGS[:, :, :], reads=[bOGS],
                      slot=f"p3g{c % 2}")
            for c in range(4):
                VS, bVS = C.rotbuf("rVS", 1, [128, 4, 512], BF16)
                proj_tm(C, ("rv", c), H, bH, 8,
                        lambda i, ps, bps, VS=VS, bVS=bVS: cp(C, "act" if i % 2 == 0 else "dve", VS[:, i, :], ps[:, :],
                                                              [bps], [bVS]), resident=(RVW[:, c, :], bRVW[c]))
                P.dma("pool", RV_d[sl, c * 512:(c + 1) * 512].rearrange("(c p) f -> p c f", p=128), VS[:, :, :],
                      reads=[bVS], slot=f"p3v{c % 2}")
        P.barrier()
    C.pstack = None


def phase5(C, X4_d, RO_d, outT_d):
    P, nc = C.P, C.nc
    x4v = X4_d[:, :].rearrange("(k p) t -> p k t", p=128)
    ov = outT_d[:, :].rearrange("(k p) t -> p k t", p=128)
    for i in range(2):
        for nm in ("xin", "xout", "p5o"):
            P.slot(f"{nm}{i}")
    for t in range(C.NT):
        C.W.schedule(p5_keys())
    with ExitStack() as st:
        C.pstack = st
        C.rot = {}

        RW = P.sb([128, 4, 4096], BF16, stack=st, name="reto_res")
        bRW = [Buf(f"reto{c}") for c in range(4)]
        for c in range(4):
            off, n = C.pk.chunks[("reto", c)]
            P.dma("sp", RW[:, c, 0:n], bass.AP(C.W.wbf, off, [[n, 128], [1, n]]), writes=[bRW[c]],
                  slot=P.slot(f"s5st{c}"))

        def load_in(t):
            sl = slice(t * TT, (t + 1) * TT)
            X, bX = C.rotbuf("X", 2, [128, 8, TT], F32, nb=8)
            RO, bRO = C.rotbuf("RO", 2, [128, 16, TT], BF16)
            P.dma("sp", X[:, :, :], x4v[:, :, sl], writes=bX, slot=f"xin{t % 2}")
            P.dma("sp", RO[:, :, :], RO_d[:, :, sl].rearrange("h p t -> p h t"), writes=[bRO], slot=f"p5o{t % 2}")
            return (X, bX), (RO, bRO)
        for t in range(C.NT):
            sl = slice(t * TT, (t + 1) * TT)
            i2 = t % 2
            if t == 0:
                nxt_in = load_in(0)
            (X, bX), (RO, bRO) = nxt_in
            if t + 1 < C.NT:
                nxt_in = load_in(t + 1)
            H, bH = C.rotbuf("H", 1, [128, 8, TT], BF16, nb=8)
            resid_proj(C, "reto", 4, 16, lambda kt: RO[:, kt, :], lambda kt: [bRO], X, bX, (C.G, 40, H, bH),
                       resident=(RW, bRW))
            HO, bHO = C.rotbuf("HO", 2, [128, 8, TT], F32, nb=8)
            emit_ffn(C, "f21", X, bX, H, bH, nxt=(C.G, 48, HO, bHO))
            P.dma("pool", ov[:, :, sl], HO[:, :, :], reads=bHO, slot=f"xout{i2}")
        P.barrier()
    C.pstack = None


def full_pack(inp, meta):
    pk = WPack()
    def ffn(tag, a, b):
        if meta:
            pack_ffn(pk, tag, None, None, True)
        else:
            pack_ffn(pk, tag, inp[a + "_w1"][b], inp[a + "_w2"][b])
    ffn("f10", "ffn1", 0)
    pack_p1(pk, inp, meta)
    pk.first = pk.off
    ffn("f20", "ffn2", 0)
    ffn("f11", "ffn1", 1)
    pack_p3(pk, inp, meta)
    pack_p5(pk, inp, meta)
    ffn("f21", "ffn2", 1)
    return pk


def build_full(L, dbg_outputs=()):
    pk = full_pack(None, True)
    nc = bass.Bass("TRN2", target_bir_lowering=False)

    def dram(name, shape, dt, kind="Internal"):
        if name in dbg_outputs:
            kind = "ExternalOutput"
        return nc.dram_tensor(name, list(shape), dt, kind=kind)
    xT = dram("xT", [D, L], F32, "ExternalInput")
    nrm = dram("nrm", [128, 56], F32, "ExternalInput")
    gn = dram("gn", [128, 20], F32, "ExternalInput")
    wf32 = dram("wf32", [pk.off], F32, "ExternalInput")
    s5p = dram("s5p", [128, S5P_COLS], F32, "ExternalInput")
    gkw = dram("gkw", [32, 512], F32, "ExternalInput")
    gkb = dram("gkb", [128, 4], F32, "ExternalInput")
    outT = dram("outT", [D, L], F32, "ExternalOutput")
    wbf = dram("wbf", [pk.off], BF16)
    X1 = dram("X1", [D, L], F32)
    X4 = dram("X4", [D, L], F32)
    UT = dram("UT", [8, 128, L], BF16)
    GQ = dram("GQ", [2, 128, L], BF16)
    GK = dram("GK", [2, 128, L], BF16)
    GV = dram("GV", [L, 512], BF16)
    GLO = dram("GLO", [32, L], BF16)
    GOG = dram("GOG", [4, 128, L], BF16)
    GY = dram("GY", [4, 128, L], BF16)
    GO = dram("GO", [4, 128, L], BF16)
    RQ = dram("RQ", [8, 128, L], BF16)
    RK = dram("RK", [8, 128, L], BF16)
    RV = dram("RV", [L, 2048], BF16)
    ROG = dram("ROG", [16, 128, L], BF16)
    SBD = dram("SBD", [L // RC, 128, 8, 256], BF16)
    RO = dram("RO", [16, 128, L], BF16)
    WIN_d, WOUT_d, TOEP_d = s5_dram(nc)
    with ExitStack() as st:
        P = Prog(nc, st)
        C = Ctx(nc, P, L, pk)
        setup_common(C, nrm, gn)
        rotary_setup(C)
        castA = cast_gen(C, wf32, wbf, 0, pk.first, "A")
        for _ in range(4):
            next(castA, None)
        C.W = WStream(C, wbf)
        s5_setup(C, s5p, WIN_d, WOUT_d, TOEP_d, bg=castA)
        castB = cast_gen(C, wf32, wbf, pk.first, pk.off, "B")
        phase1(C, xT, X1, UT, GQ, GK, GV, GLO, GOG)
        NCH = L // 16
        per = max(4, -(-NCH // (2 * (L // GC))))
        def both():
            k = 0
            while True:
                if k % per == 0:
                    next(castB, None)
                k += 1
                yield

        def mid(g):
            def merged():
                b = both()
                for _ in g:
                    next(b)
                    yield
            gla_main(C, GQ, GK, GV, GLO, GOG, gkw, gkb, GO, bg=merged(), bg_per_chunk=per)
        s5_main(C, UT, WIN_d, WOUT_d, TOEP_d, GY, mid=mid)
        for _ in castB:
            pass
        P.barrier()
        phase3(C, X1, GY, GO, X4, RQ, RK, RV, ROG)
        ret_main(C, RQ, RK, RV, ROG, SBD, RO)
        phase5(C, X4, RO, outT)
        P.barrier()
        C.ninst = P.ninst
    return nc, pk


def host_inputs(inp):
    pk = full_pack(inp, False)
    wimg = pk.image()
    names = ["ffn1_norm", "mix_norm", "ffn2_norm"]
    nrm = np.zeros((128, 56), np.float32)
    col = 0
    for layer in range(2):
        order = [inp["ffn1_norm"][layer], inp["mix_norm"][layer], inp["ffn2_norm"][layer]]
        if layer == 1:
            pass
        for j, g in enumerate(order):
            pass
    cols = [inp["ffn1_norm"][0], inp["mix_norm"][0], inp["ffn2_norm"][0],
            inp["ffn1_norm"][1], inp["mix_norm"][1], inp["ffn2_norm"][1], inp["final_norm"]]
    for i, g in enumerate(cols):
        nrm[:, 8 * i:8 * i + 8] = np.asarray(g, np.float32).reshape(8, 128).T
    gn = np.zeros((128, 20), np.float32)
    gn[:, 0:4] = inp["gla_norm"][0].reshape(4, 128).T
    gn[:, 4:20] = inp["ret_norm"][0].reshape(16, 128).T
    gkw, gkb = gla_pack_params(inp)
    return dict(nrm=nrm, gn=gn, wf32=wimg, s5p=s5_pack_params(inp), gkw=gkw, gkb=gkb)


_CACHE = {}


def kernel(**inputs):
    inp = {k: np.asarray(v) for k, v in inputs.items()}
    x = inp["x"]
    B, L, _ = x.shape
    shared = host_inputs(inp)
    if L not in _CACHE:
        _CACHE[L] = build_full(L)
    nc, pk = _CACHE[L]
    in_maps = []
    for b in range(B):
        m = dict(shared)
        m["xT"] = np.ascontiguousarray(x[b].T)
        in_maps.append(m)
    res = run_bass_kernel_spmd(nc, in_maps, core_ids=list(range(B)))
    out = np.stack([np.ascontiguousarray(r["outT"].T) for r in res.results], axis=0)
    return out.astype(np.float32)
```

```python
import math
from contextlib import ExitStack
import numpy as np
import concourse.bass as bass
import concourse.mybir as mybir
from concourse.bass_utils import run_bass_kernel_spmd

F32 = mybir.dt.float32
BF16 = mybir.dt.bfloat16
ALU = mybir.AluOpType
AF = mybir.ActivationFunctionType

D = 1024
DFF = 2816
NJT = DFF // 128
EPS = 1e-6
TT = 512


class Buf:
    __slots__ = ("name", "w", "r")

    def __init__(self, name=""):
        self.name = name
        self.w = None
        self.r = {}


class Eng:
    def __init__(self, name, h, sem):
        self.name = name
        self.h = h
        self.sem = sem
        self.cnt = 0
        self.pending = False
        self.seen = {}


class Prog:
    def __init__(self, nc, stack):
        self.nc = nc
        self.stack = stack
        self.sems = {}
        self.dmaval = {}
        self.E = {}
        for name, h in (("pe", nc.tensor), ("act", nc.scalar), ("dve", nc.vector),
                        ("pool", nc.gpsimd), ("sp", nc.sync)):
            sem = stack.enter_context(nc.semaphore("sem_" + name))
            self.sems[name] = sem
            self.E[name] = Eng(name, h, sem)
        self.nuid = 0
        self.ninst = 0

    def sb(self, shape, dtype, name=None, stack=None):
        self.nuid += 1
        return (stack or self.stack).enter_context(
            self.nc.sbuf_tensor(name or f"sb{self.nuid}", list(shape), dtype))

    def slot(self, name):
        if name not in self.sems:
            self.sems[name] = self.stack.enter_context(self.nc.semaphore("dq_" + name))
            self.dmaval[name] = 0
        return name

    def _deps(self, reads, writes):
        deps = {}

        def add(k, v):
            if deps.get(k, 0) < v:
                deps[k] = v
        for b in reads:
            if b.w is not None:
                add(*b.w)
        for b in writes:
            if b.w is not None:
                add(*b.w)
            for k, v in b.r.items():
                add(k, v)
        return deps

    def _wait(self, e, deps, skip_self=False):
        for k, v in deps.items():
            if skip_self and k == e.name:
                continue
            if e.seen.get(k, 0) >= v:
                continue
            e.h.wait_ge(self.sems[k], v)
            e.seen[k] = v

    def op(self, eng, fn, reads=(), writes=(), signal=True):
        e = self.E[eng]
        self._wait(e, self._deps(reads, writes), skip_self=(eng == "pe"))
        ins = fn()
        self.ninst += 1
        if signal:
            e.cnt += 1
            ins.then_inc(e.sem, 1)
            e.pending = False
            val = e.cnt
        else:
            e.pending = True
            val = e.cnt + 1
        for b in writes:
            b.w = (eng, val)
            b.r = {}
        for b in reads:
            if b.r.get(eng, 0) < val:
                b.r[eng] = val
        return ins

    def dma(self, q, out, in_, reads=(), writes=(), slot=None, **kw):
        e = self.E[q]
        deps = self._deps(reads, writes)
        prev = self.dmaval[slot]
        if prev and deps.get(slot, 0) < prev:
            deps[slot] = prev
        self._wait(e, deps)
        ins = e.h.dma_start(out=out, in_=in_, **kw)
        self.ninst += 1
        self.dmaval[slot] += 16
        val = self.dmaval[slot]
        ins.then_inc(self.sems[slot], 16)
        for b in writes:
            b.w = (slot, val)
            b.r = {}
        for b in reads:
            if b.r.get(slot, 0) < val:
                b.r[slot] = val
        return ins

    def barrier(self):
        targets = {}
        for name, e in self.E.items():
            assert not e.pending, f"{name} has unsignalled instructions at barrier"
            if e.cnt:
                targets[name] = e.cnt
        for s, v in self.dmaval.items():
            if v and s not in getattr(self, "bar_exclude", ()):
                targets[s] = v
        for name, e in self.E.items():
            self._wait(e, {k: v for k, v in targets.items() if k != name})


class WPack:
    def __init__(self):
        self.chunks = {}
        self.arrays = []
        self.off = 0

    def add(self, key, arr):
        arr = np.ascontiguousarray(arr, dtype=np.float32)
        assert arr.shape[0] == 128
        n = arr.size // 128
        self.chunks[key] = (self.off, n)
        self.arrays.append(arr.reshape(-1))
        self.off += arr.size

    def add_meta(self, key, n):
        self.chunks[key] = (self.off, n)
        self.off += 128 * n

    def image(self):
        return np.concatenate(self.arrays)


def kt_split(w):
    k, n = w.shape
    return w.reshape(k // 128, 128, n).transpose(1, 0, 2)


def pack_ffn(pk, tag, w1, w2, meta_only=False):
    for f in range(NJT):
        key = (tag, "w1", f)
        if meta_only:
            pk.add_meta(key, 2 * 8 * 128)
            continue
        g = kt_split(w1[:, f * 128:(f + 1) * 128])
        u = kt_split(w1[:, DFF + f * 128:DFF + (f + 1) * 128])
        pk.add(key, np.stack([g, u], axis=1))
    for m in range(8):
        key = (tag, "w2", m)
        if meta_only:
            pk.add_meta(key, NJT * 128)
            continue
        pk.add(key, kt_split(w2[:, m * 128:(m + 1) * 128]))


def pack_fm(pk, tag, w, ncols_chunk, meta_only=False):
    k, n = w.shape
    assert n % ncols_chunk == 0
    for c in range(n // ncols_chunk):
        key = (tag, c)
        if meta_only:
            pk.add_meta(key, (k // 128) * ncols_chunk)
        else:
            pk.add(key, kt_split(w[:, c * ncols_chunk:(c + 1) * ncols_chunk]))


class Shape:
    def __init__(self, *s):
        self.shape = s


def build_pack(inp, meta_only):
    pk = WPack()
    g = (lambda k: inp[k]) if not meta_only else None
    for i, (tag, a, b) in enumerate([("f10", "ffn1", 0), ("f20", "ffn2", 0), ("f11", "ffn1", 1), ("f21", "ffn2", 1)]):
        if meta_only:
            pack_ffn(pk, tag, None, None, True)
        else:
            pack_ffn(pk, tag, g(a + "_w1")[b], g(a + "_w2")[b])
    return pk


class Ctx:
    def __init__(self, nc, P, L, pk):
        self.nc = nc
        self.P = P
        self.L = L
        self.NT = L // TT
        self.pk = pk
        self.ps = []
        self.psb = []
        for i in range(8):
            self.ps.append(P.stack.enter_context(nc.psum_tensor(f"psb{i}", [128, 512], F32)))
            self.psb.append(Buf(f"ps{i}"))
        self.psi = 0
        self.rot = {}

    def psum(self):
        i = self.psi
        self.psi = (self.psi + 1) % 7
        return self.ps[i], self.psb[i]

    def psum_stat(self):
        return self.ps[7], self.psb[7]

    def psum_fixed(self, i):
        return self.ps[i], self.psb[i]

    def psum_pool(self, name, banks):
        if not hasattr(self, "_pools"):
            self._pools = {}
        st = self._pools.setdefault(name, [0])
        i = banks[st[0] % len(banks)]
        st[0] += 1
        return self.ps[i], self.psb[i]

    def rotbuf(self, key, n, shape, dtype, nb=None):
        if key not in self.rot:
            self.nrot = getattr(self, "nrot", 0) + 1
            stk = getattr(self, "pstack", None)
            mk = (lambda i: Buf(f"{key}{i}")) if nb is None else (lambda i: [Buf(f"{key}{i}_{j}") for j in range(nb)])
            self.rot[key] = [[(self.P.sb(shape, dtype, name=f"{key}{i}_{self.nrot}", stack=stk), mk(i))
                              for i in range(n)], 0]
        lst, i = self.rot[key]
        self.rot[key][1] = (i + 1) % n
        return lst[i]


class WStream:
    def __init__(self, C, wbf, nslots=3, slot_elems=4096, q="sp", name="w"):
        P = C.P
        self.C = C
        self.wbf = wbf
        self.q = q
        self.ns = nslots
        self.tiles = [P.sb([128, slot_elems], BF16, name=f"{name}slot{i}") for i in range(nslots)]
        self.bufs = [Buf(f"{name}slot{i}") for i in range(nslots)]
        self.slots = [P.slot(f"{name}{i}") for i in range(nslots)]
        self.queue = []
        self.issued = 0
        self.consumed = 0

    def schedule(self, keys):
        self.queue.extend(keys)

    def _issue(self):
        P = self.C.P
        while self.issued < min(self.consumed + self.ns, len(self.queue)):
            i = self.issued
            off, n = self.C.pk.chunks[self.queue[i]]
            s = i % self.ns
            src = bass.AP(self.wbf, off, [[n, 128], [1, n]])
            P.dma(self.q, self.tiles[s][:, 0:n], src, writes=[self.bufs[s]], slot=self.slots[s])
            self.issued += 1

    def get(self, key):
        assert self.queue[self.consumed] == key, (self.queue[self.consumed], key)
        bg = getattr(self, "bg", None)
        if bg is not None and self.consumed % self.bg_every == 0:
            if next(bg, "done") == "done":
                self.bg = None
        self._issue()
        i = self.consumed
        self.consumed += 1
        s = i % self.ns
        return self.tiles[s], self.bufs[s]


def mm(C, out, lhsT, rhs, start, stop, reads, writes, signal=None):
    if signal is None:
        signal = stop
    return C.P.op("pe", lambda: C.nc.tensor.matmul(out, lhsT, rhs, start=start, stop=stop),
                  reads=reads, writes=writes, signal=signal)


class RmsState:
    pass


def rms_begin(C):
    st = RmsState()
    st.SQ, st.bSQ = C.rotbuf("rms_sq", 1, [128, 8, TT], BF16, nb=8)
    st.R, st.bR = C.rotbuf("rms_r", 2, [128, TT], F32)
    st.ps, st.bps = C.psum_stat()
    return st


def rms_square(C, st, X, bX, m):
    C.P.op("act", lambda: C.nc.scalar.activation(st.SQ[:, m, :], X[:, m, :], AF.Square), reads=[bX[m]], writes=[st.bSQ[m]])


def rms_stat(C, st, m):
    mm(C, st.ps[:, :], C.ones[:, :], st.SQ[:, m, :], m == 0, m == 7, [st.bSQ[m], C.b_ones], [st.bps])


def rms_finish(C, st, X, bX, G, gcol, H, bH, perm=None):
    P, nc = C.P, C.nc
    R, bR = st.R, st.bR
    P.op("act", lambda: nc.scalar.activation(R[:, :], st.ps[:, :], AF.Ln, bias=EPS, scale=1.0 / D),
         reads=[st.bps], writes=[bR])
    P.op("act", lambda: nc.scalar.activation(R[:, :], R[:, :], AF.Exp, scale=-0.5), reads=[bR], writes=[bR])
    for kt in range(8):
        P.op("dve", lambda kt=kt: nc.vector.scalar_tensor_tensor(
            H[:, kt, :], X[:, kt, :], G[:, gcol + kt:gcol + kt + 1], R[:, :], ALU.mult, ALU.mult),
            reads=[bX[kt], bR, C.bG], writes=[bH[kt]])
        if perm is not None:
            Hp, bHp = perm
            P.op("act", lambda kt=kt: nc.scalar.copy(
                Hp[:, kt, :].rearrange("p (s c) -> p c s", c=32), H[:, kt, :].rearrange("p (c s) -> p c s", s=16)),
                reads=[bH[kt]], writes=[bHp[kt]])


def emit_rmsnorm(C, X, bX, G, gcol, H, bH):
    st = rms_begin(C)
    for m in range(8):
        rms_square(C, st, X, bX, m)
        rms_stat(C, st, m)
    rms_finish(C, st, X, bX, G, gcol, H, bH)


def ffn_keys(tag):
    return [(tag, "w1", f) for f in range(NJT)] + [(tag, "w2", m) for m in range(8)]


def emit_ffn(C, tag, X, bX, H, bH, nxt=None):
    P, nc = C.P, C.nc
    A, bA = C.rotbuf("ffn_a", 1, [128, NJT, TT], BF16, nb=NJT)
    for f in range(NJT):
        wt, bw = C.W.get((tag, "w1", f))
        wv = wt[:, 0:2048].rearrange("p (g k c) -> p g k c", g=2, k=8)
        pg, bpg = C.psum()
        pu, bpu = C.psum()
        for kt in range(8):
            mm(C, pg[:, :], wv[:, 0, kt, :], H[:, kt, :], kt == 0, kt == 7, [bw, bH[kt]], [bpg])
        for kt in range(8):
            mm(C, pu[:, :], wv[:, 1, kt, :], H[:, kt, :], kt == 0, kt == 7, [bw, bH[kt]], [bpu])
        S, bS = C.rotbuf("ffn_s", 2, [128, TT], F32)
        P.op("act", lambda: nc.scalar.activation(S[:, :], pg[:, :], AF.Silu), reads=[bpg], writes=[bS])
        P.op("dve", lambda: nc.vector.tensor_tensor(A[:, f, :], S[:, :], pu[:, :], ALU.mult),
             reads=[bS, bpu], writes=[bA[f]])
    st = rms_begin(C) if nxt is not None else None
    for m in range(8):
        wt, bw = C.W.get((tag, "w2", m))
        wv = wt[:, 0:NJT * 128].rearrange("p (j c) -> p j c", j=NJT)
        py, bpy = C.psum()
        for j in range(NJT):
            mm(C, py[:, :], wv[:, j, :], A[:, j, :], j == 0, j == NJT - 1, [bw, bA[j]], [bpy])
        if st is not None and m >= 1:
            rms_stat(C, st, m - 1)
        P.op("dve", lambda: nc.vector.scalar_tensor_tensor(X[:, m, :], py[:, :], 0.5, X[:, m, :], ALU.mult, ALU.add),
             reads=[bpy, bX[m]], writes=[bX[m]])
        if st is not None:
            rms_square(C, st, X, bX, m)
    if st is not None:
        rms_stat(C, st, 7)
        G, gcol, Hn, bHn = nxt[:4]
        rms_finish(C, st, X, bX, G, gcol, Hn, bHn, perm=(nxt[4] if len(nxt) > 4 else None))


def cast_gen(C, wf32, wbf, lo, hi, tag):
    P = C.P
    CH = 128 * 4096
    sl = [P.slot(f"cast{tag}{i}") for i in range(4)]
    off = lo
    i = 0
    while off < hi:
        n = min(CH, hi - off)
        assert n % 128 == 0
        src = bass.AP(wf32, off, [[n // 128, 128], [1, n // 128]])
        dst = bass.AP(wbf, off, [[n // 128, 128], [1, n // 128]])
        P.dma("pool", dst, src, slot=sl[i % 4])
        off += n
        i += 1
        yield


def emit_cast_weights(C, wf32, wbf, lo, hi, tag):
    for _ in cast_gen(C, wf32, wbf, lo, hi, tag):
        pass


def setup_common(C, nrm_d, gn_d=None):
    P, nc = C.P, C.nc
    C.ones = P.sb([128, 128], BF16, name="ones")
    C.b_ones = Buf("ones")
    P.op("pool", lambda: nc.gpsimd.memset(C.ones[:, :], 1.0), writes=[C.b_ones])
    C.G = P.sb([128, 56], F32, name="gains")
    C.bG = Buf("gains")
    P.slot("misc")
    P.dma("sp", C.G[:, :], nrm_d[:, :], writes=[C.bG], slot="misc")
    C.pstack = None
    if gn_d is not None:
        C.GNT = P.sb([128, 20], F32, name="gnt")
        C.bGN = Buf("gn")
        P.dma("sp", C.GNT[:, :], gn_d[:, :], writes=[C.bGN], slot=P.slot("misc3"))
        C.GN = C.GNT[:, 0:4]
        C.RN = C.GNT[:, 4:20]


def tt(C, eng, out, a, b, op, reads, writes):
    h = C.P.E[eng].h
    return C.P.op(eng, lambda: h.tensor_tensor(out, a, b, op), reads=reads, writes=writes)


def ts(C, eng, out, a, s1, s2, op0, op1, reads, writes):
    h = C.P.E[eng].h
    if s2 is None:
        return C.P.op(eng, lambda: h.tensor_scalar(out, a, s1, None, op0), reads=reads, writes=writes)
    return C.P.op(eng, lambda: h.tensor_scalar(out, a, s1, s2, op0, op1), reads=reads, writes=writes)


def cp(C, eng, out, a, reads, writes):
    if eng == "act":
        return C.P.op("act", lambda: C.nc.scalar.copy(out, a), reads=reads, writes=writes)
    h = C.P.E[eng].h
    return C.P.op(eng, lambda: h.tensor_copy(out, a), reads=reads, writes=writes)


def actf(C, out, a, func, reads, writes, bias=0.0, scale=1.0):
    return C.P.op("act", lambda: C.nc.scalar.activation(out, a, func, bias=bias, scale=scale),
                  reads=reads, writes=writes)


def mset(C, eng, ap, val, writes):
    h = C.P.E[eng].h
    return C.P.op(eng, lambda: h.memset(ap, val), writes=writes)


def emit_sin(C, out, x, shift, tmp, tmpi, reads, writes, eng="dve"):
    P = C.P
    h = P.E[eng].h
    PI = math.pi
    ts(C, eng, tmp, x, shift, 1.0 / (2 * PI), ALU.add, ALU.mult, reads, writes)
    cp(C, eng, tmpi, tmp, reads, writes)
    cp(C, eng, tmp, tmpi, reads, writes)
    C1 = 6.28125
    C2 = 2 * PI - C1
    P.op(eng, lambda: h.scalar_tensor_tensor(out, tmp, -C1, x, ALU.mult, ALU.add), reads=reads, writes=writes)
    P.op(eng, lambda: h.scalar_tensor_tensor(tmp, tmp, -C2, out, ALU.mult, ALU.add), reads=reads, writes=writes)
    if shift != 0.0:
        ts(C, eng, tmp, tmp, shift, None, ALU.add, None, reads, writes)
    ts(C, eng, out, tmp, PI, 2 * PI, ALU.is_gt, ALU.mult, reads, writes)
    tt(C, eng, tmp, tmp, out, ALU.subtract, reads, writes)
    ts(C, eng, out, tmp, -PI, 2 * PI, ALU.is_lt, ALU.mult, reads, writes)
    tt(C, eng, tmp, tmp, out, ALU.add, reads, writes)
    ts(C, eng, tmp, tmp, PI, -PI, ALU.min, ALU.max, reads, writes)
    actf(C, out, tmp, AF.Sin, reads, writes)


S5P_COLS = 64 * 3 + 1024 * 4 + 8
T1 = 16


def s5_pack_params(inp):
    def nmaj(a):
        return np.transpose(a, (2, 0, 1)).reshape(64, 64)
    lamr = nmaj(inp["s5_lambda_re"][0])
    lami = nmaj(inp["s5_lambda_im"][0])
    ldt = np.broadcast_to(inp["s5_log_dt"][0].reshape(1, 64), (64, 64))
    br = np.transpose(inp["s5_b_re"][0], (2, 0, 1, 3)).reshape(64, 1024)
    bi = np.transpose(inp["s5_b_im"][0], (2, 0, 1, 3)).reshape(64, 1024)
    cr = np.transpose(inp["s5_c_re"][0], (3, 0, 1, 2)).reshape(64, 1024)
    ci = np.transpose(inp["s5_c_im"][0], (3, 0, 1, 2)).reshape(64, 1024)
    top = np.concatenate([lamr, lami, ldt, br, bi, cr, ci], axis=1)
    top = np.concatenate([top, top], axis=0)
    dp = np.zeros((128, 8), np.float32)
    dsk = inp["s5_d"][0].reshape(8, 4, 16)
    for gl in range(4):
        dp[32 * gl:32 * gl + 16, :] = dsk[:, gl, :].T
    return np.ascontiguousarray(np.concatenate([top, dp], axis=1), dtype=np.float32)


def s5_setup(C, s5p_d, WIN_d, WOUT_d, TOEP_d, bg=None):
    P, nc = C.P, C.nc
    C.s5AR = P.sb([128, 16, 2, 2], F32, name="s5AR")
    C.s5AI = P.sb([128, 16, 2, 2], F32, name="s5AI")
    C.bs5A = Buf("s5A")
    with ExitStack() as st:
        def sb(shape, dt=F32):
            return P.sb(shape, dt, stack=st), Buf()
        PR, bPR = sb([128, S5P_COLS])
        P.dma("sp", PR[:, :], s5p_d[:, :], writes=[bPR], slot="misc")
        lamr = PR[:, 0:64]; lami = PR[:, 64:128]; ldt = PR[:, 128:192]
        Br = PR[:, 192:1216].rearrange("p (g c) -> p g c", c=16)
        Bi = PR[:, 1216:2240].rearrange("p (g c) -> p g c", c=16)
        Cr = PR[:, 2240:3264].rearrange("p (g c) -> p g c", c=16)
        Ci = PR[:, 3264:4288].rearrange("p (g c) -> p g c", c=16)
        dpad = PR[:, 4288:4296]
        T, bT = sb([128, 12, 64])
        lr = T[:, 0, :]; dt = T[:, 1, :]; mag = T[:, 2, :]; th = T[:, 3, :]
        ar = T[:, 4, :]; ai = T[:, 5, :]; den = T[:, 6, :]; crr = T[:, 7, :]; cii = T[:, 8, :]
        t0 = T[:, 9, :]; t1 = T[:, 10, :]; t2 = T[:, 11, :]
        R, W = [bPR, bT], [bT]
        ts(C, "dve", lr, lamr, -1e-4, None, ALU.min, None, R, W)
        actf(C, dt, ldt, AF.Exp, R, W)
        tt(C, "dve", t0, lr, dt, ALU.mult, R, W)
        actf(C, mag, t0, AF.Exp, R, W)
        tt(C, "dve", th, lami, dt, ALU.mult, R, W)
        TI, bTI = sb([128, 64], mybir.dt.int32)
        emit_sin(C, t1, th, 0.0, t0, TI[:, :], R + [bTI], W + [bTI])
        emit_sin(C, t2, th, math.pi / 2, t0, TI[:, :], R + [bTI], W + [bTI])
        tt(C, "dve", ar, mag, t2, ALU.mult, R, W)
        tt(C, "dve", ai, mag, t1, ALU.mult, R, W)
        tt(C, "dve", den, lr, lr, ALU.mult, R, W)
        tt(C, "dve", t0, lami, lami, ALU.mult, R, W)
        tt(C, "dve", den, den, t0, ALU.add, R, W)
        P.op("dve", lambda: nc.vector.reciprocal(den, den), reads=R, writes=W)
        ts(C, "dve", t0, ar, -1.0, None, ALU.add, None, R, W)
        tt(C, "dve", t1, t0, lr, ALU.mult, R, W)
        tt(C, "dve", t2, ai, lami, ALU.mult, R, W)
        tt(C, "dve", t1, t1, t2, ALU.add, R, W)
        tt(C, "dve", crr, t1, den, ALU.mult, R, W)
        tt(C, "dve", t1, ai, lr, ALU.mult, R, W)
        tt(C, "dve", t2, t0, lami, ALU.mult, R, W)
        tt(C, "dve", t1, t1, t2, ALU.subtract, R, W)
        tt(C, "dve", cii, t1, den, ALU.mult, R, W)
        BB, bBB = sb([128, 2, 64, 16])
        TB, bTB = sb([128, 2, 64, 16])

        def bc(x):
            return x.unsqueeze(2).broadcast_to([128, 64, 16])
        R2 = [bPR, bT, bBB, bTB]
        tt(C, "dve", BB[:, 0], Br, bc(crr), ALU.mult, R2, [bBB])
        tt(C, "dve", TB[:, 0], Bi, bc(cii), ALU.mult, R2, [bTB])
        tt(C, "dve", BB[:, 0], BB[:, 0], TB[:, 0], ALU.subtract, R2, [bBB])
        tt(C, "dve", BB[:, 1], Bi, bc(crr), ALU.mult, R2, [bBB])
        tt(C, "dve", TB[:, 1], Br, bc(cii), ALU.mult, R2, [bTB])
        tt(C, "dve", BB[:, 1], BB[:, 1], TB[:, 1], ALU.add, R2, [bBB])
        PW, bPW = sb([128, 2, 17, 64])
        mset(C, "dve", PW[:, 0, 0, :], 1.0, [bPW])
        mset(C, "dve", PW[:, 1, 0, :], 0.0, [bPW])
        R3 = [bPW, bT]
        for e in range(16):
            pr, pi = PW[:, 0, e, :], PW[:, 1, e, :]
            tt(C, "dve", t0, pr, ar, ALU.mult, R3, [bT])
            tt(C, "dve", t1, pi, ai, ALU.mult, R3, [bT])
            tt(C, "dve", PW[:, 0, e + 1, :], t0, t1, ALU.subtract, R3, [bPW])
            tt(C, "dve", t0, pr, ai, ALU.mult, R3, [bT])
            tt(C, "dve", t1, pi, ar, ALU.mult, R3, [bT])
            tt(C, "dve", PW[:, 1, e + 1, :], t0, t1, ALU.add, R3, [bPW])
        for hh in range(2):
            rows = slice(64 * hh, 64 * hh + 64)
            for d in range(2):
                srcr = PW[rows, 0, 16, d * 32:(d + 1) * 32].rearrange("p (m h) -> p m h", h=2)[:, :, hh]
                srci = PW[rows, 1, 16, d * 32:(d + 1) * 32].rearrange("p (m h) -> p m h", h=2)[:, :, hh]
                for ri in range(2):
                    cp(C, "dve", C.s5AR[rows, :, d, ri], srcr, [bPW], [C.bs5A])
                ts(C, "dve", C.s5AI[rows, :, d, 0], srci, -1.0, None, ALU.mult, None, [bPW], [C.bs5A])
                cp(C, "dve", C.s5AI[rows, :, d, 1], srci, [bPW], [C.bs5A])
        ID, bID = sb([128, 128])
        mset(C, "pool", ID[:, :], 1.0, [bID])
        P.op("pool", lambda: nc.gpsimd.affine_select(
            out=ID[:, :], in_=ID[:, :], pattern=[[-1, 128]], compare_op=ALU.is_equal, fill=0.0,
            base=0, channel_multiplier=1), reads=[bID], writes=[bID])
        IDb, bIDb = sb([128, 128], BF16)
        cp(C, "dve", IDb[:, :], ID[:, :], [bID], [bIDb])
        for s_ in range(4):
            P.slot(f"s5st{s_}")
        X1, bX1 = sb([128, 64, 16])
        X2, bX2 = sb([128, 64, 16])
        BPm, bBPm = sb([128, 2, 64, 128], BF16)
        mset(C, "pool", BPm[:, :, :, :], 0.0, [bBPm])
        for ri in range(2):
            for gl in range(4):
                dst = BPm[:, ri, :, 32 * gl:32 * gl + 16].rearrange("p (a b) c -> p a b c", b=4)[:, :, gl, :]
                src = BB[:, ri].rearrange("p (a b) c -> p a b c", b=4)[:, :, gl, :]
                cp(C, "dve", dst, src, [bBB], [bBPm])
        TST = [sb([128, 4, 128], BF16) for _ in range(2)]
        for t_, b_ in TST:
            mset(C, "pool", t_[:, :, :], 0.0, [b_])
        WO = [sb([128, 16, 2, 128], BF16) for _ in range(2)]
        for t_, b_ in WO:
            mset(C, "pool", t_[:, :, :, :], 0.0, [b_])
        nst = 0
        kwo = 0
        CAe = [sb([128, 2, 64, 16], BF16) for _ in range(2)]
        X3, bX3 = sb([128, 64, 16])
        X4, bX4 = sb([128, 64, 16])
        def tick():
            if bg is not None:
                next(bg, None)
        for e in range(17):
            tick()
            ca, bca = CAe[e % 2]
            pr, pi = bc(PW[:, 0, e, :]), bc(PW[:, 1, e, :])
            tt(C, "dve", X1[:, :, :], Cr, pr, ALU.mult, [bPR, bPW, bX1], [bX1])
            tt(C, "dve", X2[:, :, :], Ci, pi, ALU.mult, [bPR, bPW, bX2], [bX2])
            tt(C, "dve", ca[:, 0], X1[:, :, :], X2[:, :, :], ALU.subtract, [bX1, bX2], [bca])
            tt(C, "pool", X3[:, :, :], Cr, pi, ALU.mult, [bPR, bPW, bX3], [bX3])
            tt(C, "pool", X4[:, :, :], Ci, pr, ALU.mult, [bPR, bPW, bX4], [bX4])
            P.op("dve", lambda: nc.vector.scalar_tensor_tensor(ca[:, 1], X3[:, :, :], -1.0, X4[:, :, :],
                                                              ALU.mult, ALU.subtract), reads=[bX3, bX4], writes=[bca])
            if e < 16:
                for d in range(2):
                    for half in range(2):
                        ps, bps = C.psum()
                        for bq in range(4):
                            blk = half * 4 + bq
                            for gl in range(4):
                                g = d * 32 + blk * 4 + gl
                                for ri in range(2):
                                    mm(C, ps[:, bq * 128 + gl * 32: bq * 128 + gl * 32 + 16],
                                       BPm[0:64, ri, g, :], ca[0:64, ri, g, :], ri == 0, ri == 1, [bBPm, bca], [bps])
                        stg, bstg = TST[nst % 2]
                        nst += 1
                        pv = ps[:, :].rearrange("p (a b c) -> p a b c", a=4, b=4)[:, :, :, 0:16]
                        sv = stg[:, :, :].rearrange("p a (b c) -> p a b c", b=4)[:, :, :, 0:16]
                        cp(C, "act", sv, pv, [bps], [bstg])
                        if d == 0 and e == 0:
                            for bq in range(4):
                                blk = half * 4 + bq
                                P.op("dve", lambda: nc.vector.scalar_tensor_tensor(
                                    stg[:, bq, :], ID[:, :], dpad[:, blk:blk + 1], stg[:, bq, :],
                                    ALU.mult, ALU.add), reads=[bID, bPR, bstg], writes=[bstg])
                        dst = TOEP_d[half * 4:half * 4 + 4, :, d, e, :].rearrange("b p c -> p b c")
                        P.dma("sp", dst, stg[:, :, :], reads=[bstg], slot=f"s5st{nst % 4}")
            if e >= 1:
                for d in range(2):
                    s = (e - 1) if d == 0 else (16 - e)
                    wo, bwo = WO[kwo % 2]
                    kwo += 1
                    for hh in range(2):
                        rows = slice(64 * hh, 64 * hh + 64)
                        for ri in range(2):
                            src = ca[rows, ri, d * 32:(d + 1) * 32, :]
                            src = src.rearrange("p (q mp h) c -> p q mp h c", mp=2, h=2)
                            for mp in range(2):
                                co = 64 * mp + 32 * hh
                                dstv = wo[rows, :, ri, co:co + 16].rearrange("p (q mp) c -> p q mp c", mp=2)[:, :, mp, :]
                                cp(C, "dve" if ri == 0 else "pool", dstv, src[:, :, mp, hh, :], [bca], [bwo])
                    for r_ in range(2):
                        dst = WOUT_d[:, :, d, s, r_, :].rearrange("m p c -> p m c")
                        P.dma("sp", dst, wo[:, :, r_, :], reads=[bwo], slot=f"s5st{(2 * kwo + r_) % 4}")
        WP, bWP = sb([128, 2, 64, 32], BF16)
        mset(C, "pool", WP[:, :, :, :], 0.0, [bWP])
        WST = [sb([128, 8, 2, 128], BF16) for _ in range(2)]
        for t_, b_ in WST:
            mset(C, "pool", t_[:, :, :, :], 0.0, [b_])
        nw = 0
        for e in range(16):
            tick()
            pr, pi = bc(PW[:, 0, e, :]), bc(PW[:, 1, e, :])
            tt(C, "dve", X1[:, :, :], BB[:, 0], pr, ALU.mult, [bBB, bPW, bX1], [bX1])
            tt(C, "dve", X2[:, :, :], BB[:, 1], pi, ALU.mult, [bBB, bPW, bX2], [bX2])
            tt(C, "dve", WP[:, 0, :, 0:16], X1[:, :, :], X2[:, :, :], ALU.subtract, [bX1, bX2], [bWP])
            tt(C, "pool", X3[:, :, :], BB[:, 1], pr, ALU.mult, [bBB, bPW, bX3], [bX3])
            tt(C, "dve", X4[:, :, :], BB[:, 0], pi, ALU.mult, [bBB, bPW, bX4], [bX4])
            tt(C, "dve", WP[:, 1, :, 0:16], X3[:, :, :], X4[:, :, :], ALU.add, [bX3, bX4], [bWP])
            for d in range(2):
                s = (15 - e) if d == 0 else e
                for ri in range(2):
                    ps, bps = C.psum()
                    for blk in range(8):
                        g0 = d * 32 + blk * 4
                        lhsT = WP[0:64, ri, g0:g0 + 4, :].rearrange("p a b -> p (a b)")
                        mm(C, ps[:, blk * 64:(blk + 1) * 64], lhsT, IDb[0:64, 0:64], True, True, [bWP, bIDb], [bps])
                    stg, bstg = WST[nw % 2]
                    nw += 1
                    pv = ps[:, :].rearrange("p (b n) -> p b n", n=64)
                    for gl in range(4):
                        r = slice(32 * gl, 32 * gl + 32)
                        cp(C, "act" if gl % 2 == 0 else "dve",
                           stg[r, :, gl // 2, (gl % 2) * 64:(gl % 2) * 64 + 64], pv[r, :, :], [bps], [bstg])
                    for a_ in range(2):
                        dst = WIN_d[:, :, a_, d, s, ri, :].rearrange("b p c -> p b c")
                        P.dma("sp", dst, stg[:, :, a_, :], reads=[bstg], slot=f"s5st{(2 * nw + a_) % 4}")
        if bg is not None:
            for _ in bg:
                pass
        P.barrier()


def s5_main(C, UT_d, WIN_d, WOUT_d, TOEP_d, GY_d, mid=None):
    P, nc = C.P, C.nc
    L = C.L
    NB = L // 512
    NCH = L // 16
    for i in range(2):
        P.slot(f"s5u{i}"); P.slot(f"s5w{i}"); P.slot(f"s5o{i}"); P.slot(f"s5tp{i}"); P.slot(f"s5wo{i}")
    with ExitStack() as st0:
        XC = P.sb([128, 16, 2, 2, NCH], BF16, stack=st0)
        bXC = Buf("XC")
        bXS = Buf("XS")
        with ExitStack() as st:
            U = [(P.sb([128, L], BF16, stack=st), Buf()) for _ in range(2)]
            Wn = [(P.sb([128, 2, 2, 16, 2, 128], BF16, stack=st), Buf()) for _ in range(1)]
            nev = 0
            for blk in range(8):
                u, bu = U[blk % 2]
                w, bw = Wn[0]
                P.dma("sp", u[:, :], UT_d[blk, :, :], writes=[bu], slot=f"s5u{blk % 2}")
                P.dma("sp", w[:, :, :, :, :, :], WIN_d[blk], writes=[bw], slot="s5w0")
                uv = u[:, :].rearrange("p (b s c) -> p b s c", s=16, c=32)
                for pair in range(2):
                    for d in range(2):
                        for ri in range(2):
                            ps, bps = C.psum()
                            ov = ps[:, 0:NCH].rearrange("p (b c) -> p b c", c=32)
                            for s in range(16):
                                mm(C, ov, w[:, pair, d, s, ri, :], uv[:, :, s, :], s == 0, s == 15, [bu, bw], [bps])
                            cp(C, "act" if nev % 2 == 0 else "dve", XC[:, 2 * blk + pair, d, ri, :], ps[:, 0:NCH],
                               [bps], [bXC])
                            nev += 1
            P.barrier()
        S = [(P.sb([128, 16, 2, 2], F32, stack=st0), Buf()) for _ in range(3)]
        TA = [(P.sb([128, 16, 2, 2], F32, stack=st0), Buf()) for _ in range(2)]
        TB = [(P.sb([128, 16, 2, 2], F32, stack=st0), Buf()) for _ in range(2)]

        def scan_gen():
            mset(C, "pool", S[0][0][:, :, :, :], 0.0, [S[0][1]])
            tot = 16 * 2 * 2 * NCH
            for i in range(NCH):
                cur, bcur = S[i % 3]
                nxt, bnxt = S[(i + 1) % 3]
                ta, bta = TA[i % 2]
                tb, btb = TB[i % 2]
                sw = bass.AP(cur, 1, [[64, 128], [2, 32], [-1, 2]])
                cf, cb = i, NCH - 1 - i
                xsel = bass.AP(XC, cf, [[tot, 128], [4 * NCH, 16], [2 * NCH + (cb - cf), 2], [NCH, 2]])
                tt(C, "pool", ta[:, :, :, :], cur[:, :, :, :], C.s5AR[:, :, :, :], ALU.mult, [bcur, C.bs5A], [bta])
                tt(C, "pool", tb[:, :, :, :].rearrange("p a b c -> p (a b) c"), sw,
                   C.s5AI[:, :, :, :].rearrange("p a b c -> p (a b) c"), ALU.mult, [bcur, C.bs5A], [btb])
                tt(C, "pool", ta[:, :, :, :], ta[:, :, :, :], tb[:, :, :, :], ALU.add, [bta, btb], [bta])
                tt(C, "pool", nxt[:, :, :, :], ta[:, :, :, :], xsel, ALU.add, [bta, bXC], [bnxt])
                cp(C, "pool", xsel, cur[:, :, :, :], [bcur, bnxt], [bXS])
                yield

        g = scan_gen()
        if mid is not None:
            mid(g)
        for _ in g:
            pass
        P.barrier()
        with ExitStack() as st:
            U = [(P.sb([128, L], BF16, stack=st), Buf()) for _ in range(2)]
            TP = [(P.sb([128, 2, 16, 128], BF16, stack=st), Buf()) for _ in range(1)]
            WOt = [(P.sb([128, 2, 16, 2, 128], BF16, stack=st), Buf()) for _ in range(2)]
            YI = (P.sb([128, L], F32, stack=st), Buf())
            YS = [(P.sb([128, 512], F32, stack=st), Buf()) for _ in range(2)]
            GS = [(P.sb([128, 512], BF16, stack=st), Buf()) for _ in range(2)]
            nev = 0
            for blk in range(8):
                u, bu = U[blk % 2]
                tp, btp = TP[0]
                yi, byi = YI
                P.dma("sp", u[:, :], UT_d[blk, :, :], writes=[bu], slot=f"s5u{blk % 2}")
                P.dma("sp", tp[:, :, :, :], TOEP_d[blk], writes=[btp], slot="s5tp0")
                for h in range(2):
                    P.dma("sp", WOt[h][0][:, :, :, :, :], WOUT_d[2 * blk + h], writes=[WOt[h][1]], slot=f"s5wo{h}")
                yv = yi[:, :].rearrange("p (b s c) -> p b s c", s=16, c=32)
                for s in range(16):
                    ps, bps = C.psum()
                    k = 0
                    for h in range(2):
                        for d in range(2):
                            for ri in range(2):
                                mm(C, ps[:, 0:NCH], WOt[h][0][:, d, s, ri, :], XC[:, 2 * blk + h, d, ri, :],
                                   k == 0, k == 7, [WOt[h][1], bXS], [bps])
                                k += 1
                    cp(C, "act" if s % 2 == 0 else "dve", yv[:, :, s, :],
                       ps[:, 0:NCH].rearrange("p (b c) -> p b c", c=32), [bps], [byi])
                for bank in range(NB):
                    ps, bps = C.psum()
                    o = bank * 512
                    for tau in range(16):
                        n = (16 - tau) * 32
                        mm(C, ps[:, tau * 32:512], tp[:, 0, tau, :], u[:, o:o + n], tau == 0, False, [btp, bu], [bps],
                           signal=False)
                    for tau in range(16):
                        n = (16 - tau) * 32
                        mm(C, ps[:, 0:n], tp[:, 1, tau, :], u[:, o + tau * 32:o + 512], False, tau == 15, [btp, bu], [bps])
                    ys, bys = YS[nev % 2]
                    gs, bgs = GS[nev % 2]
                    tt(C, "dve", ys[:, :], ps[:, :], yi[:, o:o + 512], ALU.add, [bps, byi], [bys])
                    actf(C, gs[:, :].rearrange("p (c s) -> p s c", s=16), ys[:, :].rearrange("p (s c) -> p s c", c=32),
                         AF.Gelu, [bys], [bgs])
                    for gl in range(4):
                        r0 = (blk % 2) * 64 + gl * 16
                        P.dma("pool", GY_d[blk // 2, r0:r0 + 16, o:o + 512], gs[32 * gl:32 * gl + 16, :], reads=[bgs],
                              slot=P.slot(f"s5st{gl}" if nev % 2 == 0 else f"castA{gl}"))
                    nev += 1
            P.barrier()
    P.barrier()


def s5_dram(nc, kind="Internal"):
    WIN_d = nc.dram_tensor("s5WIN", [8, 128, 2, 2, 16, 2, 128], BF16, kind=kind)
    WOUT_d = nc.dram_tensor("s5WOUT", [16, 128, 2, 16, 2, 128], BF16, kind=kind)
    TOEP_d = nc.dram_tensor("s5TOEP", [8, 128, 2, 16, 128], BF16, kind=kind)
    return WIN_d, WOUT_d, TOEP_d


def build_test_s5(L):
    pk = build_pack(None, True)
    nc = bass.Bass("TRN2", target_bir_lowering=False)
    s5p = nc.dram_tensor("s5p", [128, S5P_COLS], F32, kind="ExternalInput")
    UT = nc.dram_tensor("UT", [8, 128, L], BF16, kind="ExternalInput")
    GY = nc.dram_tensor("GY", [8, 128, L], BF16, kind="ExternalOutput")
    WIN_d, WOUT_d, TOEP_d = s5_dram(nc, "ExternalOutput")
    with ExitStack() as st:
        P = Prog(nc, st)
        C = Ctx(nc, P, L, pk)
        P.slot("misc")
        s5_setup(C, s5p, WIN_d, WOUT_d, TOEP_d)
        s5_main(C, UT, WIN_d, WOUT_d, TOEP_d, GY)
        print("instructions:", P.ninst)
    return nc


GC = 128


def gla_pack_params(inp):
    wg = np.zeros((32, 2, 256), np.float32)
    for s in range(2):
        wg[16 * s:16 * s + 16, s, :] = inp["gla_w_gk"][0, s]
    bg = np.ascontiguousarray(inp["gla_b_gk"][0].reshape(2, 2, 128).transpose(2, 0, 1), dtype=np.float32)
    return wg.reshape(32, 512), bg.reshape(128, 4)


def gla_main(C, GQ_d, GK_d, GV_d, GLO_d, GOG_d, gkw_d, gkb_d, GO_d, bg=None, bg_per_chunk=4):
    P, nc = C.P, C.nc
    L = C.L
    NT = L // TT
    NCg = L // GC
    CPT = TT // GC
    for i in range(2):
        for nm in ("gq", "gk", "gv", "gl", "gg", "go"):
            P.slot(f"{nm}{i}")
    with ExitStack() as st:
        def sb(shape, dt=F32, name=None):
            return P.sb(shape, dt, stack=st), Buf(name or "")
        WG32, bWG32 = sb([32, 512])
        WG, bWG = sb([32, 2, 2, 128], BF16)
        BG, bBG = sb([128, 4])
        P.dma("sp", WG32[:, :], gkw_d[:, :], writes=[bWG32], slot=P.slot("misc1"))
        P.dma("sp", BG[:, :], gkb_d[:, :], writes=[bBG], slot=P.slot("misc2"))
        cp(C, "dve", WG[:, :, :, :].rearrange("p a b c -> p (a b c)"), WG32[:, :], [bWG32], [bWG])
        ts(C, "dve", BG[:, :], BG[:, :], -1.0, None, ALU.mult, None, [bBG], [bBG])
        MK, bMK = sb([128, 256])
        mset(C, "pool", MK[:, :], 1.0, [bMK])
        P.op("pool", lambda: nc.gpsimd.affine_select(out=MK[:, 0:128], in_=MK[:, 0:128], pattern=[[1, 128]],
                                                     compare_op=ALU.is_ge, fill=0.0, base=0, channel_multiplier=-1),
             reads=[bMK], writes=[bMK])
        P.op("pool", lambda: nc.gpsimd.affine_select(out=MK[:, 128:256], in_=MK[:, 128:256], pattern=[[-1, 128]],
                                                     compare_op=ALU.is_gt, fill=0.0, base=0, channel_multiplier=1),
             reads=[bMK], writes=[bMK])
        MSf, bMSf = sb([128, TT])
        MSb, bMSb = sb([128, TT])
        mset(C, "pool", MSf[:, :], 1.0, [bMSf])
        mset(C, "pool", MSb[:, :], 1.0, [bMSb])
        mset(C, "pool", MSf[:, :].rearrange("p (c s) -> p c s", s=GC)[:, :, 0:1], 0.0, [bMSf])
        mset(C, "pool", MSb[:, :].rearrange("p (c s) -> p c s", s=GC)[:, :, GC - 1:GC], 0.0, [bMSb])
        IDb, bIDb = sb([128, 128], BF16)
        ID, bID = sb([128, 128])
        mset(C, "pool", ID[:, :], 1.0, [bID])
        P.op("pool", lambda: nc.gpsimd.affine_select(out=ID[:, :], in_=ID[:, :], pattern=[[-1, 128]],
                                                     compare_op=ALU.is_equal, fill=0.0, base=0, channel_multiplier=1),
             reads=[bID], writes=[bID])
        cp(C, "dve", IDb[:, :], ID[:, :], [bID], [bIDb])
        SBs, bSBs = sb([128, NCg, 2, 128], BF16, "SBs")
        def run_bg(n=1):
            if bg is not None:
                for _ in range(n):
                    next(bg, None)
        Sst = [sb([128, 2, 128]) for _ in range(2)]
        Sbf = [sb([128, 2, 128], BF16) for _ in range(2)]
        rot = {}

        def rb(key, n, shape, dt=F32):
            if key not in rot:
                rot[key] = [[sb(shape, dt, key) for _ in range(n)], 0]
            lst, i = rot[key]
            rot[key][1] = (i + 1) % n
            return lst[i]

        def rev(t, n):
            return bass.AP(t, n - 1, [[n, 128], [-1, n]])

        def gates(t, d, GL, bGL, Q, bQ, Kt, bK, getps=None):
            getps = getps or C.psum
            QD, bQD = rb("QD", 4, [128, 2, TT], BF16)
            KI, bKI = rb("KI", 4, [128, 2, TT], BF16)
            KE, bKE = rb("KE", 2, [128, 2, TT], BF16)
            DEC, bDEC = rb("DEC", 2, [128, 2, CPT])
            for th in range(2):
                ps, bps = getps()
                mm(C, ps[:, :], WG[:, d, th, :], GL[:, :], True, True, [bWG, bGL], [bps])
                E, bE = rb("gE", 1, [128, TT])
                Lg, bLg = E, bE
                Cm, bCm = rb("gC", 1, [128, TT])
                E1, bE1 = rb("gE1", 2, [128, TT])
                E2, bE2 = Cm, bCm
                col = d * 2 + th
                P.op("act", lambda: nc.scalar.activation(E[:, :], ps[:, :], AF.Exp, bias=BG[:, col:col + 1], scale=-1.0),
                     reads=[bps, bBG], writes=[bE])
                actf(C, Lg[:, :], E[:, :], AF.Ln, [bE], [bLg], bias=1.0)
                if d == 0:
                    P.op("dve", lambda: nc.vector.tensor_tensor_scan(Cm[:, :], MSf[:, :], Lg[:, :], 0.0, ALU.mult, ALU.add),
                         reads=[bMSf, bLg], writes=[bCm])
                else:
                    P.op("dve", lambda: nc.vector.tensor_tensor_scan(rev(Cm, TT), rev(MSb, TT), rev(Lg, TT), 0.0,
                                                                     ALU.mult, ALU.add),
                         reads=[bMSb, bLg], writes=[bCm])
                actf(C, E1[:, :], Cm[:, :], AF.Exp, [bCm], [bE1], scale=-1.0 / 16)
                actf(C, E2[:, :], Cm[:, :], AF.Exp, [bCm], [bE2], scale=1.0 / 16)
                P.op("dve", lambda: nc.vector.scalar_tensor_tensor(QD[:, th, :], Q[:, th, :], 0.125, E1[:, :],
                                                                  ALU.mult, ALU.mult), reads=[bQ, bE1], writes=[bQD])
                tt(C, "dve", KI[:, th, :], Kt[:, th, :], E2[:, :], ALU.mult, [bK, bE2], [bKI])
                e1c = E1[:, :].rearrange("p (c s) -> p c s", s=GC)
                dcol = e1c[:, :, GC - 1] if d == 0 else e1c[:, :, 0]
                cp(C, "dve", DEC[:, th, :], dcol, [bE1], [bDEC])
                tt(C, "dve", KE[:, th, :].rearrange("p (c s) -> p c s", s=GC),
                   KI[:, th, :].rearrange("p (c s) -> p c s", s=GC),
                   DEC[:, th, :].unsqueeze(2).broadcast_to([128, CPT, GC]), ALU.mult, [bKI, bDEC], [bKE])
            return (QD, bQD), (KI, bKI), (KE, bKE), (DEC, bDEC)

        def load_tile(t, want_q):
            i = t % 2
            sl = slice(t * TT, (t + 1) * TT)
            GL, bGL = rb("GL", 2, [32, TT], BF16)
            Kt, bK = rb("Kt", 2, [128, 2, TT], BF16)
            V, bV = rb("V", 2, [128, CPT, 512], BF16)
            Q, bQ = rb("Q", 2, [128, 2, TT], BF16)
            P.dma("sp", GL[:, :], GLO_d[:, sl], writes=[bGL], slot=f"gl{i}")
            P.dma("sp", Kt[:, :, :], GK_d[:, :, sl].rearrange("a p t -> p a t"), writes=[bK], slot=f"gk{i}")
            P.dma("sp", V[:, :, :], GV_d[sl, :].rearrange("(c p) f -> p c f", p=128), writes=[bV], slot=f"gv{i}")
            P.dma("sp", Q[:, :, :], GQ_d[:, :, sl].rearrange("a p t -> p a t"), writes=[bQ], slot=f"gq{i}")
            return (GL, bGL), (Kt, bK), (V, bV), (Q, bQ)

        def kv_update(d, c_in_tile, th, KE, bKE, V, bV, DEC, bDEC, getps=None):
            getps = getps or C.psum
            S, bS = Sst[d]
            cs = slice(c_in_tile * GC, (c_in_tile + 1) * GC)
            pt, bpt = getps()
            mm(C, pt[:, 0:128], KE[:, th, cs], IDb[:, :], True, True, [bKE, bIDb], [bpt])
            KEt, bKEt = rb("KEt", 2, [128, 128], BF16)
            cp(C, "act", KEt[:, :], pt[:, 0:128], [bpt], [bKEt])
            pk_, bpk = getps()
            for hh in range(2):
                h = th * 2 + hh
                mm(C, pk_[:, hh * 128:(hh + 1) * 128], KEt[:, :], V[:, c_in_tile, h * 128:(h + 1) * 128],
                   True, True, [bKEt, bV], [bpk], signal=(hh == 1))
            for hh in range(2):
                r = slice(64 * hh, 64 * hh + 64)
                P.op("dve", lambda: nc.vector.scalar_tensor_tensor(
                    S[r, th, :], S[r, th, :], DEC[r, th, c_in_tile:c_in_tile + 1], pk_[r, hh * 128:(hh + 1) * 128],
                    ALU.mult, ALU.add), reads=[bS, bDEC, bpk], writes=[bS])

        for d in range(2):
            mset(C, "dve", Sst[d][0][:, :, :], 0.0, [Sst[d][1]])
        dbg = getattr(C, "dbg", 9)
        for t in range(NT - 1, -1, -1):
            (GL, bGL), (Kt, bK), (V, bV), (Q, bQ) = load_tile(t, True)
            (QD, bQD), (KI, bKI), (KE, bKE), (DEC, bDEC) = gates(t, 1, GL, bGL, Q, bQ, Kt, bK)
            for ci in range(CPT - 1, -1, -1):
                if dbg < 2:
                    break
                c = t * CPT + ci
                cp(C, "act", SBs[:, c, :, :], Sst[1][0][:, :, :], [Sst[1][1]], [bSBs])
                for th in range(2):
                    kv_update(1, ci, th, KE, bKE, V, bV, DEC, bDEC)
                    run_bg(max(1, bg_per_chunk // 4))
        getR = lambda: C.psum_pool("glaR", [2, 3, 4, 5, 6])
        pos = [C.psum_fixed(0), C.psum_fixed(1)]

        def post(c, OG, bOG, cs):
            gs = slice(c * GC, (c + 1) * GC)
            GO, bGO = rb("GO", 2, [128, 4, GC], BF16)
            for hh in range(2):
                po, bpo = pos[hh]
                SQ, bSQ = rb("SQ", 2, [128, 256], BF16)
                actf(C, SQ[:, :], po[:, 0:256], AF.Square, [bpo], [bSQ])
                pst, bpst = C.psum_stat()
                mm(C, pst[:, 0:256], C.ones[:, :], SQ[:, :], True, True, [bSQ, C.b_ones], [bpst])
                RS, bRS = rb("RS", 2, [128, 256])
                actf(C, RS[:, :], pst[:, 0:256], AF.Ln, [bpst], [bRS], bias=EPS, scale=1.0 / 128)
                actf(C, RS[:, :], RS[:, :], AF.Exp, [bRS], [bRS], scale=-0.5)
                ON, bON = rb("ON", 2, [128, 256])
                tt(C, "dve", ON[:, :], po[:, 0:256], RS[:, :], ALU.mult, [bpo, bRS], [bON])
                gov = GO[:, :, :].rearrange("p (a b) i -> p a b i", b=2)[:, :, hh, :]
                ogv = OG[:, :, cs].rearrange("p (a b) i -> p a b i", b=2)[:, :, hh, :]
                tt(C, "dve", gov, ON[:, :].rearrange("p (a i) -> p a i", i=GC), ogv, ALU.mult, [bON, bOG], [bGO])
            P.dma("pool", GO_d[:, :, gs].rearrange("h p t -> p h t"), GO[:, :, :], reads=[bGO], slot=f"go{c % 2}")

        prev = None
        for t in range(NT if dbg >= 3 else 0):
            (GL, bGL), (Kt, bK), (V, bV), (Q, bQ) = load_tile(t, True)
            (QDb, bQDb), (KIb, bKIb), _, _ = gates(t, 1, GL, bGL, Q, bQ, Kt, bK, getR)
            (QD, bQD), (KI, bKI), (KE, bKE), (DEC, bDEC) = gates(t, 0, GL, bGL, Q, bQ, Kt, bK, getR)
            OG, bOG = rb("OG", 2, [128, 4, TT], BF16)
            P.dma("sp", OG[:, :, :], GOG_d[:, :, t * TT:(t + 1) * TT].rearrange("h p t -> p h t"), writes=[bOG],
                  slot=f"gg{t % 2}")
            for ci in range(CPT):
                c = t * CPT + ci
                cs = slice(ci * GC, (ci + 1) * GC)
                gs = slice(c * GC, (c + 1) * GC)
                sf, bsf = Sbf[c % 2]
                cp(C, "act", sf[:, :, :], Sst[0][0][:, :, :], [Sst[0][1]], [bsf])
                for th in range(2):
                    kv_update(0, ci, th, KE, bKE, V, bV, DEC, bDEC, getR)
                    run_bg(max(1, bg_per_chunk // 4))
                if prev is not None:
                    post(*prev)
                run_bg(max(1, bg_per_chunk // 2))
                Ats = []
                for h in range(4):
                    th, hh = h // 2, h % 2
                    r = slice(64 * hh, 64 * hh + 64)
                    psc, bpsc = getR()
                    mm(C, psc[:, 0:128], KI[r, th, cs], QD[r, th, cs], True, True, [bKI, bQD], [bpsc], signal=False)
                    mm(C, psc[:, 128:256], KIb[r, th, cs], QDb[r, th, cs], True, True, [bKIb, bQDb], [bpsc])
                    At, bAt = rb("At", 4, [128, 256], BF16)
                    tt(C, "dve", At[:, :], psc[:, 0:256], MK[:, :], ALU.mult, [bpsc, bMK], [bAt])
                    Ats.append((At, bAt))
                for h in range(4):
                    th, hh = h // 2, h % 2
                    r = slice(64 * hh, 64 * hh + 64)
                    po, bpo = pos[hh]
                    At, bAt = Ats[h]
                    oh = po[:, th * 128:(th + 1) * 128]
                    vh = V[:, ci, h * 128:(h + 1) * 128]
                    mm(C, oh, vh, At[:, 0:128], True, False, [bV, bAt], [bpo], signal=False)
                    mm(C, oh, vh, At[:, 128:256], False, False, [bV, bAt], [bpo], signal=False)
                    mm(C, oh, SBs[r, c, th, :], QDb[r, th, cs], False, False, [bSBs, bQDb], [bpo], signal=False)
                    mm(C, oh, sf[r, th, :], QD[r, th, cs], False, True, [bsf, bQD], [bpo], signal=(th == 1))
                run_bg(max(1, bg_per_chunk // 2))
                prev = (c, OG, bOG, cs)
        if prev is not None:
            post(*prev)
        P.barrier()


def build_test_gla(L):
    pk = build_pack(None, True)
    nc = bass.Bass("TRN2", target_bir_lowering=False)
    GQ = nc.dram_tensor("GQ", [2, 128, L], BF16, kind="ExternalInput")
    GK = nc.dram_tensor("GK", [2, 128, L], BF16, kind="ExternalInput")
    GV = nc.dram_tensor("GV", [L, 512], BF16, kind="ExternalInput")
    GLO = nc.dram_tensor("GLO", [32, L], BF16, kind="ExternalInput")
    GOG = nc.dram_tensor("GOG", [4, 128, L], BF16, kind="ExternalInput")
    gkw = nc.dram_tensor("gkw", [32, 512], F32, kind="ExternalInput")
    gkb = nc.dram_tensor("gkb", [128, 4], F32, kind="ExternalInput")
    nrm = nc.dram_tensor("nrm", [128, 56], F32, kind="ExternalInput")
    GO = nc.dram_tensor("GO", [4, 128, L], BF16, kind="ExternalOutput")
    with ExitStack() as st:
        P = Prog(nc, st)
        C = Ctx(nc, P, L, pk)
        setup_common(C, nrm)
        gla_main(C, GQ, GK, GV, GLO, GOG, gkw, gkb, GO)
        print("instructions:", P.ninst)
    return nc


def proj_fm(C, key, ntiles, nk, rhs_fn, rhs_bufs, evac_fn, perm_out=False):
    wt, bw = C.W.get(key)
    wv = wt[:, 0:nk * ntiles * 128].rearrange("p (k c) -> p k c", k=nk)
    for j in range(ntiles):
        ps, bps = C.psum()
        out = ps[:, :].rearrange("p (s c) -> p s c", c=32) if perm_out else ps[:, :]
        for kt in range(nk):
            rb_ = rhs_bufs(kt) if callable(rhs_bufs) else rhs_bufs
            mm(C, out, wv[:, kt, j * 128:(j + 1) * 128], rhs_fn(kt), kt == 0, kt == nk - 1, [bw] + rb_, [bps])
        evac_fn(j, ps, bps)


def proj_tm(C, key, H, bH, nk, evac_fn):
    wt, bw = C.W.get(key)
    wv = wt[:, 0:nk * 512].rearrange("p (k c) -> p k c", k=nk)
    for i in range(TT // 128):
        ps, bps = C.psum()
        for kt in range(nk):
            mm(C, ps[:, :], H[:, kt, i * 128:(i + 1) * 128], wv[:, kt, :], kt == 0, kt == nk - 1, [bw, bH[kt]], [bps])
        evac_fn(i, ps, bps)


def resid_proj(C, tag, nchunks, nk, rhs_fn, rhs_bufs, X, bX, nxt, resident=None):
    P, nc = C.P, C.nc
    st = rms_begin(C)
    for c in range(nchunks):
        if resident is not None:
            wt, bw = resident[0][:, c, :], resident[1][c]
        else:
            wt, bw = C.W.get((tag, c))
        wv = wt[:, 0:nk * 256].rearrange("p (k c) -> p k c", k=nk)
        for j in range(2):
            m = 2 * c + j
            ps, bps = C.psum()
            for kt in range(nk):
                mm(C, ps[:, :], wv[:, kt, j * 128:(j + 1) * 128], rhs_fn(kt), kt == 0, kt == nk - 1,
                   [bw] + rhs_bufs(kt), [bps])
            if m >= 1:
                rms_stat(C, st, m - 1)
            tt(C, "dve", X[:, m, :], ps[:, :], X[:, m, :], ALU.add, [bps, bX[m]], [bX[m]])
            rms_square(C, st, X, bX, m)
    rms_stat(C, st, 7)
    G, gcol, Hn, bHn = nxt
    rms_finish(C, st, X, bX, G, gcol, Hn, bHn)


def p1_keys():
    ks = ffn_keys("f10")
    ks += [("abu", c) for c in range(4)] + [("abq", 0), ("abk", 0), ("abog", 0), ("abog", 1), ("abglo", 0), ("abv", 0)]
    return ks


def pack_p1(pk, inp, meta):
    if meta:
        pack_fm(pk, "abu", Shape(1024, 1024), 256, True)
        pack_fm(pk, "abq", Shape(1024, 256), 256, True)
        pack_fm(pk, "abk", Shape(1024, 256), 256, True)
        pack_fm(pk, "abog", Shape(1024, 512), 256, True)
        pk.add_meta(("abglo", 0), 8 * 128)
        pk.add_meta(("abv", 0), 8 * 512)
        return
    w = inp["ab_w_in"][0]
    wu = np.zeros((1024, 32, 32), np.float32)
    wu[:, :, 0:16] = w[:, 0:512].reshape(1024, 32, 16)
    pack_fm(pk, "abu", wu.reshape(1024, 1024), 256)
    pack_fm(pk, "abq", w[:, 512:768], 256)
    pack_fm(pk, "abk", w[:, 768:1024], 256)
    pack_fm(pk, "abog", w[:, 1536:2048], 256)
    wg = np.zeros((1024, 128), np.float32)
    wg[:, 0:32] = w[:, 2048:2080]
    pk.add(("abglo", 0), kt_split(wg))
    pk.add(("abv", 0), kt_split(w[:, 1024:1536]))


class Shape:
    def __init__(self, *s):
        self.shape = s


def pack_fm(pk, tag, w, ncols_chunk, meta_only=False):
    k, n = w.shape
    assert n % ncols_chunk == 0
    for c in range(n // ncols_chunk):
        key = (tag, c)
        if meta_only:
            pk.add_meta(key, (k // 128) * ncols_chunk)
        else:
            pk.add(key, kt_split(w[:, c * ncols_chunk:(c + 1) * ncols_chunk]))


def phase1(C, xT_d, X1_d, UT_d, GQ_d, GK_d, GV_d, GLO_d, GOG_d):
    P, nc = C.P, C.nc
    xv = xT_d[:, :].rearrange("(k p) t -> p k t", p=128)
    x1v = X1_d[:, :].rearrange("(k p) t -> p k t", p=128)
    for i in range(2):
        for nm in ("xin", "xout", "p1u", "p1q", "p1k", "p1g", "p1l", "p1v"):
            P.slot(f"{nm}{i}")
    for t in range(C.NT):
        C.W.schedule(p1_keys())
    with ExitStack() as st:
        C.pstack = st
        C.rot = {}
        def load_x(t):
            X, bX = C.rotbuf("X", 2, [128, 8, TT], F32, nb=8)
            P.dma("sp", X[:, :, :], xv[:, :, t * TT:(t + 1) * TT], writes=bX, slot=f"xin{t % 2}")
            return X, bX
        nxt_x = load_x(0)
        nxt_h = C.rotbuf("H", 2, [128, 8, TT], BF16, nb=8)
        emit_rmsnorm(C, nxt_x[0], nxt_x[1], C.G, 0, nxt_h[0], nxt_h[1])
        for t in range(C.NT):
            sl = slice(t * TT, (t + 1) * TT)
            i2 = t % 2
            X, bX = nxt_x
            H, bH = nxt_h
            if t + 1 < C.NT:
                nxt_x = load_x(t + 1)
            Hp, bHp = C.rotbuf("Hp", 1, [128, 8, TT], BF16, nb=8)
            emit_ffn(C, "f10", X, bX, H, bH, nxt=(C.G, 8, H, bH, (Hp, bHp)))
            if t + 1 < C.NT:
                nxt_h = C.rotbuf("H", 2, [128, 8, TT], BF16, nb=8)
                emit_rmsnorm(C, nxt_x[0], nxt_x[1], C.G, 0, nxt_h[0], nxt_h[1])
            P.dma("pool", x1v[:, :, sl], X[:, :, :], reads=bX, slot=f"xout{i2}")
            US, bUS = C.rotbuf("US", 2, [128, 8, TT], BF16)
            for c in range(4):
                def ev(j, ps, bps, c=c):
                    cp(C, "act" if j == 0 else "dve", US[:, 2 * c + j, :], ps[:, :], [bps], [bUS])
                proj_fm(C, ("abu", c), 2, 8, lambda kt: Hp[:, kt, :], lambda kt: [bHp[kt]], ev)
            P.dma("pool", UT_d[:, :, sl].rearrange("b p t -> p b t"), US[:, :, :], reads=[bUS], slot=f"p1u{i2}")
            hnat = lambda kt: H[:, kt, :]
            QS, bQS = C.rotbuf("QS", 2, [128, 2, TT], BF16)
            KS, bKS = C.rotbuf("KS", 2, [128, 2, TT], BF16)
            proj_fm(C, ("abq", 0), 2, 8, hnat, lambda kt: [bH[kt]],
                    lambda j, ps, bps: cp(C, "act" if j == 0 else "dve", QS[:, j, :], ps[:, :], [bps], [bQS]))
            P.dma("pool", GQ_d[:, :, sl].rearrange("a p t -> p a t"), QS[:, :, :], reads=[bQS], slot=f"p1q{i2}")
            proj_fm(C, ("abk", 0), 2, 8, hnat, lambda kt: [bH[kt]],
                    lambda j, ps, bps: cp(C, "act" if j == 0 else "dve", KS[:, j, :], ps[:, :], [bps], [bKS]))
            P.dma("pool", GK_d[:, :, sl].rearrange("a p t -> p a t"), KS[:, :, :], reads=[bKS], slot=f"p1k{i2}")
            OGS, bOGS = C.rotbuf("OGS", 2, [128, 4, TT], BF16)
            for c in range(2):
                def ev(j, ps, bps, c=c):
                    h = 2 * c + j
                    SG, bSG = C.rotbuf("SG", 2, [128, TT], F32)
                    actf(C, SG[:, :], ps[:, :], AF.Silu, [bps], [bSG])
                    ts(C, "dve", OGS[:, h, :], SG[:, :], C.GN[:, h:h + 1], None, ALU.mult, None, [bSG, C.bGN], [bOGS])
                proj_fm(C, ("abog", c), 2, 8, hnat, lambda kt: [bH[kt]], ev)
            P.dma("pool", GOG_d[:, :, sl].rearrange("a p t -> p a t"), OGS[:, :, :], reads=[bOGS], slot=f"p1g{i2}")
            LS, bLS = C.rotbuf("LS", 2, [32, TT], BF16)
            proj_fm(C, ("abglo", 0), 1, 8, hnat, lambda kt: [bH[kt]],
                    lambda j, ps, bps: cp(C, "act", LS[:, :], ps[0:32, :], [bps], [bLS]))
            P.dma("pool", GLO_d[:, sl], LS[:, :], reads=[bLS], slot=f"p1l{i2}")
            VS, bVS = C.rotbuf("VS", 2, [128, 4, 512], BF16)
            proj_tm(C, ("abv", 0), H, bH, 8,
                    lambda i, ps, bps: cp(C, "act" if i % 2 == 0 else "dve", VS[:, i, :], ps[:, :], [bps], [bVS]))
            P.dma("pool", GV_d[sl, :].rearrange("(c p) f -> p c f", p=128), VS[:, :, :], reads=[bVS], slot=f"p1v{i2}")
        if getattr(C.W, "bg", None) is not None:
            for _ in C.W.bg:
                pass
            C.W.bg = None
        P.barrier()
    C.pstack = None


RC = 128
LGF = [math.log1p(-2.0 ** (-5 - h)) for h in range(8)]
LGB = LGF[::-1]


def ret_main(C, RQ_d, RK_d, RV_d, ROG_d, SBD_d, RO_d):
    P, nc = C.P, C.nc
    L = C.L
    NCr = L // RC
    for i in range(2):
        for nm in ("rq", "rk", "rv", "rg", "rs", "ro"):
            P.slot(f"{nm}{i}")
    P.slot("rg2")
    with ExitStack() as st:
        def sb(shape, dt=F32, name=None):
            return P.sb(shape, dt, stack=st), Buf(name or "")
        rot = {}

        def rb(key, n, shape, dt=F32):
            if key not in rot:
                rot[key] = [[sb(shape, dt, key) for _ in range(n)], 0]
            lst, i = rot[key]
            rot[key][1] = (i + 1) % n
            return lst[i]
        ID, bID = sb([128, 128])
        IDb, bIDb = sb([128, 128], BF16)
        mset(C, "pool", ID[:, :], 1.0, [bID])
        P.op("pool", lambda: nc.gpsimd.affine_select(out=ID[:, :], in_=ID[:, :], pattern=[[-1, 128]],
                                                     compare_op=ALU.is_equal, fill=0.0, base=0, channel_multiplier=1),
             reads=[bID], writes=[bID])
        cp(C, "dve", IDb[:, :], ID[:, :], [bID], [bIDb])
        EI, bEI = sb([128, 128], mybir.dt.int32)
        E, bE = sb([128, 128])
        Ep, bEp = sb([128, 128])
        En, bEn = sb([128, 128])
        P.op("pool", lambda: nc.gpsimd.iota(EI[:, :], [[1, 128]], base=0, channel_multiplier=-1), writes=[bEI])
        cp(C, "dve", E[:, :], EI[:, :], [bEI], [bE])
        ts(C, "dve", Ep[:, :], E[:, :], 0.0, None, ALU.max, None, [bE], [bEp])
        ts(C, "dve", En[:, :], E[:, :], -1.0, 0.0, ALU.mult, ALU.max, [bE], [bEn])
        DT, bDT = sb([128, 8, 128])
        ARG, bARG = sb([128, 128])
        for h in range(8):
            ts(C, "dve", ARG[:, :], Ep[:, :], LGF[h], None, ALU.mult, None, [bEp], [bARG])
            P.op("dve", lambda: nc.vector.scalar_tensor_tensor(ARG[:, :], En[:, :], LGB[h], ARG[:, :], ALU.mult, ALU.add),
                 reads=[bEn, bARG], writes=[bARG])
            actf(C, DT[:, h, :], ARG[:, :], AF.Exp, [bARG], [bDT])
        IRI, bIRI = sb([128, 128], mybir.dt.int32)
        IR, bIR = sb([128, 128])
        P.op("pool", lambda: nc.gpsimd.iota(IRI[:, :], [[1, 128]], base=0, channel_multiplier=0), writes=[bIRI])
        cp(C, "dve", IR[:, :], IRI[:, :], [bIRI], [bIR])
        IPI, bIPI = sb([128, 1], mybir.dt.int32)
        IP, bIP = sb([128, 1])
        P.op("pool", lambda: nc.gpsimd.iota(IPI[:, :], [[0, 1]], base=0, channel_multiplier=1), writes=[bIPI])
        cp(C, "dve", IP[:, :], IPI[:, :], [bIPI], [bIP])
        XIf, bXIf = sb([128, 8, 128], BF16)
        XIb, bXIb = sb([128, 8, 128], BF16)
        ZF, bZF = sb([128, 8])
        ZB, bZB = sb([128, 8])
        for h in range(8):
            actf(C, XIf[:, h, :], IR[:, :], AF.Exp, [bIR], [bXIf], bias=LGF[h], scale=LGF[h])
            actf(C, XIb[:, h, :], IR[:, :], AF.Exp, [bIR], [bXIb], bias=RC * LGB[h], scale=-LGB[h])
            actf(C, ZF[:, h:h + 1], IP[:, :], AF.Exp, [bIP], [bZF], bias=(RC - 1) * LGF[h], scale=-LGF[h])
            actf(C, ZB[:, h:h + 1], IP[:, :], AF.Exp, [bIP], [bZB], scale=LGB[h])
        GF = [math.exp(RC * LGF[h]) for h in range(8)]
        GB = [math.exp(RC * LGB[h]) for h in range(8)]
        Sp = [sb([128, 8, 256], F32, "Sstate0")[0], sb([128, 8, 256], F32, "Sstate1")[0]]
        bSp = [[Buf() for _ in range(8)], [Buf() for _ in range(8)]]

        def load_kv(c):
            i = c % 2
            cs = slice(c * RC, (c + 1) * RC)
            Kt, bK = rb("Kt", 2, [128, 8, RC], BF16)
            V, bV = rb("V", 2, [128, 2048], BF16)
            P.dma("sp", Kt[:, :, :], RK_d[:, :, cs].rearrange("h p t -> p h t"), writes=[bK], slot=f"rk{i}")
            P.dma("sp", V[:, :], RV_d[cs, :], writes=[bV], slot=f"rv{i}")
            return Kt, bK, V, bV

        def kv_update(k, Kt, bK, V, bV, Z, bZ, GAM, getps=None):
            getps = getps or C.psum
            So, bSo = Sp[k % 2], bSp[k % 2]
            Sn, bSn = Sp[(k + 1) % 2], bSp[(k + 1) % 2]
            for g in range(2):
                pt, bpt = getps()
                for hl in range(4):
                    h = 4 * g + hl
                    mm(C, pt[:, hl * 128:(hl + 1) * 128], Kt[:, h, :], IDb[:, :], True, True, [bK, bIDb], [bpt],
                       signal=(hl == 3))
                Kz, bKz = rb("Kz", 2, [128, 4, 128], BF16)
                tt(C, "dve", Kz[:, :, :], pt[:, :].rearrange("p (a b) -> p a b", b=128),
                   Z[:, 4 * g:4 * g + 4].unsqueeze(2).broadcast_to([128, 4, 128]), ALU.mult, [bpt, bZ], [bKz])
                for hp in range(2):
                    pk_, bpk = getps()
                    for hq in range(2):
                        hl = 2 * hp + hq
                        h = 4 * g + hl
                        mm(C, pk_[:, hq * 256:(hq + 1) * 256], Kz[:, hl, :], V[:, h * 256:(h + 1) * 256], True, True,
                           [bKz, bV], [bpk], signal=(hq == 1))
                    for hq in range(2):
                        h = 4 * g + 2 * hp + hq
                        P.op("dve", lambda: nc.vector.scalar_tensor_tensor(
                            Sn[:, h, :], So[:, h, :], GAM[h], pk_[:, hq * 256:(hq + 1) * 256], ALU.mult, ALU.add),
                            reads=[bSo[h], bpk], writes=[bSn[h]])

        mset(C, "dve", Sp[0][:, :, :], 0.0, bSp[0])
        k = 0
        for c in range(NCr - 1, -1, -1):
            Kt, bK, V, bV = load_kv(c)
            Sb16, bSb16 = rb("Sb16", 2, [128, 8, 256], BF16)
            cp(C, "act", Sb16[:, :, :], Sp[k % 2][:, :, :], bSp[k % 2], [bSb16])
            P.dma("act", SBD_d[c], Sb16[:, :, :], reads=[bSb16], slot=f"rs{c % 2}")
            kv_update(k, Kt, bK, V, bV, ZB, bZB, GB)
            k += 1
        P.barrier()
        mset(C, "dve", Sp[k % 2][:, :, :], 0.0, bSp[k % 2])
        RP = [4, 5, 6]
        getR = lambda: C.psum_pool("retR", RP)
        OB = [[C.psum_fixed(0), C.psum_fixed(1)], [C.psum_fixed(2), C.psum_fixed(3)]]
        for c in range(NCr):
            i = c % 2
            cs = slice(c * RC, (c + 1) * RC)
            Kt, bK, V, bV = load_kv(c)
            Q, bQ = rb("Q", 2, [128, 8, RC], BF16)
            SB, bSB = rb("SB", 2, [128, 8, 256], BF16)
            OG, bOG = rb("OG", 2, [128, 16, RC], BF16)
            P.dma("sp", Q[:, :, :], RQ_d[:, :, cs].rearrange("h p t -> p h t"), writes=[bQ], slot=f"rq{i}")
            P.dma("sp", SB[:, :, :], SBD_d[c], writes=[bSB], slot=f"rs{i}")
            P.dma("sp", OG[:, :, :], ROG_d[:, :, cs].rearrange("h p t -> p h t"), writes=[bOG], slot=f"rg{i}")
            Sf16, bSf16 = rb("Sf16", 2, [128, 8, 256], BF16)
            cp(C, "act", Sf16[:, :, :], Sp[k % 2][:, :, :], bSp[k % 2], [bSf16])
            Qf, bQf = rb("Qf", 2, [128, 8, RC], BF16)
            Qb, bQb = rb("Qb", 2, [128, 8, RC], BF16)
            tt(C, "pool", Qf[:, :, :], Q[:, :, :], XIf[:, :, :], ALU.mult, [bQ, bXIf], [bQf])
            tt(C, "pool", Qb[:, :, :], Q[:, :, :], XIb[:, :, :], ALU.mult, [bQ, bXIb], [bQb])
            kv_update(k, Kt, bK, V, bV, ZF, bZF, GF, getR)
            k += 1
            At, bAt = rb("At", 2, [128, 8, RC], BF16, )
            bAtg = [Buf(), Buf()]
            for g in range(2):
                psc, bpsc = getR()
                for hl in range(4):
                    h = 4 * g + hl
                    mm(C, psc[:, hl * 128:(hl + 1) * 128], Kt[:, h, :], Q[:, h, :], True, True, [bK, bQ], [bpsc],
                       signal=(hl == 3))
                tt(C, "dve", At[:, 4 * g:4 * g + 4, :], psc[:, :].rearrange("p (a b) -> p a b", b=128),
                   DT[:, 4 * g:4 * g + 4, :], ALU.mult, [bpsc, bDT, bAt], [bAtg[g]])
            RO, bRO = rb("RO", 2, [128, 16, RC], BF16)
            sqs = []
            for g in range(2):
                for dvt in range(2):
                    pb, bpb = OB[g][dvt]
                    for hl in range(4):
                        h = 4 * g + hl
                        oh = pb[:, hl * 128:(hl + 1) * 128]
                        vs = slice(h * 256 + dvt * 128, h * 256 + dvt * 128 + 128)
                        ss = slice(dvt * 128, dvt * 128 + 128)
                        mm(C, oh, V[:, vs], At[:, h, :], True, False, [bV, bAtg[g]], [bpb], signal=False)
                        mm(C, oh, SB[:, h, ss], Qb[:, h, :], False, False, [bSB, bQb], [bpb], signal=False)
                        mm(C, oh, Sf16[:, h, ss], Qf[:, h, :], False, True, [bSf16, bQf], [bpb], signal=(hl == 3))
                for dvt in range(2):
                    pb, bpb = OB[g][dvt]
                    SQ, bSQ = rb("SQ", 4, [128, 512], BF16)
                    actf(C, SQ[:, :], pb[:, :], AF.Square, [bpb], [bSQ])
                    sqs.append((g, dvt, SQ, bSQ))
            for g in range(2):
                pst, bpst = C.psum_stat()
                for (g_, dvt, SQ, bSQ) in sqs:
                    if g_ == g:
                        mm(C, pst[:, :], C.ones[:, :], SQ[:, :], dvt == 0, dvt == 1, [bSQ, C.b_ones], [bpst])
                RS, bRS = rb("RS", 2, [128, 512])
                actf(C, RS[:, :], pst[:, :], AF.Ln, [bpst], [bRS], bias=EPS, scale=1.0 / 256)
                actf(C, RS[:, :], RS[:, :], AF.Exp, [bRS], [bRS], scale=-0.5)
                for dvt in range(2):
                    pb, bpb = OB[g][dvt]
                    ON, bON = rb("ON", 2, [128, 512])
                    tt(C, "dve", ON[:, :], pb[:, :], RS[:, :], ALU.mult, [bpb, bRS], [bON])
                    rov = RO[:, 8 * g:8 * g + 8, :].rearrange("p (a b) i -> p a b i", b=2)[:, :, dvt, :]
                    ogv = OG[:, 8 * g:8 * g + 8, :].rearrange("p (a b) i -> p a b i", b=2)[:, :, dvt, :]
                    tt(C, "pool", rov, ON[:, :].rearrange("p (a i) -> p a i", i=RC), ogv, ALU.mult, [bON, bOG], [bRO])
            P.dma("pool", RO_d[:, :, cs].rearrange("h p t -> p h t"), RO[:, :, :], reads=[bRO], slot=f"ro{c % 2}")
        P.barrier()


def build_test_ret(L):
    pk = build_pack(None, True)
    nc = bass.Bass("TRN2", target_bir_lowering=False)
    RQ = nc.dram_tensor("RQ", [8, 128, L], BF16, kind="ExternalInput")
    RK = nc.dram_tensor("RK", [8, 128, L], BF16, kind="ExternalInput")
    RV = nc.dram_tensor("RV", [L, 2048], BF16, kind="ExternalInput")
    ROG = nc.dram_tensor("ROG", [16, 128, L], BF16, kind="ExternalInput")
    nrm = nc.dram_tensor("nrm", [128, 56], F32, kind="ExternalInput")
    SBD = nc.dram_tensor("SBD", [L // RC, 128, 8, 256], BF16, kind="Internal")
    RO = nc.dram_tensor("RO", [16, 128, L], BF16, kind="ExternalOutput")
    with ExitStack() as st:
        P = Prog(nc, st)
        C = Ctx(nc, P, L, pk)
        setup_common(C, nrm)
        ret_main(C, RQ, RK, RV, ROG, SBD, RO)
        print("instructions:", P.ninst)
    return nc


def pad_rows_s5(w):
    out = np.zeros((32, 32, w.shape[1]), np.float32)
    out[:, 0:16, :] = w.reshape(32, 16, w.shape[1])
    return out.reshape(1024, w.shape[1])


def pack_p3(pk, inp, meta):
    if meta:
        pack_fm(pk, "wglu", Shape(512, 512), 256, True)
        pack_fm(pk, "abo", Shape(1024, 1024), 256, True)
        for h in range(8):
            pk.add_meta(("rqk", h), 8 * 256)
        pack_fm(pk, "rog", Shape(1024, 2048), 256, True)
        pack_fm(pk, "rv", Shape(1024, 2048), 512, True)
        return
    pack_fm(pk, "wglu", inp["s5_w_glu"][0], 256)
    pack_fm(pk, "abo", inp["ab_w_out"][0], 256)
    w = inp["ret_w_in"][0]
    for h in range(8):
        q = w[:, h * 128:(h + 1) * 128]
        k = w[:, 1024 + h * 128:1024 + (h + 1) * 128]
        pk.add(("rqk", h), kt_split(np.concatenate([q, k], axis=1)))
    pack_fm(pk, "rog", w[:, 4096:6144], 256)
    pack_fm(pk, "rv", w[:, 2048:4096], 512)


def pack_p5(pk, inp, meta):
    if meta:
        pack_fm(pk, "reto", Shape(2048, 1024), 256, True)
    else:
        pack_fm(pk, "reto", inp["ret_w_out"][0], 256)


def p3_keys():
    ks = [("wglu", c) for c in range(2)]
    ks += ffn_keys("f20") + ffn_keys("f11")
    ks += [("rqk", h) for h in range(8)] + [("rog", c) for c in range(8)] + [("rv", c) for c in range(4)]
    return ks


def p5_keys():
    return ffn_keys("f21")


def rotary_setup(C):
    P, nc = C.P, C.nc
    C.ROT = P.sb([128, 4], F32, name="rotc")
    C.bROT = Buf("rotc")
    IPI = P.sb([128, 1], mybir.dt.int32, name="rot_ipi")
    bI = Buf()
    P.op("pool", lambda: nc.gpsimd.iota(IPI[:, :], [[0, 1]], base=0, channel_multiplier=1), writes=[bI])
    R = [bI, C.bROT]
    cp(C, "dve", C.ROT[:, 2:3], IPI[:, :], R, [C.bROT])
    ts(C, "dve", C.ROT[:, 3:4], C.ROT[:, 2:3], 64.0, None, ALU.is_ge, None, R, [C.bROT])
    ts(C, "dve", C.ROT[:, 1:2], C.ROT[:, 3:4], 2.0, -1.0, ALU.mult, ALU.add, R, [C.bROT])
    P.op("dve", lambda: nc.vector.scalar_tensor_tensor(C.ROT[:, 2:3], C.ROT[:, 3:4], -64.0, C.ROT[:, 2:3],
                                                      ALU.mult, ALU.add), reads=R, writes=[C.bROT])
    actf(C, C.ROT[:, 0:1], C.ROT[:, 2:3], AF.Exp, R, [C.bROT], scale=-math.log(10000.0) / 64.0)


def phase3(C, X1_d, GY_d, GO_d, X4_d, RQ_d, RK_d, RV_d, ROG_d):
    P, nc = C.P, C.nc
    x1v = X1_d[:, :].rearrange("(k p) t -> p k t", p=128)
    x4v = X4_d[:, :].rearrange("(k p) t -> p k t", p=128)
    for i in range(2):
        for nm in ("xin", "xout", "p3y", "p3o", "p3q", "p3k", "p3g", "p3v"):
            P.slot(f"{nm}{i}")
    for i in range(8):
        P.slot(f"rsw{i}")
    for t in range(C.NT):
        C.W.schedule(p3_keys())
    with ExitStack() as st:
        C.pstack = st
        C.rot = {}
        POSI = P.sb([128, TT], mybir.dt.int32, stack=st)
        bPOSI = Buf()
        TI = P.sb([128, TT], mybir.dt.int32, stack=st)

        ABW = P.sb([128, 4, 2048], BF16, stack=st, name="abo_res")
        bABW = [Buf(f"abw{c}") for c in range(4)]
        for c in range(4):
            off, n = C.pk.chunks[("abo", c)]
            P.dma("sp", ABW[:, c, 0:n], bass.AP(C.W.wbf, off, [[n, 128], [1, n]]), writes=[bABW[c]],
                  slot=P.slot(f"s5st{c}"))

        def load_in(t):
            sl = slice(t * TT, (t + 1) * TT)
            X, bX = C.rotbuf("X", 2, [128, 8, TT], F32, nb=8)
            GY, bGY = C.rotbuf("GY", 2, [128, 4, TT], BF16)
            GO, bGO = C.rotbuf("GO", 2, [128, 4, TT], BF16)
            P.dma("sp", X[:, :, :], x1v[:, :, sl], writes=bX, slot=f"xin{t % 2}")
            P.dma("sp", GY[:, :, :], GY_d[:, :, sl].rearrange("b p t -> p b t"), writes=[bGY], slot=f"p3y{t % 2}")
            P.dma("sp", GO[:, :, :], GO_d[:, :, sl].rearrange("h p t -> p h t"), writes=[bGO], slot=f"p3o{t % 2}")
            return (X, bX), (GY, bGY), (GO, bGO)
        for t in range(C.NT):
            sl = slice(t * TT, (t + 1) * TT)
            i2 = t % 2
            if t == 0:
                nxt_in = load_in(0)
            (X, bX), (GY, bGY), (GO, bGO) = nxt_in
            if t + 1 < C.NT:
                nxt_in = load_in(t + 1)
            H, bH = C.rotbuf("H", 1, [128, 8, TT], BF16, nb=8)
            S5O, bS5O = C.rotbuf("S5O", 1, [128, 4, TT], BF16)
            for c in range(2):
                def ev(j, ps, bps, c=c):
                    m = 2 * c + j
                    SG, bSG = C.rotbuf("SG", 2, [128, TT], F32)
                    actf(C, SG[:, :], ps[:, :], AF.Sigmoid, [bps], [bSG])
                    tt(C, "dve", S5O[:, m, :], GY[:, m, :], SG[:, :], ALU.mult, [bGY, bSG], [bS5O])
                proj_fm(C, ("wglu", c), 2, 4, lambda kt: GY[:, kt, :], [bGY], ev)
            resid_proj(C, "abo", 4, 8, lambda kt: S5O[:, kt, :] if kt < 4 else GO[:, kt - 4, :],
                       lambda kt: [bS5O, bGO], X, bX, (C.G, 16, H, bH), resident=(ABW, bABW))
            emit_ffn(C, "f20", X, bX, H, bH, nxt=(C.G, 24, H, bH))
            emit_ffn(C, "f11", X, bX, H, bH, nxt=(C.G, 32, H, bH))
            P.dma("pool", x4v[:, :, sl], X[:, :, :], reads=bX, slot=f"xout{i2}")
            TB, bTB = C.rotbuf("rtab", 1, [128, 6, TT], F32)
            RT = [bTB, C.bROT, bPOSI]
            P.op("pool", lambda: nc.gpsimd.iota(POSI[:, :], [[1, TT]], base=t * TT, channel_multiplier=0),
                 reads=[bPOSI], writes=[bPOSI])
            cp(C, "dve", TB[:, 0, :], POSI[:, :], RT, [bTB])
            ts(C, "dve", TB[:, 0, :], TB[:, 0, :], C.ROT[:, 0:1], None, ALU.mult, None, RT, [bTB])
            emit_sin(C, TB[:, 3, :], TB[:, 0, :], 0.0, TB[:, 1, :], TI[:, :], RT, [bTB])
            emit_sin(C, TB[:, 2, :], TB[:, 0, :], math.pi / 2, TB[:, 1, :], TI[:, :], RT, [bTB])
            ts(C, "dve", TB[:, 3, :], TB[:, 3, :], C.ROT[:, 1:2], None, ALU.mult, None, RT, [bTB])
            ksc = 128.0 ** -0.5
            ts(C, "dve", TB[:, 4, :], TB[:, 2, :], ksc, None, ALU.mult, None, RT, [bTB])
            ts(C, "dve", TB[:, 5, :], TB[:, 3, :], ksc, None, ALU.mult, None, RT, [bTB])
            hnat = lambda kt: H[:, kt, :]
            for h in range(8):
                def ev(j, ps, bps, h=h):
                    nsw = C.nsw = getattr(C, "nsw", 0) + 1
                    r4 = nsw % 4
                    QF, bQF = C.rotbuf("rQF", 4, [128, TT], F32)
                    QS, bQS = C.rotbuf("rQS", 4, [128, TT], F32, nb=2)
                    T1, bT1 = C.rotbuf("rT1", 2, [128, TT], F32)
                    T2, bT2 = C.rotbuf("rT2", 2, [128, TT], F32)
                    O_, bO_ = C.rotbuf("rOQ", 4, [128, TT], BF16)
                    cp(C, "act", QF[:, :], ps[:, :], [bps], [bQF])
                    P.dma("act", QS[0:64, :], QF[64:128, :], reads=[bQF], writes=[bQS[0]], slot=f"rsw{2 * r4}")
                    P.dma("act", QS[64:128, :], QF[0:64, :], reads=[bQF], writes=[bQS[1]], slot=f"rsw{2 * r4 + 1}")
                    base = 2 if j == 0 else 4
                    tt(C, "dve", T1[:, :], QF[:, :], TB[:, base, :], ALU.mult, [bQF, bTB], [bT1])
                    tt(C, "pool" if j == 0 else "dve", T2[:, :], QS[:, :], TB[:, base + 1, :], ALU.mult, bQS + [bTB], [bT2])
                    tt(C, "dve", O_[:, :], T1[:, :], T2[:, :], ALU.add, [bT1, bT2], [bO_])
                    dst = (RQ_d if j == 0 else RK_d)[h, :, sl]
                    P.dma("pool", dst, O_[:, :], reads=[bO_], slot=f"p3q{h % 2}" if j == 0 else f"p3k{h % 2}")
                proj_fm(C, ("rqk", h), 2, 8, hnat, lambda kt: [bH[kt]], ev)
            for c in range(8):
                OGS, bOGS = C.rotbuf("rOGS", 2, [128, 2, TT], BF16)

                def ev(j, ps, bps, c=c, OGS=OGS, bOGS=bOGS):
                    idx = 2 * c + j
                    SG, bSG = C.rotbuf("SG", 2, [128, TT], F32)
                    actf(C, SG[:, :], ps[:, :], AF.Silu, [bps], [bSG])
                    ts(C, "dve", OGS[:, j, :], SG[:, :], C.RN[:, idx:idx + 1], None, ALU.mult, None, [bSG, C.bGN], [bOGS])
                proj_fm(C, ("rog", c), 2, 8, hnat, lambda kt: [bH[kt]], ev)
                P.dma("pool", ROG_d[2 * c:2 * c + 2, :, sl].rearrange("a p t -> p a t"), OGS[:, :, :], reads=[bOGS],
                      slot=f"p3g{c % 2}")
            for c in range(4):
                VS, bVS = C.rotbuf("rVS", 2, [128, 4, 512], BF16)
                proj_tm(C, ("rv", c), H, bH, 8,
                        lambda i, ps, bps, VS=VS, bVS=bVS: cp(C, "act" if i % 2 == 0 else "dve", VS[:, i, :], ps[:, :],
                                                              [bps], [bVS]))
                P.dma("pool", RV_d[sl, c * 512:(c + 1) * 512].rearrange("(c p) f -> p c f", p=128), VS[:, :, :],
                      reads=[bVS], slot=f"p3v{c % 2}")
        P.barrier()
    C.pstack = None


def phase5(C, X4_d, RO_d, outT_d):
    P, nc = C.P, C.nc
    x4v = X4_d[:, :].rearrange("(k p) t -> p k t", p=128)
    ov = outT_d[:, :].rearrange("(k p) t -> p k t", p=128)
    for i in range(2):
        for nm in ("xin", "xout", "p5o"):
            P.slot(f"{nm}{i}")
    for t in range(C.NT):
        C.W.schedule(p5_keys())
    with ExitStack() as st:
        C.pstack = st
        C.rot = {}

        RW = P.sb([128, 4, 4096], BF16, stack=st, name="reto_res")
        bRW = [Buf(f"reto{c}") for c in range(4)]
        for c in range(4):
            off, n = C.pk.chunks[("reto", c)]
            P.dma("sp", RW[:, c, 0:n], bass.AP(C.W.wbf, off, [[n, 128], [1, n]]), writes=[bRW[c]],
                  slot=P.slot(f"s5st{c}"))

        def load_in(t):
            sl = slice(t * TT, (t + 1) * TT)
            X, bX = C.rotbuf("X", 2, [128, 8, TT], F32, nb=8)
            RO, bRO = C.rotbuf("RO", 2, [128, 16, TT], BF16)
            P.dma("sp", X[:, :, :], x4v[:, :, sl], writes=bX, slot=f"xin{t % 2}")
            P.dma("sp", RO[:, :, :], RO_d[:, :, sl].rearrange("h p t -> p h t"), writes=[bRO], slot=f"p5o{t % 2}")
            return (X, bX), (RO, bRO)
        for t in range(C.NT):
            sl = slice(t * TT, (t + 1) * TT)
            i2 = t % 2
            if t == 0:
                nxt_in = load_in(0)
            (X, bX), (RO, bRO) = nxt_in
            if t + 1 < C.NT:
                nxt_in = load_in(t + 1)
            H, bH = C.rotbuf("H", 1, [128, 8, TT], BF16, nb=8)
            resid_proj(C, "reto", 4, 16, lambda kt: RO[:, kt, :], lambda kt: [bRO], X, bX, (C.G, 40, H, bH),
                       resident=(RW, bRW))
            HO, bHO = C.rotbuf("HO", 2, [128, 8, TT], F32, nb=8)
            emit_ffn(C, "f21", X, bX, H, bH, nxt=(C.G, 48, HO, bHO))
            P.dma("pool", ov[:, :, sl], HO[:, :, :], reads=bHO, slot=f"xout{i2}")
        P.barrier()
    C.pstack = None


def full_pack(inp, meta):
    pk = WPack()
    def ffn(tag, a, b):
        if meta:
            pack_ffn(pk, tag, None, None, True)
        else:
            pack_ffn(pk, tag, inp[a + "_w1"][b], inp[a + "_w2"][b])
    ffn("f10", "ffn1", 0)
    pack_p1(pk, inp, meta)
    pk.first = pk.off
    ffn("f20", "ffn2", 0)
    ffn("f11", "ffn1", 1)
    pack_p3(pk, inp, meta)
    pack_p5(pk, inp, meta)
    ffn("f21", "ffn2", 1)
    return pk


def build_full(L, dbg_outputs=()):
    pk = full_pack(None, True)
    nc = bass.Bass("TRN2", target_bir_lowering=False)

    def dram(name, shape, dt, kind="Internal"):
        if name in dbg_outputs:
            kind = "ExternalOutput"
        return nc.dram_tensor(name, list(shape), dt, kind=kind)
    xT = dram("xT", [D, L], F32, "ExternalInput")
    nrm = dram("nrm", [128, 56], F32, "ExternalInput")
    gn = dram("gn", [128, 20], F32, "ExternalInput")
    wf32 = dram("wf32", [pk.off], F32, "ExternalInput")
    s5p = dram("s5p", [128, S5P_COLS], F32, "ExternalInput")
    gkw = dram("gkw", [32, 512], F32, "ExternalInput")
    gkb = dram("gkb", [128, 4], F32, "ExternalInput")
    outT = dram("outT", [D, L], F32, "ExternalOutput")
    wbf = dram("wbf", [pk.off], BF16)
    X1 = dram("X1", [D, L], F32)
    X4 = dram("X4", [D, L], F32)
    UT = dram("UT", [8, 128, L], BF16)
    GQ = dram("GQ", [2, 128, L], BF16)
    GK = dram("GK", [2, 128, L], BF16)
    GV = dram("GV", [L, 512], BF16)
    GLO = dram("GLO", [32, L], BF16)
    GOG = dram("GOG", [4, 128, L], BF16)
    GY = dram("GY", [4, 128, L], BF16)
    GO = dram("GO", [4, 128, L], BF16)
    RQ = dram("RQ", [8, 128, L], BF16)
    RK = dram("RK", [8, 128, L], BF16)
    RV = dram("RV", [L, 2048], BF16)
    ROG = dram("ROG", [16, 128, L], BF16)
    SBD = dram("SBD", [L // RC, 128, 8, 256], BF16)
    RO = dram("RO", [16, 128, L], BF16)
    WIN_d, WOUT_d, TOEP_d = s5_dram(nc)
    with ExitStack() as st:
        P = Prog(nc, st)
        C = Ctx(nc, P, L, pk)
        setup_common(C, nrm, gn)
        rotary_setup(C)
        castA = cast_gen(C, wf32, wbf, 0, pk.first, "A")
        for _ in range(4):
            next(castA, None)
        C.W = WStream(C, wbf)
        s5_setup(C, s5p, WIN_d, WOUT_d, TOEP_d, bg=castA)
        castB = cast_gen(C, wf32, wbf, pk.first, pk.off, "B")
        phase1(C, xT, X1, UT, GQ, GK, GV, GLO, GOG)
        NCH = L // 16
        per = max(4, -(-NCH // (2 * (L // GC))))
        def both():
            k = 0
            while True:
                if k % per == 0:
                    next(castB, None)
                k += 1
                yield

        def mid(g):
            def merged():
                b = both()
                for _ in g:
                    next(b)
                    yield
            gla_main(C, GQ, GK, GV, GLO, GOG, gkw, gkb, GO, bg=merged(), bg_per_chunk=per)
        s5_main(C, UT, WIN_d, WOUT_d, TOEP_d, GY, mid=mid)
        for _ in castB:
            pass
        P.barrier()
        phase3(C, X1, GY, GO, X4, RQ, RK, RV, ROG)
        ret_main(C, RQ, RK, RV, ROG, SBD, RO)
        phase5(C, X4, RO, outT)
        P.barrier()
        C.ninst = P.ninst
    return nc, pk


def host_inputs(inp):
    pk = full_pack(inp, False)
    wimg = pk.image()
    names = ["ffn1_norm", "mix_norm", "ffn2_norm"]
    nrm = np.zeros((128, 56), np.float32)
    col = 0
    for layer in range(2):
        order = [inp["ffn1_norm"][layer], inp["mix_norm"][layer], inp["ffn2_norm"][layer]]
        if layer == 1:
            pass
        for j, g in enumerate(order):
            pass
    cols = [inp["ffn1_norm"][0], inp["mix_norm"][0], inp["ffn2_norm"][0],
            inp["ffn1_norm"][1], inp["mix_norm"][1], inp["ffn2_norm"][1], inp["final_norm"]]
    for i, g in enumerate(cols):
        nrm[:, 8 * i:8 * i + 8] = np.asarray(g, np.float32).reshape(8, 128).T
    gn = np.zeros((128, 20), np.float32)
    gn[:, 0:4] = inp["gla_norm"][0].reshape(4, 128).T
    gn[:, 4:20] = inp["ret_norm"][0].reshape(16, 128).T
    gkw, gkb = gla_pack_params(inp)
    return dict(nrm=nrm, gn=gn, wf32=wimg, s5p=s5_pack_params(inp), gkw=gkw, gkb=gkb)


_CACHE = {}


def kernel(**inputs):
    inp = {k: np.asarray(v) for k, v in inputs.items()}
    x = inp["x"]
    B, L, _ = x.shape
    shared = host_inputs(inp)
    if L not in _CACHE:
        _CACHE[L] = build_full(L)
    nc, pk = _CACHE[L]
    in_maps = []
    for b in range(B):
        m = dict(shared)
        m["xT"] = np.ascontiguousarray(x[b].T)
        in_maps.append(m)
    res = run_bass_kernel_spmd(nc, in_maps, core_ids=list(range(B)))
    out = np.stack([np.ascontiguousarray(r["outT"].T) for r in res.results], axis=0)
    return out.astype(np.float32)
```
